# Optimizing a Trainium2 kernel written in Bass

```python
import math
import functools
import jax
import jax.numpy as jnp
from jax import lax
import numpy as np

D_MODEL = 4096
BATCH = 2
SEQ = 4096
DEPTH = 2

CTX_LEN = 256
GRID_W = 64
EPS = 1e-6

DN_HEADS = 16
DN_HEAD_DIM = 128
DN_WIDTH = DN_HEADS * DN_HEAD_DIM
DN_CONV = 4
DN_CHUNK = 64

LRU_WIDTH = 2048
LRU_BLOCKS = 16
LRU_BLOCK_DIM = LRU_WIDTH // LRU_BLOCKS
LRU_CONV = 4
LRU_C = 8.0

SC_WIDTH = D_MODEL
SC_CONV = 3

OFF_LRU = 3 * DN_WIDTH
OFF_BETA = OFF_LRU + LRU_WIDTH
OFF_ALPHA = OFF_BETA + 2 * DN_HEADS
AB_STATE = OFF_ALPHA + 2 * DN_HEADS
OFF_LRU_GATE = AB_STATE + DN_WIDTH
AB_IN = OFF_LRU_GATE + LRU_WIDTH
AB_OUT = DN_WIDTH + LRU_WIDTH

kernel_name = 'hybrid_deltanet_rglru_shortconv_dit'

F32 = jnp.float32


def _rms_norm(x, w):
    xf = x.astype(F32)
    y = xf * lax.rsqrt(jnp.mean(xf * xf, axis=-1, keepdims=True) + EPS)
    return (y * w.astype(F32)).astype(x.dtype)


def _l2norm(x):
    return x * lax.rsqrt(jnp.sum(x * x, axis=-1, keepdims=True) + EPS)


def _rev(t, axis, on):
    return jnp.flip(t, axis=axis) if on else t


def _dw_conv(x, w, b=None):
    k = w.shape[0]
    left = k // 2
    y = lax.conv_general_dilated(
        x, w[:, None, :].astype(x.dtype), window_strides=(1,), padding=[(left, k - 1 - left)],
        dimension_numbers=('NWC', 'WIO', 'NWC'), feature_group_count=x.shape[-1])
    if b is not None:
        y = y + b.astype(x.dtype)
    return y


def _to_col_major(t, rows):
    bsz, length, ch = t.shape
    return t.reshape(bsz, rows, GRID_W, ch).swapaxes(1, 2).reshape(bsz, length, ch)


def _to_raster(t, rows):
    bsz, length, ch = t.shape
    return t.reshape(bsz, GRID_W, rows, ch).swapaxes(1, 2).reshape(bsz, length, ch)


def _short_conv_heads(p, conv_w):
    bsz, length, width = p.shape
    y = jax.nn.silu(_dw_conv(p, conv_w)).astype(F32)
    return y.reshape(bsz, length, width // DN_WIDTH, DN_HEADS, DN_HEAD_DIM).transpose(2, 0, 3, 1, 4)


def _decay_gates(p_beta, p_alpha, a_log, dt_bias):
    bsz, length, _ = p_beta.shape
    beta = jax.nn.sigmoid(p_beta.astype(F32)).reshape(bsz, length, 2, DN_HEADS)
    alpha = p_alpha.astype(F32).reshape(bsz, length, 2, DN_HEADS)
    g = -jnp.exp(a_log.astype(F32)) * jax.nn.softplus(alpha + dt_bias.astype(F32))
    return beta.transpose(2, 0, 3, 1), g.transpose(2, 0, 3, 1)


def _delta_chunks(k, v, beta, g):
    bsz, nh, length, dk = k.shape
    n = length // DN_CHUNK
    kc = k.reshape(bsz, nh, n, DN_CHUNK, dk)
    vc = v.reshape(bsz, nh, n, DN_CHUNK, v.shape[-1])
    bc = beta.reshape(bsz, nh, n, DN_CHUNK)
    g_cum = jnp.cumsum(g.reshape(bsz, nh, n, DN_CHUNK), axis=-1)
    idx = jnp.arange(DN_CHUNK)
    diff = g_cum[..., :, None] - g_cum[..., None, :]
    decay = jnp.exp(jnp.where(idx[:, None] >= idx[None, :], diff, -jnp.inf))
    kk = jnp.einsum('bhntd,bhnsd->bhnts', kc, kc)
    lower = jnp.where(idx[:, None] > idx[None, :], bc[..., :, None] * kk * decay, 0.0)
    t_mat = lower + jnp.eye(DN_CHUNK, dtype=lower.dtype)
    solve = functools.partial(lax.linalg.triangular_solve, left_side=True, lower=True, unit_diagonal=True)
    w = solve(t_mat, (bc * jnp.exp(g_cum))[..., None] * kc)
    u = solve(t_mat, bc[..., None] * vc)
    k_end = kc * jnp.exp(g_cum[..., -1:] - g_cum)[..., None]
    g_end = jnp.exp(g_cum[..., -1])
    return g_cum, decay, kc, w, u, k_end, g_end


def _delta_step(s, w_c, u_c, ke_c, ge_c):
    u_c = u_c - jnp.einsum('bhtd,bhdv->bhtv', w_c, s)
    s_new = ge_c[..., None, None] * s + jnp.einsum('bhtd,bhtv->bhdv', ke_c, u_c)
    return u_c, s_new


def _delta_final_state(k, v, beta, g, s0):
    _, _, _, w, u, k_end, g_end = _delta_chunks(k, v, beta, g)

    def step(s, xs):
        _, s = _delta_step(s, *xs)
        return s, None

    s, _ = lax.scan(step, s0, tuple(jnp.moveaxis(t, 2, 0) for t in (w, u, k_end, g_end)))
    return s


def _delta_outputs(q, k, v, beta, g, s0):
    g_cum, decay, kc, w, u, k_end, g_end = _delta_chunks(k, v, beta, g)
    bsz, nh, length, dk = q.shape
    qc = q.reshape(bsz, nh, -1, DN_CHUNK, dk)
    a_qk = jnp.einsum('bhntd,bhnsd->bhnts', qc, kc) * decay
    q_g = qc * jnp.exp(g_cum)[..., None]

    def step(s, xs):
        w_c, u_c, ke_c, ge_c, aqk_c, qg_c = xs
        u_c, s_new = _delta_step(s, w_c, u_c, ke_c, ge_c)
        o = jnp.einsum('bhtd,bhdv->bhtv', qg_c, s) + jnp.einsum('bhts,bhsv->bhtv', aqk_c, u_c)
        return s_new, o

    xs = tuple(jnp.moveaxis(t, 2, 0) for t in (w, u, k_end, g_end, a_qk, q_g))
    _, o = lax.scan(step, s0, xs)
    return jnp.moveaxis(o, 0, 2).reshape(bsz, nh, length, -1)


def _rglru_gates(xc, w_r, b_r, w_i, b_i, lam):
    blocks = xc.reshape(xc.shape[0], xc.shape[1], LRU_BLOCKS, LRU_BLOCK_DIM)
    r = jax.nn.sigmoid(jnp.einsum('blnd,nde->blne', blocks, w_r).reshape(xc.shape) + b_r)
    i = jax.nn.sigmoid(jnp.einsum('blnd,nde->blne', blocks, w_i).reshape(xc.shape) + b_i)
    log_a = -LRU_C * r * jax.nn.softplus(-lam)
    b = jnp.sqrt(-jnp.expm1(2.0 * log_a)) * (i * xc)
    return log_a, b


def _linear_scan(a, b, h0):
    b = b.at[:, 0].add(a[:, 0] * h0)

    def combine(prev, nxt):
        return prev[0] * nxt[0], nxt[0] * prev[1] + nxt[1]

    _, h = lax.associative_scan(combine, (a, b), axis=1)
    return h


def _linear_final_state(log_a, b):
    suffix = lax.cumsum(log_a, axis=1, reverse=True) - log_a
    return jnp.sum(jnp.exp(suffix) * b, axis=1)


def _ab_mixer(h, hc, w_in, qkv_conv, a_log, dt_bias, dn_norm, lru_conv_w, lru_conv_b,
              lru_w_r, lru_b_r, lru_w_i, lru_b_i, lru_lambda, w_out):
    bsz, length, _ = h.shape
    rows = length // GRID_W
    sh = DN_WIDTH
    proj = h @ w_in
    proj_c = hc @ w_in[:, sh:AB_STATE]

    qkv = _short_conv_heads(proj[..., :OFF_LRU], qkv_conv)
    q = _l2norm(qkv[0]) * DN_HEAD_DIM ** -0.5
    k = _l2norm(qkv[1])
    v = qkv[2]
    beta, g = _decay_gates(proj[..., OFF_BETA:OFF_ALPHA], proj[..., OFF_ALPHA:AB_STATE], a_log, dt_bias)
    kv_c = _short_conv_heads(proj_c[..., :OFF_LRU - sh], qkv_conv[:, sh:])
    k_c = _l2norm(kv_c[0])
    v_c = kv_c[1]
    beta_c, g_c = _decay_gates(proj_c[..., OFF_BETA - sh:OFF_ALPHA - sh], proj_c[..., OFF_ALPHA - sh:],
                               a_log, dt_bias)
    s0 = jnp.zeros((bsz, DN_HEADS, DN_HEAD_DIM, DN_HEAD_DIM), F32)
    o_dn = jnp.zeros_like(v)
    for d in range(2):
        s_ctx = _delta_final_state(_rev(k_c, 2, d), _rev(v_c, 2, d), _rev(beta_c[d], 2, d),
                                   _rev(g_c[d], 2, d), s0)
        o = _delta_outputs(_rev(q, 2, d), _rev(k, 2, d), _rev(v, 2, d), _rev(beta[d], 2, d),
                           _rev(g[d], 2, d), s_ctx)
        o_dn = o_dn + _rev(o, 2, d)
    o_dn = _rms_norm(o_dn, dn_norm).transpose(0, 2, 1, 3).reshape(bsz, length, DN_WIDTH)

    xc = _dw_conv(_to_col_major(proj[..., OFF_LRU:OFF_BETA], rows), lru_conv_w, lru_conv_b).astype(F32)
    xc_c = _dw_conv(proj_c[..., OFF_LRU - sh:OFF_BETA - sh], lru_conv_w, lru_conv_b).astype(F32)
    h_lru = jnp.zeros_like(xc)
    for d in range(2):
        la_c, b_c = _rglru_gates(_rev(xc_c, 1, d), lru_w_r[d], lru_b_r[d], lru_w_i[d], lru_b_i[d], lru_lambda[d])
        h0 = _linear_final_state(la_c, b_c)
        la, bb = _rglru_gates(_rev(xc, 1, d), lru_w_r[d], lru_b_r[d], lru_w_i[d], lru_b_i[d], lru_lambda[d])
        h_lru = h_lru + _rev(_linear_scan(jnp.exp(la), bb, h0), 1, d)
    h_lru = _to_raster(h_lru, rows)

    y = jnp.concatenate([o_dn * jax.nn.silu(proj[..., AB_STATE:OFF_LRU_GATE]),
                         h_lru * jax.nn.silu(proj[..., OFF_LRU_GATE:])], axis=-1)
    return y @ w_out


def _sc_mixer(h, w_in, conv_w, w_out):
    bsz, length, _ = h.shape
    rows = length // GRID_W
    b_g, c_g, x_in, gate = jnp.split(h @ w_in, 4, axis=-1)
    z = (c_g * x_in).reshape(bsz * rows, GRID_W, SC_WIDTH)
    z = _dw_conv(z, conv_w).reshape(bsz, length, SC_WIDTH)
    return (b_g * z * jax.nn.silu(gate)) @ w_out


def setup_inputs(seed: int = 0) -> dict:
    key = jax.random.key(seed)
    ks = iter(jax.random.split(key, 32))
    ne, no = (DEPTH + 1) // 2, DEPTH // 2
    dm = D_MODEL

    def nrm(shape, scale):
        return jax.random.normal(next(ks), shape, F32) * scale

    a_pow = jax.random.uniform(next(ks), (ne, 2, LRU_WIDTH), F32, 0.9, 0.999)
    a_base = a_pow ** (1.0 / LRU_C)
    lru_lambda = jnp.log(a_base) - jnp.log1p(-a_base)
    a_mag = jax.random.uniform(next(ks), (ne, 2, DN_HEADS), F32, 1.0, 16.0)
    dt = jnp.exp(jax.random.uniform(next(ks), (ne, 2, DN_HEADS), F32, math.log(1e-3), math.log(1e-1)))
    dt_bias = dt + jnp.log(-jnp.expm1(-dt))
    return {
        'x': nrm((BATCH, SEQ, dm), 1.0),
        'c': nrm((BATCH, dm), 1.0),
        'ctx': nrm((BATCH, CTX_LEN, dm), 1.0),
        'c_ctx': nrm((dm,), 1.0),
        'mod_w': nrm((DEPTH, dm, 3 * dm), dm ** -0.5),
        'mod_b': nrm((DEPTH, 3 * dm), 0.02),
        'norm_w': 1.0 + nrm((DEPTH, dm), 0.02),
        'ab_w_in': nrm((ne, dm, AB_IN), dm ** -0.5),
        'ab_qkv_conv': nrm((ne, DN_CONV, 3 * DN_WIDTH), DN_CONV ** -0.5),
        'ab_a_log': jnp.log(a_mag),
        'ab_dt_bias': dt_bias,
        'ab_dn_norm': 1.0 + nrm((ne, DN_HEAD_DIM), 0.02),
        'ab_lru_conv_w': nrm((ne, LRU_CONV, LRU_WIDTH), LRU_CONV ** -0.5),
        'ab_lru_conv_b': nrm((ne, LRU_WIDTH), 0.02),
        'ab_lru_w_r': nrm((ne, 2, LRU_BLOCKS, LRU_BLOCK_DIM, LRU_BLOCK_DIM), LRU_BLOCK_DIM ** -0.5),
        'ab_lru_b_r': nrm((ne, 2, LRU_WIDTH), 0.02),
        'ab_lru_w_i': nrm((ne, 2, LRU_BLOCKS, LRU_BLOCK_DIM, LRU_BLOCK_DIM), LRU_BLOCK_DIM ** -0.5),
        'ab_lru_b_i': nrm((ne, 2, LRU_WIDTH), 0.02),
        'ab_lru_lambda': lru_lambda,
        'ab_w_out': nrm((ne, AB_OUT, dm), AB_OUT ** -0.5),
        'sc_w_in': nrm((no, dm, 4 * SC_WIDTH), dm ** -0.5),
        'sc_conv': nrm((no, SC_CONV, SC_WIDTH), SC_CONV ** -0.5),
        'sc_w_out': nrm((no, SC_WIDTH, dm), SC_WIDTH ** -0.5),
        'final_norm_w': 1.0 + nrm((dm,), 0.02),
    }


def reference(x, c, ctx, c_ctx, mod_w, mod_b, norm_w, ab_w_in, ab_qkv_conv, ab_a_log, ab_dt_bias,
              ab_dn_norm, ab_lru_conv_w, ab_lru_conv_b, ab_lru_w_r, ab_lru_b_r, ab_lru_w_i, ab_lru_b_i,
              ab_lru_lambda, ab_w_out, sc_w_in, sc_conv, sc_w_out, final_norm_w):
    dm = D_MODEL
    silu_c = jax.nn.silu(c)
    silu_cc = jax.nn.silu(c_ctx)
    for layer in range(DEPTH):
        j = layer // 2
        shift, scale, gate = jnp.split(silu_c @ mod_w[layer] + mod_b[layer], 3, axis=-1)
        hn = _rms_norm(x, norm_w[layer]) * (1.0 + scale[:, None, :]) + shift[:, None, :]
        if layer % 2 == 0:
            shift_c, scale_c = jnp.split(silu_cc @ mod_w[layer][:, :2 * dm] + mod_b[layer][:2 * dm], 2)
            hc = _rms_norm(ctx, norm_w[layer]) * (1.0 + scale_c) + shift_c
            y = _ab_mixer(hn, hc, ab_w_in[j], ab_qkv_conv[j], ab_a_log[j], ab_dt_bias[j], ab_dn_norm[j],
                          ab_lru_conv_w[j], ab_lru_conv_b[j], ab_lru_w_r[j], ab_lru_b_r[j], ab_lru_w_i[j],
                          ab_lru_b_i[j], ab_lru_lambda[j], ab_w_out[j])
        else:
            y = _sc_mixer(hn, sc_w_in[j], sc_conv[j], sc_w_out[j])
        x = x + (gate[:, None, :] * y).astype(x.dtype)
    return _rms_norm(x, final_norm_w)
```

```python
import numpy as np
import ml_dtypes
from contextlib import ExitStack
import concourse.bass as bass
import concourse.mybir as mybir
from concourse.bass_utils import run_bass_kernel_spmd

F32 = mybir.dt.float32
BF16 = mybir.dt.bfloat16
ALU = mybir.AluOpType
AF = mybir.ActivationFunctionType
AX = mybir.AxisListType

ENGS = ['pe', 'dve', 'act', 'pool', 'sp']
NCORES = 8
D = 4096
EPS = 1e-6


class Prog:
    def __init__(self, nc, stack):
        self.nc = nc
        self.stack = stack
        self.semstack = stack
        self.rec = []
        self.lastw = {}
        self.readers = {}
        self.last_on = {}
        self.nbuf = 0
        self.ninst = 0
        self.sems = {}

    def sb(self, shape, dtype, name=None, stack=None):
        self.nbuf += 1
        name = name or ("t%d" % self.nbuf)
        return (stack or self.stack).enter_context(self.nc.sbuf_tensor(name, list(shape), dtype))

    def ps(self, shape, dtype=F32, name=None, stack=None):
        self.nbuf += 1
        name = name or ("p%d" % self.nbuf)
        return (stack or self.stack).enter_context(self.nc.psum_tensor(name, list(shape), dtype))

    def _deps(self, reads, writes):
        deps = []
        for k in reads:
            if k in self.lastw:
                deps.append(self.lastw[k])
        for k in writes:
            if k in self.lastw:
                deps.append(self.lastw[k])
            deps.extend(self.readers.get(k, ()))
        return deps

    def _commit(self, rid, reads, writes):
        for k in writes:
            self.lastw[k] = rid
            self.readers[k] = []
        for k in reads:
            if k in writes:
                continue
            self.readers.setdefault(k, []).append(rid)

    def _add(self, kind, eng, fn, reads, writes, semkey, inc):
        rid = len(self.rec)
        self.rec.append((kind, eng, fn, self._deps(reads, writes), semkey, inc))
        self.last_on[semkey] = rid
        self._commit(rid, reads, writes)
        return rid

    def op(self, eng, fn, reads=(), writes=()):
        return self._add('op', eng, fn, reads, writes, eng, 1)

    def dma(self, q, out, in_, reads=(), writes=(), semkey=None):
        if semkey is None:
            semkey = 'd_' + (writes[0] if writes else reads[0])
        return self._add('dma', q, lambda e, out=out, in_=in_: e.dma_start(out=out, in_=in_), reads, writes, semkey, 16)

    def custom(self, q, fn, reads, writes, semkey, inc):
        return self._add('dma', q, fn, reads, writes, semkey, inc)

    def barrier(self):
        self.rec.append(('barrier', None, None, dict(self.last_on), None, 0))

    def finish(self):
        nc = self.nc
        self.rec.append(('barrier', 'sp', None, dict(self.last_on), None, 0))
        rec = self.rec
        needed = set()
        for kind, eng, fn, deps, semkey, inc in rec:
            if kind == 'barrier':
                needed.update(deps.values())
            else:
                for d in deps:
                    if not (eng == 'pe' and rec[d][4] == 'pe'):
                        needed.add(d)
        cnt = {}
        value = {}
        seen = {e: {} for e in ENGS}
        ops = {e: [] for e in ENGS}

        def sem_of(sk):
            if sk not in self.sems:
                self.sems[sk] = self.semstack.enter_context(nc.semaphore("q%d" % len(self.sems)))
            return self.sems[sk]

        def emit_waits(eng, pairs):
            need = {}
            for sk, v in pairs:
                if seen[eng].get(sk, 0) >= v:
                    continue
                if need.get(sk, 0) < v:
                    need[sk] = v
            for sk, v in need.items():
                seen[eng][sk] = v
                sem = sem_of(sk)
                ops[eng].append(lambda e, sem=sem, v=v: e.wait_ge(sem, v))
                self.ninst += 1

        for rid, (kind, eng, fn, deps, semkey, inc) in enumerate(rec):
            if kind == 'barrier':
                pairs = [(sk, cnt[sk]) for sk in deps if cnt.get(sk, 0) > 0]
                for e in (ENGS if eng is None else [eng]):
                    emit_waits(e, pairs)
                continue
            pairs = []
            for d in deps:
                if eng == 'pe' and rec[d][4] == 'pe':
                    continue
                pairs.append(value[d])
            emit_waits(eng, pairs)
            if kind == 'dma' or rid in needed:
                cnt[semkey] = cnt.get(semkey, 0) + inc
                value[rid] = (semkey, cnt[semkey])
                sem = sem_of(semkey)
                ops[eng].append(lambda e, fn=fn, sem=sem, inc=inc: fn(e).then_inc(sem, inc))
            else:
                ops[eng].append(lambda e, fn=fn: fn(e))
            self.ninst += 1
        with nc.Block() as block:
            def mk(name):
                def run(e):
                    for f in ops[name]:
                        f(e)
                return run
            block.tensor(mk('pe'))
            block.vector(mk('dve'))
            block.scalar(mk('act'))
            block.gpsimd(mk('pool'))
            block.sync(mk('sp'))


def _launch(nc, in_maps):
    res = run_bass_kernel_spmd(nc, in_maps, core_ids=list(range(NCORES)))
    return res.results


L0_COLS = 2 * 3 * D // NCORES


def build_l0():
    nc = bass.Bass("TRN2", target_bir_lowering=False)
    csT = nc.dram_tensor("csT", [128, 96], F32, kind="ExternalInput").ap()
    w = nc.dram_tensor("w", [D, L0_COLS], F32, kind="ExternalInput").ap()
    bias = nc.dram_tensor("bias", [3, L0_COLS], F32, kind="ExternalInput").ap()
    o = nc.dram_tensor("o", [3, L0_COLS], F32, kind="ExternalOutput").ap()
    with ExitStack() as st:
        P = Prog(nc, st)
        emit_l0(P, csT, w, bias, o)
        P.finish()
    return nc


def emit_l0(P, csT, w, bias, o, okey='st_o'):
    NB = L0_COLS // 512
    with ExitStack() as st:
        P.stack = st
        cs = P.sb([128, 96], F32)
        sT = P.sb([128, 96], F32)
        bt = P.sb([3, L0_COLS], F32)
        res = P.sb([3, L0_COLS], F32)
        P.dma('sp', cs[:], csT, writes=['cs'])
        P.dma('sp', bt[:], bias, writes=['bt'])
        P.op('act', lambda e: e.activation(out=sT[:], in_=cs[:], func=AF.Silu), reads=['cs'], writes=['sT'])
        NS = 4
        wts = [P.sb([128, L0_COLS], F32) for _ in range(NS)]
        pss = [P.ps([128, 512], F32) for _ in range(NB)]
        for kc in range(32):
            s = kc % NS
            P.dma('sp', wts[s][:], w[kc * 128:(kc + 1) * 128, :], writes=['wt%d' % s])
            for nb in range(NB):
                P.op('pe', lambda e, s=s, nb=nb, kc=kc: e.matmul(
                    pss[nb][0:3, :], lhsT=sT[:, kc * 3:(kc + 1) * 3], rhs=wts[s][:, nb * 512:(nb + 1) * 512],
                    start=(kc == 0), stop=(kc == 31)), reads=['sT', 'wt%d' % s], writes=['ps%d' % nb])
        for nb in range(NB):
            P.op('dve', lambda e, nb=nb: e.tensor_tensor(out=res[:, nb * 512:(nb + 1) * 512], in0=pss[nb][0:3, :],
                                                        in1=bt[:, nb * 512:(nb + 1) * 512], op=ALU.add),
                 reads=['ps%d' % nb, 'bt'], writes=['res%d' % nb])
        P.dma('sp', o, res[:], reads=['res%d' % nb for nb in range(NB)], writes=['l0out'], semkey=okey)
        P.barrier()
    P.stack = P.semstack


def run_l0(c, c_ctx, mod_w, mod_b):
    cs = np.concatenate([c, c_ctx[None, :]], axis=0)
    csT = np.ascontiguousarray(cs.reshape(3, 32, 128).transpose(2, 1, 0)).reshape(128, 96)
    wall = np.concatenate([mod_w[0], mod_w[1]], axis=1)
    ball = np.concatenate([mod_b[0], mod_b[1]], axis=0)
    in_maps = []
    for r in range(NCORES):
        sl = slice(r * L0_COLS, (r + 1) * L0_COLS)
        in_maps.append({"csT": csT, "w": np.ascontiguousarray(wall[:, sl]),
                        "bias": np.ascontiguousarray(np.broadcast_to(ball[sl][None, :], (3, L0_COLS)))})
    nc = build_l0()
    res = _launch(nc, in_maps)
    return np.concatenate([r["o"] for r in res], axis=1)


L2_TOK = 1024
L2_T = 256


def make_identity(P, dtype_out=BF16):
    idf = P.sb([128, 128], F32)
    P.op('pool', lambda e: e.memset(idf[:], 0.0), writes=['idf'])
    P.op('pool', lambda e: e.affine_select(out=idf[:], in_=idf[:], pattern=[[-1, 128]], compare_op=ALU.not_equal,
                                           fill=1.0, base=0, channel_multiplier=1), reads=['idf'], writes=['idf'])
    if dtype_out == F32:
        return idf, 'idf'
    idb = P.sb([128, 128], BF16)
    P.op('dve', lambda e: e.tensor_copy(out=idb[:], in_=idf[:]), reads=['idf'], writes=['idb'])
    return idb, 'idb'


def build_l2():
    nc = bass.Bass("TRN2", target_bir_lowering=False)
    io = {}
    io['x'] = nc.dram_tensor("x", [L2_TOK, D], F32, kind="ExternalInput").ap()
    io['yT'] = nc.dram_tensor("yT", [128, 32 * L2_TOK], BF16, kind="ExternalInput").ap()
    io['wo'] = nc.dram_tensor("wo", [8 * 128, 32 * 512], F32, kind="ExternalInput").ap()
    io['wi'] = nc.dram_tensor("wi", [32 * 128, 32 * 4 * 128], F32, kind="ExternalInput").ap()
    io['wo1'] = nc.dram_tensor("wo1", [8 * 128, 32 * 512], F32, kind="ExternalInput").ap()
    io['cw'] = nc.dram_tensor("cw", [128, 96], F32, kind="ExternalInput").ap()
    io['m1'] = nc.dram_tensor("m1", [128, 96], F32, kind="ExternalInput").ap()
    io['g0b'] = nc.dram_tensor("g0b", [128, D], F32, kind="ExternalInput").ap()
    io['g1b'] = nc.dram_tensor("g1b", [128, D], F32, kind="ExternalInput").ap()
    io['fnb'] = nc.dram_tensor("fnb", [128, D], F32, kind="ExternalInput").ap()
    io['out'] = nc.dram_tensor("out", [L2_TOK, D], F32, kind="ExternalOutput").ap()
    with ExitStack() as st:
        P = Prog(nc, st)
        emit_l2b(P, io)
        P.finish()
        print("L2 instructions:", P.ninst)
    return nc


def emit_l2(P, io, ysrc=None):
    x, wo, wi, wo1 = io['x'], io['wo'], io['wi'], io['wo1']
    cw_d, m1_d, g0b_d, g1b_d, fnb_d, out = io['cw'], io['m1'], io['g0b'], io['g1b'], io['fnb'], io['out']
    if ysrc is None:
        yT3 = io['yT'].rearrange("p (k t) -> p k t", k=32)
    NT = L2_T // 128
    with ExitStack() as st:
        P.stack = st
        idb, idk = make_identity(P)
        if ysrc is not None:
            idxy = P.sb([128, 128], mybir.dt.int32)
            P.dma('sp', idxy[:], ysrc[1], writes=['idxy'])
        cw = P.sb([128, 96], F32); m1 = P.sb([128, 96], F32)
        a1 = P.sb([128, 32], F32); a2 = P.sb([128, 32], F32)
        P.dma('sp', cw[:], cw_d, writes=['cw'])
        P.dma('sp', m1[:], m1_d, writes=['m1'])
        P.op('dve', lambda e: e.scalar_tensor_tensor(out=a1[:], in0=m1[:, 32:64], scalar=1.0, in1=m1[:, 64:96],
                                                     op0=ALU.add, op1=ALU.mult), reads=['m1'], writes=['a1'])
        P.op('dve', lambda e: e.tensor_copy(out=a2[:], in_=m1[:, 0:32]), reads=['m1'], writes=['a2'])
        gb = P.sb([128, D], F32)
        x1 = P.sb([128, NT, D], F32)
        yTs = P.sb([128, 32, L2_T], BF16)
        hn1T = P.sb([128, 32, L2_T], BF16)
        y1T = P.sb([128, 32, L2_T], BF16)
        xn = P.sb([128, NT, D], BF16)
        wsl = [P.sb([128, 32 * 512], BF16) for _ in range(2)]
        small = P.sb([128, 16], F32)
        tmp = [P.sb([128, 512], F32) for _ in range(2)]
        ev = [P.sb([128, 5, L2_T], F32) for _ in range(2)]
        psb = [P.ps([128, 512], F32) for _ in range(6)]
        pst = [P.ps([128, 1024], BF16) for _ in range(2)]
        wcount = [0]

        def load_w(src_rows):
            s = wcount[0] % 2
            wcount[0] += 1
            P.dma('pool', wsl[s][:], src_rows, writes=['w%d' % s])
            return s

        def proj_residual(src, srck, wdram, gate_dram, ps_base):
            P.dma('sp', gb[:], gate_dram, writes=['gb'])
            for nb in range(8):
                s = load_w(wdram[nb * 128:(nb + 1) * 128, :])
                wv = wsl[s][:].rearrange("p (k c) -> p k c", k=32)
                for tt in range(NT):
                    pi = (nb * NT + tt) % 2
                    ps = psb[pi]
                    for kc in range(32):
                        P.op('pe', lambda e, ps=ps, kc=kc, tt=tt, wv=wv: e.matmul(
                            ps[:], lhsT=src[:, kc, tt * 128:(tt + 1) * 128], rhs=wv[:, kc, :],
                            start=(kc == 0), stop=(kc == 31)), reads=list(srck) + ['w%d' % s], writes=['psb%d' % pi])
                    tb = tmp[(nb * NT + tt) % 2]; tk = 'tmp%d' % ((nb * NT + tt) % 2)
                    P.op('dve', lambda e, ps=ps, tb=tb, nb=nb: e.tensor_tensor(
                        out=tb[:], in0=ps[:], in1=gb[:, nb * 512:(nb + 1) * 512], op=ALU.mult),
                        reads=['gb'], writes=[tk, 'psb%d' % pi])
                    xk = 'x1.%d.%d' % (tt, nb)
                    P.op('dve', lambda e, tb=tb, nb=nb, tt=tt: e.tensor_tensor(
                        out=x1[:, tt, nb * 512:(nb + 1) * 512], in0=x1[:, tt, nb * 512:(nb + 1) * 512], in1=tb[:],
                        op=ALU.add), reads=[tk, xk], writes=[xk])

        for ps_i in range(L2_TOK // L2_T):
            t0 = ps_i * L2_T
            for tt in range(NT):
                P.dma('sp', x1[:, tt, :], x[t0 + tt * 128:t0 + (tt + 1) * 128, :],
                      writes=['x1.%d.%d' % (tt, nb) for nb in range(8)])
            if ysrc is None:
                P.dma('sp', yTs[:], yT3[:, :, t0:t0 + L2_T], writes=['yTs'])
            else:
                for kc in range(32):
                    col = kc * 4 + ps_i
                    P.custom('pool', lambda e, kc=kc, col=col: e.indirect_dma_start(
                        out=yTs[:, kc, :], out_offset=None, in_=ysrc[0],
                        in_offset=bass.IndirectOffsetOnAxis(ap=idxy[:, col:col + 1], axis=0)),
                        reads=['yg2', 'idxy'], writes=['yTs.%d' % kc], semkey='d_yTs', inc=16)
            proj_residual(yTs, ['yTs'] if ysrc is None else ['yTs.%d' % k for k in range(32)], wo, g0b_d, 0)
            for tt in range(NT):
                xkeys = ['x1.%d.%d' % (tt, nb) for nb in range(8)]
                ss = small[:, tt:tt + 1]; rs = small[:, 4 + tt:5 + tt]
                P.op('dve', lambda e, ss=ss: e.memset(ss, 0.0), writes=['ss%d' % tt])
                P.op('act', lambda e, tt=tt, ss=ss: e.activation(out=xn[:, tt, :], in_=x1[:, tt, :], func=AF.Square, accum_out=ss),
                     reads=xkeys + ['ss%d' % tt], writes=['xn%d' % tt, 'ss%d' % tt])
                P.op('act', lambda e, ss=ss, rs=rs: e.activation(out=rs, in_=ss, func=AF.Sqrt, bias=EPS, scale=1.0 / D),
                     reads=['ss%d' % tt], writes=['rs%d' % tt])
                P.op('dve', lambda e, rs=rs: e.reciprocal(out=rs, in_=rs), reads=['rs%d' % tt], writes=['rs%d' % tt])
                P.op('dve', lambda e, tt=tt, rs=rs: e.tensor_scalar(out=xn[:, tt, :], in0=x1[:, tt, :], scalar1=rs, scalar2=None,
                                                                    op0=ALU.mult), reads=xkeys + ['rs%d' % tt], writes=['xn%d' % tt])
            for kc in range(32):
                half = kc % 2
                pk = 'pst%d' % half
                for tt in range(NT):
                    P.op('pe', lambda e, kc=kc, tt=tt, half=half: e.transpose(
                        out=pst[half][:, tt * 128:(tt + 1) * 128], in_=xn[:, tt, kc * 128:(kc + 1) * 128],
                        identity=idb[:]), reads=['xn%d' % tt, idk], writes=[pk])
                P.op('act', lambda e, kc=kc, half=half: e.activation(
                    out=hn1T[:, kc, :], in_=pst[half][:, 0:L2_T], func=AF.Identity,
                    bias=a2[:, kc:kc + 1], scale=a1[:, kc:kc + 1]), reads=['a1', 'a2'], writes=['hn1T', pk])
            for cc in range(32):
                s = load_w(wi[cc * 128:(cc + 1) * 128, :])
                wv = wsl[s][:].rearrange("p (k g c) -> p k g c", k=32, g=4)
                par = cc % 2
                pg = []
                for g in range(4):
                    bank = psb[2 + par * 2 + g // 2]
                    pv = bank[:, (g % 2) * 256:(g % 2) * 256 + L2_T]
                    pkey = 'pcb%d.%d' % (par, g // 2)
                    for kc in range(32):
                        P.op('pe', lambda e, pv=pv, kc=kc, g=g, wv=wv: e.matmul(
                            pv, lhsT=wv[:, kc, g, :], rhs=hn1T[:, kc, :], start=(kc == 0), stop=(kc == 31)),
                            reads=['hn1T', 'w%d' % s], writes=[pkey])
                    pg.append((pv, pkey))
                E = ev[par]; ek = 'ev%d' % par
                xin, cx, z, sg, tq = E[:, 0, :], E[:, 1, :], E[:, 2, :], E[:, 3, :], E[:, 4, :]
                P.op('act', lambda e, xin=xin, pv=pg[2][0]: e.activation(out=xin, in_=pv, func=AF.Copy),
                     writes=[ek + 'x', pg[2][1]])
                P.op('act', lambda e, sg=sg, pv=pg[3][0]: e.activation(out=sg, in_=pv, func=AF.Silu),
                     writes=[ek + 's', pg[3][1]])
                P.op('dve', lambda e, cx=cx, xin=xin, pv=pg[1][0]: e.tensor_tensor(out=cx, in0=pv, in1=xin, op=ALU.mult),
                     reads=[ek + 'x'], writes=[ek + 'c', pg[1][1]])
                cx3 = cx.rearrange("p (r c) -> p r c", c=64)
                z3 = z.rearrange("p (r c) -> p r c", c=64)
                P.op('dve', lambda e, z=z, cx=cx, cc=cc: e.tensor_scalar(out=z, in0=cx, scalar1=cw[:, cc * 3 + 1:cc * 3 + 2], scalar2=None,
                                                                        op0=ALU.mult), reads=[ek + 'c', 'cw'], writes=[ek + 'z'])
                P.op('dve', lambda e, z3=z3, cx3=cx3, cc=cc: e.scalar_tensor_tensor(
                    out=z3[:, :, 1:64], in0=cx3[:, :, 0:63], scalar=cw[:, cc * 3:cc * 3 + 1], in1=z3[:, :, 1:64],
                    op0=ALU.mult, op1=ALU.add), reads=[ek + 'c', ek + 'z', 'cw'], writes=[ek + 'z'])
                P.op('dve', lambda e, z3=z3, cx3=cx3, cc=cc: e.scalar_tensor_tensor(
                    out=z3[:, :, 0:63], in0=cx3[:, :, 1:64], scalar=cw[:, cc * 3 + 2:cc * 3 + 3], in1=z3[:, :, 0:63],
                    op0=ALU.mult, op1=ALU.add), reads=[ek + 'c', ek + 'z', 'cw'], writes=[ek + 'z'])
                P.op('dve', lambda e, tq=tq, z=z, sg=sg: e.tensor_tensor(out=tq, in0=z, in1=sg, op=ALU.mult),
                     reads=[ek + 'z', ek + 's'], writes=[ek + 't'])
                P.op('dve', lambda e, tq=tq, cc=cc, pv=pg[0][0]: e.tensor_tensor(out=y1T[:, cc, :], in0=pv, in1=tq, op=ALU.mult),
                     reads=[ek + 't'], writes=['y1T', pg[0][1]])
            proj_residual(y1T, ['y1T'], wo1, g1b_d, 0)
            P.dma('sp', gb[:], fnb_d, writes=['gb'])
            for tt in range(NT):
                xkeys = ['x1.%d.%d' % (tt, nb) for nb in range(8)]
                ss = small[:, 8 + tt:9 + tt]; rs = small[:, 12 + tt:13 + tt]
                P.op('dve', lambda e, ss=ss: e.memset(ss, 0.0), writes=['fs%d' % tt])
                P.op('act', lambda e, tt=tt, ss=ss: e.activation(out=xn[:, tt, :], in_=x1[:, tt, :], func=AF.Square, accum_out=ss),
                     reads=xkeys + ['fs%d' % tt], writes=['xn%d' % tt, 'fs%d' % tt])
                P.op('act', lambda e, ss=ss, rs=rs: e.activation(out=rs, in_=ss, func=AF.Sqrt, bias=EPS, scale=1.0 / D),
                     reads=['fs%d' % tt], writes=['fr%d' % tt])
                P.op('dve', lambda e, rs=rs: e.reciprocal(out=rs, in_=rs), reads=['fr%d' % tt], writes=['fr%d' % tt])
                P.op('dve', lambda e, tt=tt, rs=rs: e.scalar_tensor_tensor(
                    out=x1[:, tt, :], in0=x1[:, tt, :], scalar=rs, in1=gb[:], op0=ALU.mult, op1=ALU.mult),
                    reads=xkeys + ['fr%d' % tt, 'gb'], writes=xkeys)
                P.dma('sp', out[t0 + tt * 128:t0 + (tt + 1) * 128, :], x1[:, tt, :], reads=xkeys, semkey='st_out%d' % tt)
        P.barrier()
    P.stack = P.semstack


def emit_l2b(P, io):
    T = 512
    NT = 4
    x, wo, wi, wo1 = io['x'], io['wo'], io['wi'], io['wo1']
    cw_d, m1_d, g0b_d, g1b_d, fnb_d, out = io['cw'], io['m1'], io['g0b'], io['g1b'], io['fnb'], io['out']
    yT3 = io['yT'].rearrange("p (k t) -> p k t", k=32)
    with ExitStack() as st:
        P.stack = st
        idb, idk = make_identity(P)
        cw = P.sb([128, 96], F32); m1 = P.sb([128, 96], F32)
        a1 = P.sb([128, 32], F32); a2 = P.sb([128, 32], F32)
        P.dma('sp', cw[:], cw_d, writes=['cw'])
        P.dma('sp', m1[:], m1_d, writes=['m1'])
        P.op('dve', lambda e: e.scalar_tensor_tensor(out=a1[:], in0=m1[:, 32:64], scalar=1.0, in1=m1[:, 64:96],
                                                     op0=ALU.add, op1=ALU.mult), reads=['m1'], writes=['a1'])
        P.op('dve', lambda e: e.tensor_copy(out=a2[:], in_=m1[:, 0:32]), reads=['m1'], writes=['a2'])
        x1 = P.sb([128, NT, D], F32)
        bufA = P.sb([128, 32, T], BF16)
        bufB = P.sb([128, 32 * T], BF16)
        yTs = bufA; hn1T = bufA
        xn = bufB[:].rearrange("p (t d) -> p t d", t=NT)
        y1T = bufB[:].rearrange("p (k t) -> p k t", k=32)
        AK = ['bufA']; BK = ['bufB']
        gblk = [P.sb([128, 512], F32) for _ in range(2)]
        wsl = [P.sb([128, 32 * 512], BF16) for _ in range(2)]
        small = P.sb([128, 16], F32)
        tmp = [P.sb([128, 512], F32) for _ in range(2)]
        E = P.sb([128, 3, T], F32)
        psb = [P.ps([128, 512], F32) for _ in range(6)]
        pst = [P.ps([128, 1024], BF16) for _ in range(2)]
        wcount = [0]
        gcount = [0]

        def load_w(src_rows):
            s = wcount[0] % 2
            wcount[0] += 1
            P.dma('pool', wsl[s][:], src_rows, writes=['w%d' % s])
            return s

        def load_g(gate_dram, nb):
            gi = gcount[0] % 2
            gcount[0] += 1
            P.dma('sp', gblk[gi][:], gate_dram[:, nb * 512:(nb + 1) * 512], writes=['gblk%d' % gi])
            return gi

        def proj_residual(src, srck, wdram, gate_dram):
            for nb in range(8):
                s = load_w(wdram[nb * 128:(nb + 1) * 128, :])
                gi = load_g(gate_dram, nb)
                wv = wsl[s][:].rearrange("p (k c) -> p k c", k=32)
                for tt in range(NT):
                    pi = (nb * NT + tt) % 2
                    ps = psb[pi]
                    for kc in range(32):
                        P.op('pe', lambda e, ps=ps, kc=kc, tt=tt, wv=wv: e.matmul(
                            ps[:], lhsT=src[:, kc, tt * 128:(tt + 1) * 128], rhs=wv[:, kc, :],
                            start=(kc == 0), stop=(kc == 31)), reads=list(srck) + ['w%d' % s], writes=['psb%d' % pi])
                    tb = tmp[pi]; tk = 'tmp%d' % pi
                    P.op('dve', lambda e, ps=ps, tb=tb, gi=gi: e.tensor_tensor(out=tb[:], in0=ps[:], in1=gblk[gi][:], op=ALU.mult),
                         reads=['gblk%d' % gi], writes=[tk, 'psb%d' % pi])
                    xk = 'x1.%d.%d' % (tt, nb)
                    P.op('dve', lambda e, tb=tb, nb=nb, tt=tt: e.tensor_tensor(
                        out=x1[:, tt, nb * 512:(nb + 1) * 512], in0=x1[:, tt, nb * 512:(nb + 1) * 512], in1=tb[:],
                        op=ALU.add), reads=[tk, xk], writes=[xk])

        def row_stats(tt, base):
            xkeys = ['x1.%d.%d' % (tt, nb) for nb in range(8)]
            ss = small[:, base + tt:base + tt + 1]; rs = small[:, base + 4 + tt:base + 5 + tt]
            sk = 'ss%d.%d' % (base, tt); rk = 'rs%d.%d' % (base, tt)
            P.op('dve', lambda e, ss=ss: e.memset(ss, 0.0), writes=[sk])
            P.op('act', lambda e, tt=tt, ss=ss: e.activation(out=xn[:, tt, :], in_=x1[:, tt, :], func=AF.Square, accum_out=ss),
                 reads=xkeys + [sk], writes=BK + [sk])
            P.op('act', lambda e, ss=ss, rs=rs: e.activation(out=rs, in_=ss, func=AF.Sqrt, bias=EPS, scale=1.0 / D), reads=[sk], writes=[rk])
            P.op('dve', lambda e, rs=rs: e.reciprocal(out=rs, in_=rs), reads=[rk], writes=[rk])
            return xkeys, rs, rk

        for ps_i in range(L2_TOK // T):
            t0 = ps_i * T
            for tt in range(NT):
                P.dma('sp', x1[:, tt, :], x[t0 + tt * 128:t0 + (tt + 1) * 128, :], writes=['x1.%d.%d' % (tt, nb) for nb in range(8)])
            P.dma('sp', yTs[:], yT3[:, :, t0:t0 + T], writes=AK)
            proj_residual(yTs, AK, wo, g0b_d)
            for tt in range(NT):
                xkeys, rs, rk = row_stats(tt, 0)
                P.op('dve', lambda e, tt=tt, rs=rs: e.tensor_scalar(out=xn[:, tt, :], in0=x1[:, tt, :], scalar1=rs, scalar2=None, op0=ALU.mult),
                     reads=xkeys + [rk], writes=BK)
            for kc in range(32):
                half = kc % 2
                pk = 'pst%d' % half
                for tt in range(NT):
                    P.op('pe', lambda e, kc=kc, tt=tt, half=half: e.transpose(
                        out=pst[half][:, tt * 128:(tt + 1) * 128], in_=xn[:, tt, kc * 128:(kc + 1) * 128], identity=idb[:]),
                        reads=BK + [idk], writes=[pk])
                P.op('act', lambda e, kc=kc, half=half: e.activation(
                    out=hn1T[:, kc, :], in_=pst[half][:, 0:T], func=AF.Identity, bias=a2[:, kc:kc + 1], scale=a1[:, kc:kc + 1]),
                    reads=['a1', 'a2'], writes=AK + [pk])
            for cc in range(32):
                s = load_w(wi[cc * 128:(cc + 1) * 128, :])
                wv = wsl[s][:].rearrange("p (k g c) -> p k g c", k=32, g=4)
                pg = {}
                for g in (1, 2, 3, 0):
                    pv = psb[2 + g][:, 0:T]
                    pkey = 'pcb%d' % g
                    for kc in range(32):
                        P.op('pe', lambda e, pv=pv, kc=kc, g=g, wv=wv: e.matmul(
                            pv, lhsT=wv[:, kc, g, :], rhs=hn1T[:, kc, :], start=(kc == 0), stop=(kc == 31)),
                            reads=AK + ['w%d' % s], writes=[pkey])
                    pg[g] = (pv, pkey)
                xin, z, sg = E[:, 0, :], E[:, 1, :], E[:, 2, :]
                cx = xin
                tq = sg
                P.op('act', lambda e, pv=pg[2][0]: e.activation(out=xin, in_=pv, func=AF.Copy), writes=['evx', 'evc', pg[2][1]])
                P.op('act', lambda e, pv=pg[3][0]: e.activation(out=sg, in_=pv, func=AF.Silu), writes=['evs', 'evt', pg[3][1]])
                P.op('dve', lambda e, pv=pg[1][0]: e.tensor_tensor(out=cx, in0=pv, in1=xin, op=ALU.mult), reads=['evx'], writes=['evc', 'evx', pg[1][1]])
                cx3 = cx.rearrange("p (r c) -> p r c", c=64)
                z3 = z.rearrange("p (r c) -> p r c", c=64)
                P.op('dve', lambda e, cc=cc: e.tensor_scalar(out=z, in0=cx, scalar1=cw[:, cc * 3 + 1:cc * 3 + 2], scalar2=None, op0=ALU.mult),
                     reads=['evc', 'cw'], writes=['evz'])
                P.op('dve', lambda e, cc=cc: e.scalar_tensor_tensor(out=z3[:, :, 1:64], in0=cx3[:, :, 0:63], scalar=cw[:, cc * 3:cc * 3 + 1],
                                                                  in1=z3[:, :, 1:64], op0=ALU.mult, op1=ALU.add), reads=['evc', 'evz', 'cw'], writes=['evz'])
                P.op('dve', lambda e, cc=cc: e.scalar_tensor_tensor(out=z3[:, :, 0:63], in0=cx3[:, :, 1:64], scalar=cw[:, cc * 3 + 2:cc * 3 + 3],
                                                                  in1=z3[:, :, 0:63], op0=ALU.mult, op1=ALU.add), reads=['evc', 'evz', 'cw'], writes=['evz'])
                P.op('dve', lambda e: e.tensor_tensor(out=tq, in0=z, in1=sg, op=ALU.mult), reads=['evz', 'evs'], writes=['evt', 'evs'])
                P.op('dve', lambda e, cc=cc, pv=pg[0][0]: e.tensor_tensor(out=y1T[:, cc, :], in0=pv, in1=tq, op=ALU.mult),
                     reads=['evt'], writes=BK + [pg[0][1]])
            proj_residual(y1T, BK, wo1, g1b_d)
            stats = [row_stats(tt, 8) for tt in range(NT)]
            for nb in range(8):
                gi = load_g(fnb_d, nb)
                for tt in range(NT):
                    xkeys, rs, rk = stats[tt]
                    xk = 'x1.%d.%d' % (tt, nb)
                    P.op('dve', lambda e, tt=tt, nb=nb, rs=rs, gi=gi: e.scalar_tensor_tensor(
                        out=x1[:, tt, nb * 512:(nb + 1) * 512], in0=x1[:, tt, nb * 512:(nb + 1) * 512], scalar=rs, in1=gblk[gi][:],
                        op0=ALU.mult, op1=ALU.mult), reads=[xk, rk, 'gblk%d' % gi], writes=[xk])
            for tt in range(NT):
                P.dma('sp', out[t0 + tt * 128:t0 + (tt + 1) * 128, :], x1[:, tt, :], reads=['x1.%d.%d' % (tt, nb) for nb in range(8)],
                      semkey='st_out%d' % tt)
        P.barrier()
    P.stack = P.semstack


def l2_layouts(inputs):
    def blk(w):
        return np.ascontiguousarray(w.reshape(32, 128, 8, 512).transpose(2, 1, 0, 3)).reshape(8 * 128, 32 * 512)
    wo = blk(inputs['ab_w_out'][0])
    wo1 = blk(inputs['sc_w_out'][0])
    wi = inputs['sc_w_in'][0].reshape(32, 128, 4, 32, 128)
    wi = np.ascontiguousarray(wi.transpose(3, 1, 0, 2, 4)).reshape(32 * 128, 32 * 4 * 128)
    cw = np.ascontiguousarray(inputs['sc_conv'][0].reshape(3, 32, 128).transpose(2, 1, 0)).reshape(128, 96)
    return wo, wi, wo1, cw


def pcol(v):
    return np.ascontiguousarray(v.reshape(32, 128).T)


def run_l2(inputs, modv, yT_full):
    wo, wi, wo1, cw = l2_layouts(inputs)
    fnb = np.ascontiguousarray(np.broadcast_to(inputs['final_norm_w'][None, :], (128, D)))
    in_maps = []
    for r in range(NCORES):
        b, tq = r // 4, r % 4
        tsl = slice(tq * L2_TOK, (tq + 1) * L2_TOK)
        gate0 = modv[b, 2 * D:3 * D]
        sh1, sc1, gate1 = (modv[b, 3 * D + i * D:3 * D + (i + 1) * D] for i in range(3))
        m1 = np.concatenate([pcol(sh1), pcol(sc1), pcol(inputs['norm_w'][1])], axis=1)
        yTl = np.ascontiguousarray(yT_full[b][:, tsl].reshape(32, 128, L2_TOK).transpose(1, 0, 2)).reshape(128, 32 * L2_TOK)
        in_maps.append({
            "x": np.ascontiguousarray(inputs['x'][b, tsl]), "yT": yTl, "wo": wo, "wi": wi, "wo1": wo1, "cw": cw,
            "m1": np.ascontiguousarray(m1),
            "g0b": np.ascontiguousarray(np.broadcast_to(gate0[None, :], (128, D))),
            "g1b": np.ascontiguousarray(np.broadcast_to(gate1[None, :], (128, D))),
            "fnb": fnb})
    nc = build_l2()
    res = _launch(nc, in_maps)
    out = np.zeros((2, 4096, D), np.float32)
    for r in range(NCORES):
        b, tq = r // 4, r % 4
        out[b, tq * L2_TOK:(tq + 1) * L2_TOK] = res[r]["out"]
    return out


L1_TOK = 4352
NCHK = 34
XW = 4358
NEG = -1.0e30
M_MASKF, M_MASKB, M_NEGF, M_NEGB, M_STRF, M_STRB, M_BD2, M_LM0, M_ID = 0, 1, 2, 3, 4, 5, 6, 7, 13


def l1_groups():
    gs = [(0, 0, 256)]
    for g in range(1, 9):
        gs.append((g, 256 + 512 * (g - 1), 512))
    return gs


def xpad(u):
    return u + 2 if u < 256 else u + 5


L1_INPUTS = [("xa", [L1_TOK, D]), ("wmain", [24 * 128, 32 * 128]), ("wab", [128, 32 * 16]), ("mod0", [128, 160]),
             ("cwq", [128, 48]), ("alog", [128, 272]), ("dtb", [128, 272]), ("dnw", [128, 1]), ("lcw", [128, 16]),
             ("lcb", [128, 4]), ("lwr", [128, 1024]), ("lwi", [128, 1024]), ("lbr", [128, 8]), ("lbi", [128, 8]),
             ("llam", [128, 8]), ("msk", [128, 14 * 128])]


def build_l1(debug=False):
    nc = bass.Bass("TRN2", target_bir_lowering=False)
    io = {}
    for name, shape in L1_INPUTS:
        io[name] = nc.dram_tensor(name, list(shape), F32, kind="ExternalInput").ap()
    io['yT'] = nc.dram_tensor("yT", [1024, 4096], BF16, kind="ExternalOutput").ap()
    io['hnT_d'] = nc.dram_tensor("hnT_d", [128, 9 * 32 * 512], BF16, kind="Internal").ap()
    with ExitStack() as st:
        P = Prog(nc, st)
        emit_l1(P, io)
        P.finish()
        print("L1 instructions:", P.ninst)
    return nc


def emit_l1(P, io):
    xa, wmain, wab, mod0 = io['xa'], io['wmain'], io['wab'], io['mod0']
    cwq_d, alog_d, dtb_d, dnw_d = io['cwq'], io['alog'], io['dtb'], io['dnw']
    lcw_d, lcb_d, lwr_d, lwi_d = io['lcw'], io['lcb'], io['lwr'], io['lwi']
    lbr_d, lbi_d, llam_d, msk_d = io['lbr'], io['lbi'], io['llam'], io['msk']
    yT = io['yT']
    hnTg = io['hnT_d'].rearrange("p (g k t) -> p g k t", g=9, k=32)
    groups = l1_groups()

    with ExitStack() as st:
        P.stack = st
        msk = P.sb([128, 14, 128], F32)
        P.dma('sp', msk[:].rearrange("p a b -> p (a b)"), msk_d, writes=['msk'])
        ident = msk[:, M_ID, :]
        identb = P.sb([128, 128], BF16)
        P.op('dve', lambda e: e.tensor_copy(out=identb[:], in_=ident), reads=['msk'], writes=['identb'])
        ones = P.sb([128, 128], F32)
        P.op('pool', lambda e: e.memset(ones[:], 1.0), writes=['ones'])
        m0 = P.sb([128, 160], F32)
        P.dma('sp', m0[:], mod0, writes=['m0'])
        a1 = P.sb([128, 64], F32); a2 = P.sb([128, 64], F32)
        P.op('dve', lambda e: e.scalar_tensor_tensor(out=a1[:, 0:32], in0=m0[:, 32:64], scalar=1.0, in1=m0[:, 128:160],
                                                     op0=ALU.add, op1=ALU.mult), reads=['m0'], writes=['a1'])
        P.op('dve', lambda e: e.scalar_tensor_tensor(out=a1[:, 32:64], in0=m0[:, 96:128], scalar=1.0, in1=m0[:, 128:160],
                                                     op0=ALU.add, op1=ALU.mult), reads=['m0', 'a1'], writes=['a1'])
        P.op('dve', lambda e: e.tensor_copy(out=a2[:, 0:32], in_=m0[:, 0:32]), reads=['m0'], writes=['a2'])
        P.op('dve', lambda e: e.tensor_copy(out=a2[:, 32:64], in_=m0[:, 64:96]), reads=['m0', 'a2'], writes=['a2'])
        cwq = P.sb([128, 48], F32); P.dma('sp', cwq[:], cwq_d, writes=['cwq'])
        dnw = P.sb([128, 1], F32); P.dma('sp', dnw[:], dnw_d, writes=['dnw'])
        ABT = P.sb([128, NCHK, 16], F32)
        BETA = P.sb([128, NCHK, 8], F32); NBETA = P.sb([128, NCHK, 8], F32)
        G = P.sb([128, NCHK, 8], F32); GC = P.sb([128, NCHK, 8], F32)
        EGC = P.sb([128, NCHK, 8], F32); GE = P.sb([128, NCHK, 8], F32); DEND = P.sb([128, NCHK, 8], F32)
        stat = P.sb([128, 64], F32)

        with ExitStack() as ph:
            xt = [P.sb([128, D], F32, stack=ph) for _ in range(3)]
            xnb = [P.sb([128, 4, D], BF16, stack=ph) for _ in range(2)]
            hst = [P.sb([128, 32, 512], BF16, stack=ph) for _ in range(2)]
            wabs = P.sb([128, 32, 16], BF16, stack=ph)
            P.dma('pool', wabs[:].rearrange("p k c -> p (k c)"), wab, writes=['wabs'])
            pst = [P.ps([128, 1024], BF16, stack=ph) for _ in range(2)]
            psab = [P.ps([128, 512], F32, stack=ph) for _ in range(2)]
            ti = 0
            for (g, tok0, W) in groups:
                ntile = W // 128
                gs = g % 2
                mo = 32 if g == 0 else 0
                for tt in range(ntile):
                    s = ti % 3
                    sc = ti % 16
                    ss = stat[:, sc:sc + 1]; rs = stat[:, 16 + sc:17 + sc]
                    P.dma('sp', xt[s][:], xa[tok0 + tt * 128: tok0 + (tt + 1) * 128, :], writes=['xt%d' % s])
                    P.op('dve', lambda e, ss=ss: e.memset(ss, 0.0), writes=['ss%d' % sc])
                    P.op('act', lambda e, s=s, gs=gs, tt=tt, ss=ss: e.activation(out=xnb[gs][:, tt, :], in_=xt[s][:], func=AF.Square, accum_out=ss),
                         reads=['xt%d' % s, 'ss%d' % sc], writes=['xnb%d.%d' % (gs, tt), 'ss%d' % sc])
                    P.op('act', lambda e, ss=ss, rs=rs: e.activation(out=rs, in_=ss, func=AF.Sqrt, bias=EPS, scale=1.0 / D),
                         reads=['ss%d' % sc], writes=['rs%d' % sc])
                    P.op('dve', lambda e, rs=rs: e.reciprocal(out=rs, in_=rs), reads=['rs%d' % sc], writes=['rs%d' % sc])
                    P.op('dve', lambda e, s=s, gs=gs, tt=tt, rs=rs: e.tensor_scalar(out=xnb[gs][:, tt, :], in0=xt[s][:], scalar1=rs, scalar2=None, op0=ALU.mult),
                         reads=['xt%d' % s, 'rs%d' % sc], writes=['xnb%d.%d' % (gs, tt)])
                    ti += 1
                for kc in range(32):
                    half = kc % 2
                    for tt in range(ntile):
                        P.op('pe', lambda e, kc=kc, tt=tt, gs=gs, half=half: e.transpose(
                            out=pst[half][:, tt * 128:(tt + 1) * 128], in_=xnb[gs][:, tt, kc * 128:(kc + 1) * 128], identity=identb[:]),
                            reads=['xnb%d.%d' % (gs, tt), 'identb'], writes=['pst%d' % half])
                    P.op('act', lambda e, kc=kc, gs=gs, half=half, W=W, mo=mo: e.activation(
                        out=hst[gs][:, kc, 0:W], in_=pst[half][:, 0:W], func=AF.Identity,
                        bias=a2[:, mo + kc:mo + kc + 1], scale=a1[:, mo + kc:mo + kc + 1]),
                        reads=['a1', 'a2'], writes=['hst%d' % gs, 'pst%d' % half])
                for tt in range(ntile):
                    T = tok0 // 128 + tt
                    bank = psab[0] if T < 32 else psab[1]
                    bk = 'psab0' if T < 32 else 'psab1'
                    col = (T % 32) * 16
                    for kc in range(32):
                        P.op('pe', lambda e, bank=bank, col=col, kc=kc, gs=gs, tt=tt: e.matmul(
                            bank[:, col:col + 16], lhsT=hst[gs][:, kc, tt * 128:(tt + 1) * 128], rhs=wabs[:, kc, :],
                            start=(kc == 0), stop=(kc == 31)), reads=['hst%d' % gs, 'wabs'], writes=[bk])
                P.dma('sp', hnTg[:, g, :, 0:W], hst[gs][:, :, 0:W], reads=['hst%d' % gs], writes=['hnT.%d' % g])
            ABf = ABT[:].rearrange("p c k -> p (c k)")
            P.op('dve', lambda e: e.tensor_copy(out=ABf[:, 0:512], in_=psab[0][:, :]), writes=['ABT', 'psab0'])
            P.op('dve', lambda e: e.tensor_copy(out=ABf[:, 512:544], in_=psab[1][:, 0:32]), reads=['ABT'], writes=['ABT', 'psab1'])
        P.barrier()

        with ExitStack() as ph:
            alog = P.sb([128, NCHK, 8], F32, stack=ph); dtb = P.sb([128, NCHK, 8], F32, stack=ph)
            P.dma('sp', alog[:].rearrange("p c k -> p (c k)"), alog_d, writes=['alog'])
            P.dma('sp', dtb[:].rearrange("p c k -> p (c k)"), dtb_d, writes=['dtb'])
            tz = P.sb([128, NCHK, 8], F32, stack=ph)
            GF = P.sb([128, 2, NCHK * 4], F32, stack=ph)
            pg = [P.ps([128, 512], F32, stack=ph) for _ in range(3)]
            P.op('act', lambda e: e.activation(out=BETA[:], in_=ABT[:, :, 0:8], func=AF.Sigmoid), reads=['ABT'], writes=['BETA'])
            P.op('act', lambda e: e.mul(out=NBETA[:], in_=BETA[:], mul=-1.0), reads=['BETA'], writes=['NBETA'])
            P.op('dve', lambda e: e.tensor_tensor(out=tz[:], in0=ABT[:, :, 8:16], in1=dtb[:], op=ALU.add), reads=['ABT', 'dtb'], writes=['tz'])
            P.op('act', lambda e: e.activation(out=tz[:], in_=tz[:], func=AF.Exp), reads=['tz'], writes=['tz'])
            P.op('act', lambda e: e.activation(out=tz[:], in_=tz[:], func=AF.Ln, bias=1.0, scale=1.0), reads=['tz'], writes=['tz'])
            P.op('act', lambda e: e.activation(out=alog[:], in_=alog[:], func=AF.Exp), reads=['alog'], writes=['alog'])
            P.op('dve', lambda e: e.scalar_tensor_tensor(out=G[:], in0=tz[:], scalar=-1.0, in1=alog[:], op0=ALU.mult, op1=ALU.mult),
                 reads=['tz', 'alog'], writes=['G'])
            for d in range(2):
                P.op('dve', lambda e, d=d: e.tensor_copy(out=GF[:, d, :].rearrange("p (c k) -> p c k", k=4), in_=G[:, :, d * 4:(d + 1) * 4]),
                     reads=['G', 'GF'], writes=['GF'])
            for d in range(2):
                P.op('pe', lambda e, d=d: e.matmul(pg[d][:, 0:NCHK * 4], lhsT=msk[:, M_MASKF + d, :], rhs=GF[:, d, :], start=True, stop=True),
                     reads=['msk', 'GF'], writes=['pg%d' % d])
                P.op('dve', lambda e, d=d: e.tensor_copy(out=GC[:, :, d * 4:(d + 1) * 4], in_=pg[d][:, 0:NCHK * 4].rearrange("p (c k) -> p c k", k=4)),
                     reads=['GC'], writes=['GC', 'pg%d' % d])
            P.op('pe', lambda e: e.matmul(pg[2][:, 0:NCHK * 8], lhsT=ones[:], rhs=G[:].rearrange("p c k -> p (c k)"), start=True, stop=True),
                 reads=['ones', 'G'], writes=['pg2'])
            P.op('dve', lambda e: e.tensor_copy(out=tz[:].rearrange("p c k -> p (c k)"), in_=pg[2][:, 0:NCHK * 8]), reads=['tz'], writes=['tz', 'pg2'])
            P.op('act', lambda e: e.activation(out=GE[:], in_=tz[:], func=AF.Exp), reads=['tz'], writes=['GE'])
            P.op('act', lambda e: e.activation(out=EGC[:], in_=GC[:], func=AF.Exp), reads=['GC'], writes=['EGC'])
            P.op('dve', lambda e: e.tensor_tensor(out=tz[:], in0=tz[:], in1=GC[:], op=ALU.subtract), reads=['tz', 'GC'], writes=['tz'])
            P.op('act', lambda e: e.activation(out=DEND[:], in_=tz[:], func=AF.Exp), reads=['tz'], writes=['DEND'])
        P.barrier()

        def inproj(ph, chunks, evac, skip=lambda ci, g: False):
            n = len(chunks)
            wsb = P.sb([128, n, 32 * 128], BF16, stack=ph)
            for ci, ch in enumerate(chunks):
                P.dma('pool', wsb[:, ci, :], wmain[ch * 128:(ch + 1) * 128, :], writes=['wsb%d' % ci])
            hg = [P.sb([128, 32, 512], BF16, stack=ph) for _ in range(2)]
            psb = [P.ps([128, 512], F32, stack=ph) for _ in range(3)]
            cnt = 0
            for (g, tok0, W) in groups:
                s = g % 2
                P.dma('sp', hg[s][:, :, 0:W], hnTg[:, g, :, 0:W], reads=['hnT.%d' % g], writes=['hg%d' % s])
                for ci, ch in enumerate(chunks):
                    if skip(ci, g):
                        continue
                    pi = cnt % 3
                    cnt += 1
                    wv = wsb[:, ci, :].rearrange("p (k c) -> p k c", k=32)
                    for kc in range(32):
                        P.op('pe', lambda e, pi=pi, W=W, wv=wv, kc=kc, s=s: e.matmul(
                            psb[pi][:, 0:W], lhsT=wv[:, kc, :], rhs=hg[s][:, kc, 0:W], start=(kc == 0), stop=(kc == 31)),
                            reads=['wsb%d' % ci, 'hg%d' % s], writes=['ipb%d' % pi])
                    evac(ci, g, tok0, W, psb[pi][:, 0:W], 'ipb%d' % pi)

        def conv4(X, S_, wt, wbase, bias=None, eng='dve'):
            for (lo, hi) in ((0, 256), (256, L1_TOK)):
                p0 = xpad(lo) - 2
                n = hi - lo
                for j in range(4):
                    src = X[:, p0 + j:p0 + j + n]
                    w = wt[:, wbase + j:wbase + j + 1]
                    if j == 0:
                        if bias is None:
                            P.op(eng, lambda e, src=src, w=w, lo=lo, hi=hi: e.tensor_scalar(out=S_[:, lo:hi], in0=src, scalar1=w, scalar2=None, op0=ALU.mult),
                                 reads=['X', 'cw'], writes=['Sw'])
                        else:
                            P.op(eng, lambda e, src=src, w=w, lo=lo, hi=hi: e.tensor_scalar(out=S_[:, lo:hi], in0=src, scalar1=w, scalar2=bias, op0=ALU.mult, op1=ALU.add),
                                 reads=['X', 'cw'], writes=['Sw'])
                    else:
                        P.op(eng, lambda e, src=src, w=w, lo=lo, hi=hi: e.scalar_tensor_tensor(out=S_[:, lo:hi], in0=src, scalar=w, in1=S_[:, lo:hi], op0=ALU.mult, op1=ALU.add),
                             reads=['X', 'cw', 'Sw'], writes=['Sw'])

        blocks9 = [(0, 256)] + [(256 + 512 * i, 256 + 512 * (i + 1)) for i in range(8)]

        for hl in range(4):
            with ExitStack() as hs:
                XQ = [P.sb([128, XW], F32, stack=hs) for _ in range(3)]
                SG = P.sb([128, 4096], BF16, stack=hs)
                for i in range(3):
                    P.op('pool', lambda e, i=i: e.memset(XQ[i][:], 0.0), writes=['X%d.%d' % (i, g) for g in range(9)])
                with ExitStack() as ph:
                    def evac_head(ci, g, tok0, W, ps, pk):
                        if ci < 3:
                            dst = XQ[ci][:, xpad(tok0):xpad(tok0) + W]
                            if (ci + g) % 2 == 0:
                                P.op('act', lambda e, dst=dst, ps=ps: e.activation(out=dst, in_=ps, func=AF.Copy), writes=['X%d.%d' % (ci, g), pk])
                            else:
                                P.op('dve', lambda e, dst=dst, ps=ps: e.tensor_copy(out=dst, in_=ps), writes=['X%d.%d' % (ci, g), pk])
                        else:
                            P.op('act', lambda e, ps=ps, tok0=tok0, W=W: e.activation(out=SG[:, tok0 - 256:tok0 - 256 + W], in_=ps, func=AF.Silu),
                                 writes=['SG.%d' % g, pk])
                    inproj(ph, [hl, 4 + hl, 8 + hl, 16 + hl], evac_head, skip=lambda ci, g: (ci == 3 and g == 0))
                P.barrier()
                with ExitStack() as ph:
                    S_ = P.sb([128, L1_TOK], F32, stack=ph)
                    qT = P.sb([128, L1_TOK], BF16, stack=ph); kT = P.sb([128, L1_TOK], BF16, stack=ph); vT = P.sb([128, L1_TOK], BF16, stack=ph)
                    ktok = P.sb([128, NCHK, 128], BF16, stack=ph); vtok = P.sb([128, NCHK, 128], BF16, stack=ph)
                    OT = P.sb([128, 4096], F32, stack=ph)
                    ybuf = P.sb([128, 4096], BF16, stack=ph)
                    wk = [P.sb([128, 512], F32, stack=ph) for _ in range(3)]
                    pf = [P.ps([128, 512], F32, stack=ph) for _ in range(6)]
                    ptb = [P.ps([128, 1024], BF16, stack=ph) for _ in range(2)]
                    Xkeys = lambda i: ['X%d.%d' % (i, g) for g in range(9)]
                    for i, dstT in enumerate((qT, kT, vT)):
                        P.op('dve', lambda e: e.memset(stat[:, 32:33], 0.0), reads=Xkeys(i) + ['cwq'], writes=['X', 'cw'])
                        conv4(XQ[i], S_, cwq, (i * 4 + hl) * 4)
                        P.op('act', lambda e: e.activation(out=S_[:], in_=S_[:], func=AF.Silu), reads=['Sw'], writes=['Sw'])
                        if i == 2:
                            P.op('dve', lambda e: e.tensor_copy(out=vT[:], in_=S_[:]), reads=['Sw'], writes=['vT'])
                            continue
                        c1 = 128.0 if i == 0 else 1.0
                        for bi, (lo, hi) in enumerate(blocks9):
                            n = hi - lo
                            w0, w1 = wk[0], wk[1]
                            pb = pf[bi % 2]; pk = 'pf%d' % (bi % 2)
                            P.op('act', lambda e, lo=lo, hi=hi, n=n, w0=w0: e.activation(out=w0[:, 0:n], in_=S_[:, lo:hi], func=AF.Square), reads=['Sw'], writes=['wk0'])
                            P.op('pe', lambda e, n=n, w0=w0, pb=pb: e.matmul(pb[:, 0:n], lhsT=ones[:], rhs=w0[:, 0:n], start=True, stop=True), reads=['ones', 'wk0'], writes=[pk])
                            P.op('act', lambda e, n=n, w1=w1, pb=pb, c1=c1: e.activation(out=w1[:, 0:n], in_=pb[:, 0:n], func=AF.Sqrt, bias=c1 * EPS, scale=c1), writes=['wk1', pk])
                            P.op('dve', lambda e, n=n, w1=w1: e.reciprocal(out=w1[:, 0:n], in_=w1[:, 0:n]), reads=['wk1'], writes=['wk1'])
                            P.op('dve', lambda e, lo=lo, hi=hi, n=n, w1=w1, dstT=dstT: e.tensor_tensor(out=dstT[:, lo:hi], in0=S_[:, lo:hi], in1=w1[:, 0:n], op=ALU.mult),
                                 reads=['Sw', 'wk1'], writes=['qT' if i == 0 else 'kT'])
                    tcount = 0
                    for srcT, sk, dtok, dk in ((kT, 'kT', ktok, 'ktok'), (vT, 'vT', vtok, 'vtok')):
                        for c4 in range(0, NCHK, 4):
                            n = min(4, NCHK - c4)
                            par = tcount % 2; tcount += 1
                            for i2 in range(n):
                                P.op('pe', lambda e, par=par, i2=i2, c4=c4, srcT=srcT: e.transpose(
                                    out=ptb[par][:, i2 * 128:(i2 + 1) * 128], in_=srcT[:, (c4 + i2) * 128:(c4 + i2 + 1) * 128], identity=identb[:]),
                                    reads=[sk, 'identb'], writes=['ptb%d' % par])
                            eng = 'act' if par == 0 else 'dve'
                            if eng == 'act':
                                P.op('act', lambda e, par=par, n=n, c4=c4, dtok=dtok: e.activation(
                                    out=dtok[:, c4:c4 + n, :].rearrange("p c k -> p (c k)"), in_=ptb[par][:, 0:n * 128], func=AF.Copy),
                                    reads=[dk], writes=[dk, 'ptb%d' % par])
                            else:
                                P.op('dve', lambda e, par=par, n=n, c4=c4, dtok=dtok: e.tensor_copy(
                                    out=dtok[:, c4:c4 + n, :].rearrange("p c k -> p (c k)"), in_=ptb[par][:, 0:n * 128]),
                                    reads=[dk], writes=[dk, 'ptb%d' % par])
                    P.op('pool', lambda e: e.memset(OT[:], 0.0), writes=['OT.%d' % c for c in range(2, NCHK)])
                    TF = [[P.sb([128, 10, 128], F32, stack=ph) for _ in range(2)] for _ in range(2)]
                    TB = [[P.sb([128, 7, 128], BF16, stack=ph) for _ in range(2)] for _ in range(2)]
                    SF = [P.sb([128, 128], F32, stack=ph) for _ in range(2)]
                    SB_ = [P.sb([128, 128], BF16, stack=ph) for _ in range(2)]
                    FN = ['R', 'E', 'DmT', 'EG', 'DmTs', 'MTn', 'Yd', 'Xd', 'Cn', 'Zn']
                    BN = ['AqkT', 'Yb', 'Kg', 'WnT', 'Up', 'QgT', 'Ke']
                    bankc = [0, 0, 0, 0]

                    def step(c, d, par):
                        col = d * 4 + hl
                        tf = {nm: TF[d][par][:, i3, :] for i3, nm in enumerate(FN)}
                        tb = {nm: TB[d][par][:, i3, :] for i3, nm in enumerate(BN)}
                        tf['S'] = SF[d][:, :]
                        tb['Sb'] = SB_[d][:, :]
                        K = lambda nm: (nm + str(d)) if nm in ('S', 'Sb') else (nm + str(d) + 'ab'[par])
                        TL = []
                        TS = []
                        cur = [TL]

                        def OP(eng, fn, reads=(), writes=()):
                            cur[0].append((eng, fn, tuple(reads), tuple(writes)))

                        def bank():
                            q = d + 2 * par
                            b = (q, 4 + q)[bankc[q] % 2] if par == 0 else q
                            bankc[q] += 1
                            return pf[b][:, 0:128], 'pf%d' % b
                        gcol = G[:, c, col:col + 1]; gccol = GC[:, c, col:col + 1]
                        ksl = kT[:, c * 128:(c + 1) * 128]; qsl = qT[:, c * 128:(c + 1) * 128]
                        latent = c >= 2
                        OP('act', lambda e: e.activation(out=tf['R'], in_=msk[:, M_MASKF + d, :], func=AF.Copy, scale=gcol),
                             reads=['msk', 'G'], writes=[K('R')])
                        pgc, kgc = bank()
                        OP('pe', lambda e: e.matmul(pgc, lhsT=ones[:], rhs=tf['R'], start=True, stop=True), reads=['ones', K('R')], writes=[kgc])
                        OP('dve', lambda e: e.scalar_tensor_tensor(out=tf['E'], in0=pgc, scalar=gccol, in1=msk[:, M_NEGF + d, :], op0=ALU.subtract, op1=ALU.add),
                             reads=['GC', 'msk'], writes=[K('E'), kgc])
                        if latent:
                            OP('act', lambda e: e.activation(out=tf['EG'], in_=pgc, func=AF.Exp), writes=[K('EG'), kgc])
                        OP('act', lambda e: e.activation(out=tf['DmT'], in_=tf['E'], func=AF.Exp), reads=[K('E')], writes=[K('DmT')])
                        OP('dve', lambda e: e.tensor_tensor(out=tf['DmTs'], in0=tf['DmT'], in1=msk[:, M_STRF + d, :], op=ALU.mult),
                             reads=[K('DmT'), 'msk'], writes=[K('DmTs')])
                        pkk, kkk = bank()
                        OP('pe', lambda e: e.matmul(pkk, lhsT=ksl, rhs=ksl, start=True, stop=True), reads=['kT'], writes=[kkk])
                        OP('dve', lambda e: e.scalar_tensor_tensor(out=tf['MTn'], in0=pkk, scalar=NBETA[:, c, col:col + 1], in1=tf['DmTs'], op0=ALU.mult, op1=ALU.mult),
                             reads=['NBETA', K('DmTs')], writes=[K('MTn'), kkk])
                        if latent:
                            pqk, kqk = bank()
                            OP('pe', lambda e: e.matmul(pqk, lhsT=ksl, rhs=qsl, start=True, stop=True), reads=['kT', 'qT'], writes=[kqk])
                            OP('dve', lambda e: e.tensor_tensor(out=tb['AqkT'], in0=pqk, in1=tf['DmT'], op=ALU.mult), reads=[K('DmT')], writes=[K('AqkT'), kqk])
                        OP('dve', lambda e: e.tensor_tensor(out=tf['Yd'], in0=tf['MTn'], in1=msk[:, M_BD2, :], op=ALU.mult), reads=[K('MTn'), 'msk'], writes=[K('Yd')])
                        OP('dve', lambda e: e.tensor_tensor(out=tf['Yd'], in0=tf['Yd'], in1=ident, op=ALU.add), reads=[K('Yd'), 'msk'], writes=[K('Yd')])
                        ptx, ktx = bank()
                        OP('pe', lambda e: e.transpose(out=ptx, in_=tf['Yd'], identity=ident), reads=[K('Yd'), 'msk'], writes=[ktx])
                        OP('act', lambda e: e.activation(out=tf['Xd'], in_=ptx, func=AF.Copy), writes=[K('Xd'), ktx])
                        for li in range(6):
                            last = li == 5
                            OP('dve', lambda e, li=li: e.tensor_tensor(out=tf['Cn'], in0=tf['MTn'], in1=msk[:, M_LM0 + li, :], op=ALU.mult),
                                 reads=[K('MTn'), 'msk'], writes=[K('Cn')])
                            pz, kz = bank()
                            OP('pe', lambda e, pz=pz: e.matmul(pz, lhsT=tf['Cn'], rhs=tf['Xd'], start=True, stop=True), reads=[K('Cn'), K('Xd')], writes=[kz])
                            OP('act', lambda e, pz=pz: e.activation(out=tf['Zn'], in_=pz, func=AF.Copy), writes=[K('Zn'), kz])
                            if not last:
                                px, kx = bank()
                                OP('pe', lambda e, px=px: e.matmul(px, lhsT=tf['Yd'], rhs=tf['Zn'], start=True, stop=True), reads=[K('Yd'), K('Zn')], writes=[kx])
                                if par == 1:
                                    OP('dve', lambda e, px=px: e.tensor_tensor(out=tf['Xd'], in0=px, in1=tf['Xd'], op=ALU.add), reads=[K('Xd')], writes=[K('Xd'), kx])
                            py, ky = bank()
                            OP('pe', lambda e, py=py: e.matmul(py, lhsT=tf['Zn'], rhs=tf['Yd'], start=True, stop=True), reads=[K('Yd'), K('Zn')], writes=[ky])
                            if not last and par == 0:
                                OP('dve', lambda e, px=px: e.tensor_tensor(out=tf['Xd'], in0=px, in1=tf['Xd'], op=ALU.add), reads=[K('Xd')], writes=[K('Xd'), kx])
                            OP('dve', lambda e, py=py: e.tensor_tensor(out=tf['Yd'], in0=py, in1=tf['Yd'], op=ALU.add), reads=[K('Yd')], writes=[K('Yd'), ky])
                        OP('act', lambda e: e.activation(out=tb['Yb'], in_=tf['Yd'], func=AF.Copy), reads=[K('Yd')], writes=[K('Yb')])
                        OP('pool', lambda e: e.tensor_scalar(out=tb['Kg'], in0=ktok[:, c, :], scalar1=EGC[:, c, col:col + 1], scalar2=None, op0=ALU.mult),
                             reads=['ktok', 'EGC'], writes=[K('Kg')])
                        pw, kw = bank()
                        OP('pe', lambda e: e.matmul(pw, lhsT=tb['Kg'], rhs=tb['Yb'], start=True, stop=True), reads=[K('Kg'), K('Yb')], writes=[kw])
                        OP('act', lambda e: e.mul(out=tb['WnT'], in_=pw, mul=-1.0), writes=[K('WnT'), kw])
                        if latent:
                            OP('dve', lambda e: e.tensor_tensor(out=tb['QgT'], in0=qsl, in1=tf['EG'], op=ALU.mult), reads=['qT', K('EG')], writes=[K('QgT')])
                        OP('pool', lambda e: e.tensor_scalar(out=tb['Ke'], in0=ktok[:, c, :], scalar1=DEND[:, c, col:col + 1], scalar2=None, op0=ALU.mult),
                             reads=['ktok', 'DEND'], writes=[K('Ke')])
                        cur[0] = TS
                        pu, ku = bank()
                        OP('pe', lambda e: e.matmul(pu, lhsT=tb['Yb'], rhs=vtok[:, c, :], start=True, stop=False), reads=[K('Yb'), 'vtok'], writes=[ku])
                        OP('pe', lambda e: e.matmul(pu, lhsT=tb['WnT'], rhs=tb['Sb'], start=False, stop=True), reads=[K('WnT'), K('Sb')], writes=[ku])
                        OP('dve', lambda e: e.tensor_scalar(out=tb['Up'], in0=pu, scalar1=BETA[:, c, col:col + 1], scalar2=None, op0=ALU.mult),
                             reads=['BETA'], writes=[K('Up'), ku])
                        if latent:
                            po, ko = bank()
                            OP('pe', lambda e: e.matmul(po, lhsT=tb['Sb'], rhs=tb['QgT'], start=True, stop=False), reads=[K('Sb'), K('QgT')], writes=[ko])
                            OP('pe', lambda e: e.matmul(po, lhsT=tb['Up'], rhs=tb['AqkT'], start=False, stop=True), reads=[K('Up'), K('AqkT')], writes=[ko])
                            osl = OT[:, (c - 2) * 128:(c - 1) * 128]
                            OP('dve', lambda e: e.tensor_tensor(out=osl, in0=po, in1=osl, op=ALU.add), reads=['OT.%d' % c], writes=['OT.%d' % c, ko])
                        psn, ksn = bank()
                        OP('pe', lambda e: e.matmul(psn, lhsT=tb['Ke'], rhs=tb['Up'], start=True, stop=True), reads=[K('Ke'), K('Up')], writes=[ksn])
                        OP('dve', lambda e: e.scalar_tensor_tensor(out=tf['S'], in0=tf['S'], scalar=GE[:, c, col:col + 1], in1=psn, op0=ALU.mult, op1=ALU.add),
                             reads=[K('S'), 'GE'], writes=[K('S'), ksn])
                        OP('act', lambda e: e.activation(out=tb['Sb'], in_=tf['S'], func=AF.Copy), reads=[K('S')], writes=[K('Sb')])
                        return TL, TS

                    for d in range(2):
                        P.op('pool', lambda e, d=d: e.memset(SF[d][:], 0.0), writes=['S%d' % d])
                        P.op('pool', lambda e, d=d: e.memset(SB_[d][:], 0.0), writes=['Sb%d' % d])

                    def interleave(lists):
                        n = max(len(l) for l in lists)
                        for k in range(n):
                            for l in lists:
                                if k < len(l):
                                    P.op(*l[k])
                    order_f = list(range(NCHK))
                    order_b = [1, 0] + list(range(NCHK - 1, 1, -1))
                    for i in range(0, NCHK, 2):
                        st4 = [step(order_f[i], 0, 0), step(order_b[i], 1, 0), step(order_f[i + 1], 0, 1), step(order_b[i + 1], 1, 1)]
                        interleave([x[0] for x in st4])
                        interleave([st4[0][1], st4[1][1]])
                        interleave([st4[2][1], st4[3][1]])
                    for bi in range(8):
                        lo, hi = bi * 512, (bi + 1) * 512
                        okeys = ['OT.%d' % c for c in range(2 + bi * 4, 6 + bi * 4)]
                        pb = pf[bi % 2]; pk = 'pf%d' % (bi % 2)
                        P.op('act', lambda e, lo=lo, hi=hi: e.activation(out=wk[0][:], in_=OT[:, lo:hi], func=AF.Square), reads=okeys, writes=['wk0'])
                        P.op('pe', lambda e, pb=pb: e.matmul(pb[:], lhsT=ones[:], rhs=wk[0][:], start=True, stop=True), reads=['ones', 'wk0'], writes=[pk])
                        P.op('act', lambda e, pb=pb: e.activation(out=wk[1][:], in_=pb[:], func=AF.Sqrt, bias=EPS, scale=1.0 / 128), writes=['wk1', pk])
                        P.op('dve', lambda e: e.reciprocal(out=wk[1][:], in_=wk[1][:]), reads=['wk1'], writes=['wk1'])
                        P.op('dve', lambda e, lo=lo, hi=hi: e.tensor_tensor(out=wk[2][:], in0=OT[:, lo:hi], in1=wk[1][:], op=ALU.mult), reads=okeys + ['wk1'], writes=['wk2'])
                        P.op('dve', lambda e, lo=lo, hi=hi: e.scalar_tensor_tensor(out=ybuf[:, lo:hi], in0=wk[2][:], scalar=dnw[:, 0:1], in1=SG[:, lo:hi], op0=ALU.mult, op1=ALU.mult),
                             reads=['wk2', 'dnw'] + ['SG.%d' % (bi + 1)], writes=['ybuf'])
                    P.dma('sp', yT[hl * 128:(hl + 1) * 128, :], ybuf[:], reads=['ybuf'], writes=['yTout.%d' % hl], semkey='st_y%d' % hl)
                P.barrier()

        lcw = P.sb([128, 16], F32); P.dma('sp', lcw[:], lcw_d, writes=['lcw'])
        lcb = P.sb([128, 4], F32); P.dma('sp', lcb[:], lcb_d, writes=['lcb'])
        lwr = P.sb([128, 1024], F32); P.dma('sp', lwr[:], lwr_d, writes=['lwr'])
        lwi = P.sb([128, 1024], F32); P.dma('sp', lwi[:], lwi_d, writes=['lwi'])
        lbr = P.sb([128, 8], F32); P.dma('sp', lbr[:], lbr_d, writes=['lbr'])
        lbi = P.sb([128, 8], F32); P.dma('sp', lbi[:], lbi_d, writes=['lbi'])
        lam = P.sb([128, 8], F32); P.dma('sp', lam[:], llam_d, writes=['lam'])
        n8 = P.sb([128, 8], F32); n16 = P.sb([128, 8], F32)
        P.op('act', lambda e: e.activation(out=lam[:], in_=lam[:], func=AF.Exp, scale=-1.0), reads=['lam'], writes=['lam'])
        P.op('act', lambda e: e.activation(out=lam[:], in_=lam[:], func=AF.Ln, bias=1.0, scale=1.0), reads=['lam'], writes=['lam'])
        P.op('act', lambda e: e.mul(out=n8[:], in_=lam[:], mul=-8.0), reads=['lam'], writes=['n8'])
        P.op('act', lambda e: e.mul(out=n16[:], in_=lam[:], mul=-16.0), reads=['lam'], writes=['n16'])

        for lp in range(2):
            with ExitStack() as hs:
                XL = [P.sb([128, XW], F32, stack=hs) for _ in range(2)]
                SGL = [P.sb([128, 4096], BF16, stack=hs) for _ in range(2)]
                for i in range(2):
                    P.op('pool', lambda e, i=i: e.memset(XL[i][:], 0.0), writes=['XL%d.%d' % (i, g) for g in range(9)])
                with ExitStack() as ph:
                    def evac_lru(ci, g, tok0, W, ps, pk):
                        if ci < 2:
                            if g == 0:
                                dst = XL[ci][:, 2:258]
                                src = ps
                            else:
                                r0 = 8 * (g - 1)
                                dst = XL[ci][:, 261:261 + 4096].rearrange("p (c r) -> p r c", r=64)[:, r0:r0 + 8, :]
                                src = ps.rearrange("p (r c) -> p r c", c=64)
                            if (ci + g) % 2 == 0:
                                P.op('act', lambda e, dst=dst, src=src: e.activation(out=dst, in_=src, func=AF.Copy), writes=['XL%d.%d' % (ci, g), pk])
                            else:
                                P.op('dve', lambda e, dst=dst, src=src: e.tensor_copy(out=dst, in_=src), writes=['XL%d.%d' % (ci, g), pk])
                        else:
                            P.op('act', lambda e, ps=ps, tok0=tok0, W=W, ci=ci: e.activation(out=SGL[ci - 2][:, tok0 - 256:tok0 - 256 + W], in_=ps, func=AF.Silu),
                                 writes=['SGL%d.%d' % (ci - 2, g), pk])
                    inproj(ph, [12 + 2 * lp, 13 + 2 * lp, 20 + 2 * lp, 21 + 2 * lp], evac_lru, skip=lambda ci, g: (ci >= 2 and g == 0))
                P.barrier()
                with ExitStack() as ph:
                    XC = P.sb([128, L1_TOK], F32, stack=ph)
                    A = P.sb([128, L1_TOK], F32, stack=ph); Bv = P.sb([128, L1_TOK], F32, stack=ph)
                    HF = P.sb([128, L1_TOK], F32, stack=ph); HB = P.sb([128, L1_TOK], F32, stack=ph)
                    ybl = P.sb([128, 4096], BF16, stack=ph)
                    wk = [P.sb([128, 512], F32, stack=ph) for _ in range(3)]
                    pf = [P.ps([128, 512], F32, stack=ph) for _ in range(4)]
                    for cl in range(2):
                        c = 2 * lp + cl
                        P.op('dve', lambda e: e.memset(stat[:, 32:33], 0.0), reads=['XL%d.%d' % (cl, g) for g in range(9)] + ['lcw', 'lcb'], writes=['X', 'cw'])
                        conv4(XL[cl], XC, lcw, c * 4, bias=lcb[:, c:c + 1])
                        for d in range(2):
                            pc = d * 4 + c
                            for bi, (lo, hi) in enumerate(blocks9):
                                n = hi - lo
                                pr = pf[(bi % 2) * 2]; pi_ = pf[(bi % 2) * 2 + 1]
                                kr = 'pf%d' % ((bi % 2) * 2); ki = 'pf%d' % ((bi % 2) * 2 + 1)
                                P.op('pe', lambda e, pr=pr, n=n, lo=lo, hi=hi, pc=pc: e.matmul(pr[:, 0:n], lhsT=lwr[:, pc * 128:(pc + 1) * 128], rhs=XC[:, lo:hi], start=True, stop=True),
                                     reads=['lwr', 'Sw'], writes=[kr])
                                P.op('pe', lambda e, pi_=pi_, n=n, lo=lo, hi=hi, pc=pc: e.matmul(pi_[:, 0:n], lhsT=lwi[:, pc * 128:(pc + 1) * 128], rhs=XC[:, lo:hi], start=True, stop=True),
                                     reads=['lwi', 'Sw'], writes=[ki])
                                P.op('act', lambda e, pr=pr, n=n, pc=pc: e.activation(out=wk[0][:, 0:n], in_=pr[:, 0:n], func=AF.Sigmoid, bias=lbr[:, pc:pc + 1], scale=1.0),
                                     reads=['lbr'], writes=['wk0', kr])
                                P.op('act', lambda e, pi_=pi_, n=n, pc=pc: e.activation(out=wk[1][:, 0:n], in_=pi_[:, 0:n], func=AF.Sigmoid, bias=lbi[:, pc:pc + 1], scale=1.0),
                                     reads=['lbi'], writes=['wk1', ki])
                                P.op('act', lambda e, n=n, lo=lo, hi=hi, pc=pc: e.activation(out=A[:, lo:hi], in_=wk[0][:, 0:n], func=AF.Exp, scale=n8[:, pc:pc + 1]),
                                     reads=['wk0', 'n8'], writes=['A'])
                                P.op('act', lambda e, n=n, pc=pc: e.activation(out=wk[2][:, 0:n], in_=wk[0][:, 0:n], func=AF.Exp, scale=n16[:, pc:pc + 1]),
                                     reads=['wk0', 'n16'], writes=['wk2'])
                                P.op('act', lambda e, n=n: e.activation(out=wk[2][:, 0:n], in_=wk[2][:, 0:n], func=AF.Sqrt, bias=1.0, scale=-1.0),
                                     reads=['wk2'], writes=['wk2'])
                                P.op('dve', lambda e, n=n, lo=lo, hi=hi: e.tensor_tensor(out=wk[1][:, 0:n], in0=wk[1][:, 0:n], in1=XC[:, lo:hi], op=ALU.mult),
                                     reads=['wk1', 'Sw'], writes=['wk1'])
                                P.op('dve', lambda e, n=n, lo=lo, hi=hi: e.tensor_tensor(out=Bv[:, lo:hi], in0=wk[1][:, 0:n], in1=wk[2][:, 0:n], op=ALU.mult),
                                     reads=['wk1', 'wk2'], writes=['Bv'])
                            if d == 0:
                                P.op('dve', lambda e: e.tensor_tensor_scan(out=HF[:, 0:256], data0=A[:, 0:256], data1=Bv[:, 0:256], initial=0.0, op0=ALU.mult, op1=ALU.add),
                                     reads=['A', 'Bv'], writes=['HF'])
                                P.op('dve', lambda e: e.tensor_tensor_scan(out=HF[:, 256:L1_TOK], data0=A[:, 256:L1_TOK], data1=Bv[:, 256:L1_TOK], initial=HF[:, 255:256], op0=ALU.mult, op1=ALU.add),
                                     reads=['A', 'Bv', 'HF'], writes=['HF'])
                            else:
                                P.op('dve', lambda e: e.tensor_tensor_scan(out=HB[:, 255::-1], data0=A[:, 255::-1], data1=Bv[:, 255::-1], initial=0.0, op0=ALU.mult, op1=ALU.add),
                                     reads=['A', 'Bv'], writes=['HB'])
                                P.op('dve', lambda e: e.tensor_tensor_scan(out=HB[:, L1_TOK - 1:255:-1], data0=A[:, L1_TOK - 1:255:-1], data1=Bv[:, L1_TOK - 1:255:-1], initial=HB[:, 0:1], op0=ALU.mult, op1=ALU.add),
                                     reads=['A', 'Bv', 'HB'], writes=['HB'])
                        P.op('dve', lambda e: e.tensor_tensor(out=HF[:, 256:L1_TOK], in0=HF[:, 256:L1_TOK], in1=HB[:, 256:L1_TOK], op=ALU.add), reads=['HF', 'HB'], writes=['HF'])
                        P.op('dve', lambda e, cl=cl: e.tensor_tensor(
                            out=ybl[:].rearrange("p (r c) -> p r c", c=64), in0=HF[:, 256:L1_TOK].rearrange("p (c r) -> p r c", r=64),
                            in1=SGL[cl][:].rearrange("p (r c) -> p r c", c=64), op=ALU.mult),
                            reads=['HF'] + ['SGL%d.%d' % (cl, g) for g in range(1, 9)], writes=['ybl'])
                        P.dma('sp', yT[512 + c * 128:512 + (c + 1) * 128, :], ybl[:], reads=['ybl'], writes=['yTout.%d' % (4 + c)], semkey='st_yl%d' % c)
                P.barrier()
    P.stack = P.semstack


def l1_masks():
    i = np.arange(128)
    s, t = i[:, None], i[None, :]
    m = np.zeros((14, 128, 128), np.float32)
    m[M_MASKF] = (s <= t); m[M_MASKB] = (s >= t)
    m[M_NEGF] = np.where(t >= s, 0.0, NEG); m[M_NEGB] = np.where(t <= s, 0.0, NEG)
    m[M_STRF] = (t > s); m[M_STRB] = (t < s)
    m[M_BD2] = (s // 2 == t // 2) & (s != t)
    for li, b in enumerate((2, 4, 8, 16, 32, 64)):
        m[M_LM0 + li] = (s // (2 * b) == t // (2 * b)) & (s // b != t // b)
    m[M_ID] = (s == t)
    return np.ascontiguousarray(m.transpose(1, 0, 2)).reshape(128, 14 * 128)


def l1_in_maps(inputs, modv):
    w_in = inputs['ab_w_in'][0]
    qc = inputs['ab_qkv_conv'][0]
    msk = l1_masks()
    in_maps = []
    for r in range(NCORES):
        b, j = r // 4, r % 4
        heads = [4 * j + h for h in range(4)]
        cols = []
        for base in (0, 2048, 4096):
            for h in heads:
                cols.append(np.arange(base + h * 128, base + (h + 1) * 128))
        for c in range(4):
            cols.append(np.arange(6144 + (4 * j + c) * 128, 6144 + (4 * j + c + 1) * 128))
        for h in heads:
            cols.append(np.arange(8256 + h * 128, 8256 + (h + 1) * 128))
        for c in range(4):
            cols.append(np.arange(10304 + (4 * j + c) * 128, 10304 + (4 * j + c + 1) * 128))
        cols = np.concatenate(cols)
        wc = w_in[:, cols].reshape(32, 128, 24, 128)
        wmain = np.ascontiguousarray(wc.transpose(2, 1, 0, 3)).reshape(24 * 128, 32 * 128)
        abcols = [8192 + d * 16 + h for d in range(2) for h in heads] + [8224 + d * 16 + h for d in range(2) for h in heads]
        wab = np.ascontiguousarray(w_in[:, abcols].reshape(32, 128, 16).transpose(1, 0, 2)).reshape(128, 512)
        if modv is not None:
            sh0, sc0 = modv[b, 0:D], modv[b, D:2 * D]
            shc, scc = modv[2, 0:D], modv[2, D:2 * D]
            mod0 = np.concatenate([pcol(sh0), pcol(sc0), pcol(shc), pcol(scc), pcol(inputs['norm_w'][0])], axis=1)
        else:
            mod0 = np.zeros((128, 160), np.float32)
        cwq = np.zeros((128, 12, 4), np.float32)
        for which, base in enumerate((0, 2048, 4096)):
            for hl, h in enumerate(heads):
                cwq[:, which * 4 + hl, :] = qc[:, base + h * 128: base + (h + 1) * 128].T
        al = np.array([inputs['ab_a_log'][0][d, h] for d in range(2) for h in heads], np.float32)
        dt = np.array([inputs['ab_dt_bias'][0][d, h] for d in range(2) for h in heads], np.float32)
        alog = np.ascontiguousarray(np.broadcast_to(al[None, None, :], (128, NCHK, 8))).reshape(128, 272)
        dtb = np.ascontiguousarray(np.broadcast_to(dt[None, None, :], (128, NCHK, 8))).reshape(128, 272)
        lcw = np.zeros((128, 4, 4), np.float32); lcb = np.zeros((128, 4), np.float32)
        lwr = np.zeros((128, 2, 4, 128), np.float32); lwi = np.zeros((128, 2, 4, 128), np.float32)
        lbr = np.zeros((128, 2, 4), np.float32); lbi = np.zeros((128, 2, 4), np.float32); llam = np.zeros((128, 2, 4), np.float32)
        for c in range(4):
            n = 4 * j + c
            sl = slice(n * 128, (n + 1) * 128)
            lcw[:, c, :] = inputs['ab_lru_conv_w'][0][:, sl].T
            lcb[:, c] = inputs['ab_lru_conv_b'][0][sl]
            for d in range(2):
                lwr[:, d, c, :] = inputs['ab_lru_w_r'][0][d, n]
                lwi[:, d, c, :] = inputs['ab_lru_w_i'][0][d, n]
                lbr[:, d, c] = inputs['ab_lru_b_r'][0][d, sl]
                lbi[:, d, c] = inputs['ab_lru_b_i'][0][d, sl]
                llam[:, d, c] = inputs['ab_lru_lambda'][0][d, sl]
        in_maps.append({
            "xa": np.ascontiguousarray(np.concatenate([inputs['ctx'][b], inputs['x'][b]], axis=0)),
            "wmain": wmain, "wab": wab, "mod0": np.ascontiguousarray(mod0), "cwq": cwq.reshape(128, 48),
            "alog": alog, "dtb": dtb, "dnw": np.ascontiguousarray(inputs['ab_dn_norm'][0].reshape(128, 1)),
            "lcw": lcw.reshape(128, 16), "lcb": lcb, "lwr": lwr.reshape(128, 1024), "lwi": lwi.reshape(128, 1024),
            "lbr": lbr.reshape(128, 8), "lbi": lbi.reshape(128, 8), "llam": llam.reshape(128, 8), "msk": msk})
    return in_maps


def run_l1(inputs, modv):
    in_maps = l1_in_maps(inputs, modv)
    nc = build_l1()
    res = _launch(nc, in_maps)
    yT_full = np.zeros((2, 4096, 4096), ml_dtypes.bfloat16)
    for r in range(NCORES):
        b, j = r // 4, r % 4
        y = res[r]["yT"]
        yT_full[b, j * 512:(j + 1) * 512] = y[0:512]
        yT_full[b, 2048 + j * 512:2048 + (j + 1) * 512] = y[512:1024]
    return yT_full


I32 = mybir.dt.int32


def build_fused(stop=None):
    nc = bass.Bass("TRN2", target_bir_lowering=False, num_devices=NCORES)
    def din(name, shape, dt=F32):
        return nc.dram_tensor(name, list(shape), dt, kind="ExternalInput").ap()
    def dint(name, shape, dt=F32):
        return nc.dram_tensor(name, list(shape), dt, kind="Internal").ap()
    csT = din("csT", [128, 96]); w0 = din("w0", [D, L0_COLS]); bias0 = din("bias0", [3, L0_COLS])
    idxm_d = din("idxm", [8, 2], I32); selm_d = din("selm", [8, 8 * 128]); nw_d = din("nw", [128, 64])
    io1 = {}; io2 = {}; idxy_d = None
    if stop is None:
        io1 = {name: din(name, shape) for name, shape in L1_INPUTS if name != 'mod0'}
        io2 = {'x': din("x", [L2_TOK, D]), 'wo': din("wo", [8 * 128, 32 * 512]), 'wi': din("wi", [32 * 128, 32 * 4 * 128]),
               'wo1': din("wo1", [8 * 128, 32 * 512]), 'cw': din("cw", [128, 96]), 'fnb': din("fnb", [128, D])}
        idxy_d = din("idxy", [128, 128], I32)
        io2['out'] = nc.dram_tensor("out", [L2_TOK, D], F32, kind="ExternalOutput").ap()
    msrc = dint("msrc", [3, L0_COLS]); MG = dint("MG", [24, L0_COLS]); MG2 = dint("MG2", [24, L0_COLS])
    mod0_s = dint("mod0_s", [128, 160]); m1_s = dint("m1_s", [128, 96])
    g0b_s = dint("g0b_s", [128, D]); g1b_s = dint("g1b_s", [128, D])
    io1['mod0'] = mod0_s
    io1['hnT_d'] = dint("hnT_d", [128, 9 * 32 * 512], BF16)
    ysrc = dint("ysrc", [1024, 4096], BF16)
    io1['yT'] = ysrc
    YG = dint("YG", [4096, 4096], BF16); YG2 = dint("YG2", [4096, 4096], BF16)
    io2['m1'] = m1_s; io2['g0b'] = g0b_s; io2['g1b'] = g1b_s
    with ExitStack() as st:
        P = Prog(nc, st)
        emit_l0(P, csT, w0, bias0, msrc, okey='st_msrc')
        P.custom('pool', lambda e: e.collective_compute("AllGather", ALU.bypass, replica_groups=[list(range(NCORES))],
                                                        ins=[msrc.opt()], outs=[MG.opt()]),
                 reads=['l0out'], writes=['mg'], semkey='cc_mg', inc=1)
        P.dma('sp', MG2, MG, reads=['mg'], writes=['mg2'])
        P.barrier()
        if stop == 'ag':
            dbg = nc.dram_tensor("dbg", [24, L0_COLS], F32, kind="ExternalOutput").ap()
            P.dma('sp', dbg, MG2, semkey='st_dbg0')
            P.finish()
            return nc
        with ExitStack() as ph:
            P.stack = ph
            idf, idfk = make_identity(P, F32)
            idx = P.sb([8, 2], I32); P.dma('sp', idx[:], idxm_d, writes=['idxm'])
            sel = P.sb([8, 8, 128], F32); P.dma('sp', sel[:].rearrange("k r m -> k (r m)"), selm_d, writes=['sel'])
            nwt = P.sb([128, 64], F32); P.dma('sp', nwt[:], nw_d, writes=['nwt'])
            Tb = P.sb([8, L0_COLS], F32); Tc = P.sb([8, L0_COLS], F32)
            P.custom('pool', lambda e: e.indirect_dma_start(out=Tb[:, :], out_offset=None, in_=MG2,
                                                            in_offset=bass.IndirectOffsetOnAxis(ap=idx[:, 0:1], axis=0)),
                     reads=['mg2', 'idxm'], writes=['Tb'], semkey='d_Tb', inc=16)
            P.custom('pool', lambda e: e.indirect_dma_start(out=Tc[:, :], out_offset=None, in_=MG2,
                                                            in_offset=bass.IndirectOffsetOnAxis(ap=idx[:, 1:2], axis=0)),
                     reads=['mg2', 'idxm'], writes=['Tc'], semkey='d_Tc', inc=16)
            if stop == 'gather':
                dbg = nc.dram_tensor("dbg", [16, L0_COLS], F32, kind="ExternalOutput").ap()
                P.dma('sp', dbg[0:8, :], Tb[:], reads=['Tb'], semkey='st_dbg0')
                P.dma('sp', dbg[8:16, :], Tc[:], reads=['Tc'], semkey='st_dbg1')
                P.finish()
                P.stack = P.semstack
                return nc
            VB = P.sb([128, 192], F32); VC = P.sb([128, 192], F32)
            pv = [P.ps([128, 512], F32) for _ in range(4)]
            for vi, (T, tk, V, vk) in enumerate(((Tb, 'Tb', VB, 'VB'), (Tc, 'Tc', VC, 'VC'))):
                for cb in range(24):
                    P.op('pe', lambda e, vi=vi, cb=cb, T=T: e.transpose(out=pv[vi][:, cb * 8:(cb + 1) * 8], in_=T[0:8, cb * 128:(cb + 1) * 128],
                                                                         identity=idf[0:8, 0:8]), reads=[tk, idfk], writes=['pv%d' % vi])
                P.op('dve', lambda e, vi=vi, V=V: e.tensor_copy(out=V[:].rearrange("p (r cb) -> p cb r", cb=24),
                                                                in_=pv[vi][:, 0:192].rearrange("p (cb r) -> p cb r", r=8)),
                     writes=[vk, 'pv%d' % vi])
            mod0t = P.sb([128, 160], F32); m1t = P.sb([128, 96], F32)
            P.op('dve', lambda e: e.tensor_copy(out=mod0t[:, 0:64], in_=VB[:, 0:64]), reads=['VB'], writes=['mod0t'])
            P.op('dve', lambda e: e.tensor_copy(out=mod0t[:, 64:128], in_=VC[:, 0:64]), reads=['VC', 'mod0t'], writes=['mod0t'])
            P.op('dve', lambda e: e.tensor_copy(out=mod0t[:, 128:160], in_=nwt[:, 0:32]), reads=['nwt', 'mod0t'], writes=['mod0t'])
            P.op('dve', lambda e: e.tensor_copy(out=m1t[:, 0:64], in_=VB[:, 96:160]), reads=['VB'], writes=['m1t'])
            P.op('dve', lambda e: e.tensor_copy(out=m1t[:, 64:96], in_=nwt[:, 32:64]), reads=['nwt', 'm1t'], writes=['m1t'])
            P.dma('sp', mod0_s, mod0t[:], reads=['mod0t'], writes=['mod0s'])
            P.dma('sp', m1_s, m1t[:], reads=['m1t'], writes=['m1s'])
            gbt = [P.sb([128, D], F32) for _ in range(2)]
            cnt = 0
            for gi, r0 in enumerate((2, 6)):
                for (rr, c0, n, dst) in ((r0, 2048, 1024, 0), (r0 + 1, 0, 3072, 1024)):
                    for off in range(0, n, 512):
                        pi = 2 + cnt % 2
                        cnt += 1
                        P.op('pe', lambda e, pi=pi, rr=rr, c0=c0, off=off: e.matmul(pv[pi][:, 0:512], lhsT=sel[:, rr, :], rhs=Tb[0:8, c0 + off:c0 + off + 512],
                                                                                 start=True, stop=True), reads=['sel', 'Tb'], writes=['pv%d' % pi])
                        P.op('act', lambda e, pi=pi, gi=gi, dst=dst, off=off: e.activation(out=gbt[gi][:, dst + off:dst + off + 512], in_=pv[pi][:, 0:512], func=AF.Copy),
                             reads=['gbt%d' % gi], writes=['gbt%d' % gi, 'pv%d' % pi])
            P.dma('sp', g0b_s, gbt[0][:], reads=['gbt0'], writes=['g0bs'])
            P.dma('act', g1b_s, gbt[1][:], reads=['gbt1'], writes=['g1bs'])
            P.barrier()
        P.stack = P.semstack
        if stop == 'mod':
            dbg = nc.dram_tensor("dbg", [128, 160 + 96 + 2 * D], F32, kind="ExternalOutput").ap()
            P.dma('sp', dbg[:, 0:160], mod0_s, semkey='st_dbg0')
            P.dma('sp', dbg[:, 160:256], m1_s, semkey='st_dbg1')
            P.dma('sp', dbg[:, 256:256 + D], g0b_s, semkey='st_dbg2')
            P.dma('sp', dbg[:, 256 + D:256 + 2 * D], g1b_s, semkey='st_dbg3')
            P.finish()
            return nc
        emit_l1(P, io1)
        P.barrier()
        P.custom('pool', lambda e: e.collective_compute("AllGather", ALU.bypass, replica_groups=[[0, 1, 2, 3], [4, 5, 6, 7]],
                                                        ins=[ysrc.opt()], outs=[YG.opt()]),
                 reads=['yTout.%d' % i for i in range(8)], writes=['yg'], semkey='cc_yg', inc=1)
        for i in range(8):
            P.dma('sp' if i % 2 == 0 else 'act', YG2[i * 512:(i + 1) * 512, :], YG[i * 512:(i + 1) * 512, :], reads=['yg'], writes=['yg2.%d' % i])
        P.barrier()
        YV = YG2.rearrange("r (b t) -> (r b) t", t=L2_T)
        emit_l2(P, io2, ysrc=(YV, idxy_d))
        P.finish()
        print("fused instructions:", P.ninst, "semaphores:", len(P.sems))
    return nc


def run_fused(inputs):
    c, c_ctx, mod_w, mod_b = inputs['c'], inputs['c_ctx'], inputs['mod_w'], inputs['mod_b']
    cs = np.concatenate([c, c_ctx[None, :]], axis=0)
    csT = np.ascontiguousarray(cs.reshape(3, 32, 128).transpose(2, 1, 0)).reshape(128, 96)
    wall = np.concatenate([mod_w[0], mod_w[1]], axis=1)
    ball = np.concatenate([mod_b[0], mod_b[1]], axis=0)
    wo, wi, wo1, cw = l2_layouts(inputs)
    fnb = np.ascontiguousarray(np.broadcast_to(inputs['final_norm_w'][None, :], (128, D)))
    selm = np.zeros((8, 8, 128), np.float32)
    for r in range(8):
        selm[r, r, :] = 1.0
    nw = np.ascontiguousarray(np.concatenate([pcol(inputs['norm_w'][0]), pcol(inputs['norm_w'][1])], axis=1))
    l1maps = l1_in_maps(inputs, None)
    in_maps = []
    for r in range(NCORES):
        b, tq = r // 4, r % 4
        sl = slice(r * L0_COLS, (r + 1) * L0_COLS)
        m = dict(l1maps[r])
        m.pop('mod0')
        m.update({"csT": csT, "w0": np.ascontiguousarray(wall[:, sl]),
                  "bias0": np.ascontiguousarray(np.broadcast_to(ball[sl][None, :], (3, L0_COLS))),
                  "idxm": np.array([[r8 * 3 + b, r8 * 3 + 2] for r8 in range(8)], np.int32),
                  "selm": selm.reshape(8, 8 * 128), "nw": nw,
                  "x": np.ascontiguousarray(inputs['x'][b, tq * L2_TOK:(tq + 1) * L2_TOK]),
                  "wo": wo, "wi": wi, "wo1": wo1, "cw": cw, "fnb": fnb})
        idxy = np.zeros((128, 32, 4), np.int32)
        p = np.arange(128)
        for kc in range(32):
            if kc < 16:
                jj, local = kc // 4, (kc % 4) * 128 + p
            else:
                jj, local = (kc - 16) // 4, 512 + ((kc - 16) % 4) * 128 + p
            for ps_i in range(4):
                idxy[:, kc, ps_i] = (jj * 1024 + local) * 16 + tq * 4 + ps_i
        m["idxy"] = idxy.reshape(128, 128)
        in_maps.append(m)
    nc = build_fused()
    res = _launch(nc, in_maps)
    out = np.zeros((2, 4096, D), np.float32)
    for r in range(NCORES):
        b, tq = r // 4, r % 4
        out[b, tq * L2_TOK:(tq + 1) * L2_TOK] = res[r]["out"]
    return out


FUSED = False


def kernel(**inputs):
    inputs = {k: np.asarray(v) for k, v in inputs.items()}
    if FUSED:
        return run_fused(inputs)
    modv = run_l0(inputs['c'], inputs['c_ctx'], inputs['mod_w'], inputs['mod_b'])
    yT_full = run_l1(inputs, modv)
    return run_l2(inputs, modv, yT_full)
```

```python
import numpy as np
import ml_dtypes
from contextlib import ExitStack
import concourse.bass as bass
import concourse.mybir as mybir
from concourse.bass_utils import run_bass_kernel_spmd

F32 = mybir.dt.float32
BF16 = mybir.dt.bfloat16
ALU = mybir.AluOpType
AF = mybir.ActivationFunctionType
AX = mybir.AxisListType

ENGS = ['pe', 'dve', 'act', 'pool', 'sp']
NCORES = 8
D = 4096
EPS = 1e-6


class Prog:
    def __init__(self, nc, stack):
        self.nc = nc
        self.stack = stack
        self.semstack = stack
        self.rec = []
        self.lastw = {}
        self.readers = {}
        self.last_on = {}
        self.nbuf = 0
        self.ninst = 0
        self.sems = {}

    def sb(self, shape, dtype, name=None, stack=None):
        self.nbuf += 1
        name = name or ("t%d" % self.nbuf)
        return (stack or self.stack).enter_context(self.nc.sbuf_tensor(name, list(shape), dtype))

    def ps(self, shape, dtype=F32, name=None, stack=None):
        self.nbuf += 1
        name = name or ("p%d" % self.nbuf)
        return (stack or self.stack).enter_context(self.nc.psum_tensor(name, list(shape), dtype))

    def _deps(self, reads, writes):
        deps = []
        for k in reads:
            if k in self.lastw:
                deps.append(self.lastw[k])
        for k in writes:
            if k in self.lastw:
                deps.append(self.lastw[k])
            deps.extend(self.readers.get(k, ()))
        return deps

    def _commit(self, rid, reads, writes):
        for k in writes:
            self.lastw[k] = rid
            self.readers[k] = []
        for k in reads:
            if k in writes:
                continue
            self.readers.setdefault(k, []).append(rid)

    def _add(self, kind, eng, fn, reads, writes, semkey, inc):
        rid = len(self.rec)
        self.rec.append((kind, eng, fn, self._deps(reads, writes), semkey, inc))
        self.last_on[semkey] = rid
        self._commit(rid, reads, writes)
        return rid

    def op(self, eng, fn, reads=(), writes=()):
        return self._add('op', eng, fn, reads, writes, eng, 1)

    def dma(self, q, out, in_, reads=(), writes=(), semkey=None):
        if semkey is None:
            semkey = 'd_' + (writes[0] if writes else reads[0])
        return self._add('dma', q, lambda e, out=out, in_=in_: e.dma_start(out=out, in_=in_), reads, writes, semkey, 16)

    def custom(self, q, fn, reads, writes, semkey, inc):
        return self._add('dma', q, fn, reads, writes, semkey, inc)

    def barrier(self):
        self.rec.append(('barrier', None, None, dict(self.last_on), None, 0))

    def finish(self):
        nc = self.nc
        self.rec.append(('barrier', 'sp', None, dict(self.last_on), None, 0))
        rec = self.rec
        needed = set()
        for kind, eng, fn, deps, semkey, inc in rec:
            if kind == 'barrier':
                needed.update(deps.values())
            else:
                for d in deps:
                    if not (eng == 'pe' and rec[d][4] == 'pe'):
                        needed.add(d)
        cnt = {}
        value = {}
        seen = {e: {} for e in ENGS}
        ops = {e: [] for e in ENGS}

        def sem_of(sk):
            if sk not in self.sems:
                self.sems[sk] = self.semstack.enter_context(nc.semaphore("q%d" % len(self.sems)))
            return self.sems[sk]

        def emit_waits(eng, pairs):
            need = {}
            for sk, v in pairs:
                if seen[eng].get(sk, 0) >= v:
                    continue
                if need.get(sk, 0) < v:
                    need[sk] = v
            for sk, v in need.items():
                seen[eng][sk] = v
                sem = sem_of(sk)
                ops[eng].append(lambda e, sem=sem, v=v: e.wait_ge(sem, v))
                self.ninst += 1

        for rid, (kind, eng, fn, deps, semkey, inc) in enumerate(rec):
            if kind == 'barrier':
                pairs = [(sk, cnt[sk]) for sk in deps if cnt.get(sk, 0) > 0]
                for e in (ENGS if eng is None else [eng]):
                    emit_waits(e, pairs)
                continue
            pairs = []
            for d in deps:
                if eng == 'pe' and rec[d][4] == 'pe':
                    continue
                pairs.append(value[d])
            emit_waits(eng, pairs)
            if kind == 'dma' or rid in needed:
                cnt[semkey] = cnt.get(semkey, 0) + inc
                value[rid] = (semkey, cnt[semkey])
                sem = sem_of(semkey)
                ops[eng].append(lambda e, fn=fn, sem=sem, inc=inc: fn(e).then_inc(sem, inc))
            else:
                ops[eng].append(lambda e, fn=fn: fn(e))
            self.ninst += 1
        with nc.Block() as block:
            def mk(name):
                def run(e):
                    for f in ops[name]:
                        f(e)
                return run
            block.tensor(mk('pe'))
            block.vector(mk('dve'))
            block.scalar(mk('act'))
            block.gpsimd(mk('pool'))
            block.sync(mk('sp'))


def _launch(nc, in_maps):
    res = run_bass_kernel_spmd(nc, in_maps, core_ids=list(range(NCORES)))
    return res.results


L0_COLS = 2 * 3 * D // NCORES


def build_l0():
    nc = bass.Bass("TRN2", target_bir_lowering=False)
    csT = nc.dram_tensor("csT", [128, 96], F32, kind="ExternalInput").ap()
    w = nc.dram_tensor("w", [D, L0_COLS], F32, kind="ExternalInput").ap()
    bias = nc.dram_tensor("bias", [3, L0_COLS], F32, kind="ExternalInput").ap()
    o = nc.dram_tensor("o", [3, L0_COLS], F32, kind="ExternalOutput").ap()
    with ExitStack() as st:
        P = Prog(nc, st)
        emit_l0(P, csT, w, bias, o)
        P.finish()
    return nc


def emit_l0(P, csT, w, bias, o, okey='st_o'):
    NB = L0_COLS // 512
    with ExitStack() as st:
        P.stack = st
        cs = P.sb([128, 96], F32)
        sT = P.sb([128, 96], F32)
        bt = P.sb([3, L0_COLS], F32)
        res = P.sb([3, L0_COLS], F32)
        P.dma('sp', cs[:], csT, writes=['cs'])
        P.dma('sp', bt[:], bias, writes=['bt'])
        P.op('act', lambda e: e.activation(out=sT[:], in_=cs[:], func=AF.Silu), reads=['cs'], writes=['sT'])
        NS = 4
        wts = [P.sb([128, L0_COLS], F32) for _ in range(NS)]
        pss = [P.ps([128, 512], F32) for _ in range(NB)]
        for kc in range(32):
            s = kc % NS
            P.dma('sp', wts[s][:], w[kc * 128:(kc + 1) * 128, :], writes=['wt%d' % s])
            for nb in range(NB):
                P.op('pe', lambda e, s=s, nb=nb, kc=kc: e.matmul(
                    pss[nb][0:3, :], lhsT=sT[:, kc * 3:(kc + 1) * 3], rhs=wts[s][:, nb * 512:(nb + 1) * 512],
                    start=(kc == 0), stop=(kc == 31)), reads=['sT', 'wt%d' % s], writes=['ps%d' % nb])
        for nb in range(NB):
            P.op('dve', lambda e, nb=nb: e.tensor_tensor(out=res[:, nb * 512:(nb + 1) * 512], in0=pss[nb][0:3, :],
                                                        in1=bt[:, nb * 512:(nb + 1) * 512], op=ALU.add),
                 reads=['ps%d' % nb, 'bt'], writes=['res%d' % nb])
        P.dma('sp', o, res[:], reads=['res%d' % nb for nb in range(NB)], writes=['l0out'], semkey=okey)
        P.barrier()
    P.stack = P.semstack


def run_l0(c, c_ctx, mod_w, mod_b):
    cs = np.concatenate([c, c_ctx[None, :]], axis=0)
    csT = np.ascontiguousarray(cs.reshape(3, 32, 128).transpose(2, 1, 0)).reshape(128, 96)
    wall = np.concatenate([mod_w[0], mod_w[1]], axis=1)
    ball = np.concatenate([mod_b[0], mod_b[1]], axis=0)
    in_maps = []
    for r in range(NCORES):
        sl = slice(r * L0_COLS, (r + 1) * L0_COLS)
        in_maps.append({"csT": csT, "w": np.ascontiguousarray(wall[:, sl]),
                        "bias": np.ascontiguousarray(np.broadcast_to(ball[sl][None, :], (3, L0_COLS)))})
    nc = build_l0()
    res = _launch(nc, in_maps)
    return np.concatenate([r["o"] for r in res], axis=1)


L2_TOK = 1024
L2_T = 256


def make_identity(P, dtype_out=BF16):
    idf = P.sb([128, 128], F32)
    P.op('pool', lambda e: e.memset(idf[:], 0.0), writes=['idf'])
    P.op('pool', lambda e: e.affine_select(out=idf[:], in_=idf[:], pattern=[[-1, 128]], compare_op=ALU.not_equal,
                                           fill=1.0, base=0, channel_multiplier=1), reads=['idf'], writes=['idf'])
    if dtype_out == F32:
        return idf, 'idf'
    idb = P.sb([128, 128], BF16)
    P.op('dve', lambda e: e.tensor_copy(out=idb[:], in_=idf[:]), reads=['idf'], writes=['idb'])
    return idb, 'idb'


def build_l2():
    nc = bass.Bass("TRN2", target_bir_lowering=False)
    io = {}
    io['x'] = nc.dram_tensor("x", [L2_TOK, D], F32, kind="ExternalInput").ap()
    io['yT'] = nc.dram_tensor("yT", [128, 32 * L2_TOK], BF16, kind="ExternalInput").ap()
    io['wo'] = nc.dram_tensor("wo", [8 * 128, 32 * 512], F32, kind="ExternalInput").ap()
    io['wi'] = nc.dram_tensor("wi", [32 * 128, 32 * 4 * 128], F32, kind="ExternalInput").ap()
    io['wo1'] = nc.dram_tensor("wo1", [8 * 128, 32 * 512], F32, kind="ExternalInput").ap()
    io['cw'] = nc.dram_tensor("cw", [128, 96], F32, kind="ExternalInput").ap()
    io['m1'] = nc.dram_tensor("m1", [128, 96], F32, kind="ExternalInput").ap()
    io['g0b'] = nc.dram_tensor("g0b", [128, D], F32, kind="ExternalInput").ap()
    io['g1b'] = nc.dram_tensor("g1b", [128, D], F32, kind="ExternalInput").ap()
    io['fnb'] = nc.dram_tensor("fnb", [128, D], F32, kind="ExternalInput").ap()
    io['out'] = nc.dram_tensor("out", [L2_TOK, D], F32, kind="ExternalOutput").ap()
    with ExitStack() as st:
        P = Prog(nc, st)
        emit_l2b(P, io)
        P.finish()
        print("L2 instructions:", P.ninst)
    return nc


def emit_l2(P, io, ysrc=None):
    x, wo, wi, wo1 = io['x'], io['wo'], io['wi'], io['wo1']
    cw_d, m1_d, g0b_d, g1b_d, fnb_d, out = io['cw'], io['m1'], io['g0b'], io['g1b'], io['fnb'], io['out']
    if ysrc is None:
        yT3 = io['yT'].rearrange("p (k t) -> p k t", k=32)
    NT = L2_T // 128
    with ExitStack() as st:
        P.stack = st
        idb, idk = make_identity(P)
        if ysrc is not None:
            idxy = P.sb([128, 128], mybir.dt.int32)
            P.dma('sp', idxy[:], ysrc[1], writes=['idxy'])
        cw = P.sb([128, 96], F32); m1 = P.sb([128, 96], F32)
        a1 = P.sb([128, 32], F32); a2 = P.sb([128, 32], F32)
        P.dma('sp', cw[:], cw_d, writes=['cw'])
        P.dma('sp', m1[:], m1_d, writes=['m1'])
        P.op('dve', lambda e: e.scalar_tensor_tensor(out=a1[:], in0=m1[:, 32:64], scalar=1.0, in1=m1[:, 64:96],
                                                     op0=ALU.add, op1=ALU.mult), reads=['m1'], writes=['a1'])
        P.op('dve', lambda e: e.tensor_copy(out=a2[:], in_=m1[:, 0:32]), reads=['m1'], writes=['a2'])
        gb = P.sb([128, D], F32)
        x1 = P.sb([128, NT, D], F32)
        yTs = P.sb([128, 32, L2_T], BF16)
        hn1T = P.sb([128, 32, L2_T], BF16)
        y1T = P.sb([128, 32, L2_T], BF16)
        xn = P.sb([128, NT, D], BF16)
        wsl = [P.sb([128, 32 * 512], BF16) for _ in range(2)]
        small = P.sb([128, 16], F32)
        tmp = [P.sb([128, 512], F32) for _ in range(2)]
        ev = [P.sb([128, 5, L2_T], F32) for _ in range(2)]
        psb = [P.ps([128, 512], F32) for _ in range(6)]
        pst = [P.ps([128, 1024], BF16) for _ in range(2)]
        wcount = [0]

        def load_w(src_rows):
            s = wcount[0] % 2
            wcount[0] += 1
            P.dma('pool', wsl[s][:], src_rows, writes=['w%d' % s])
            return s

        def proj_residual(src, srck, wdram, gate_dram, ps_base):
            P.dma('sp', gb[:], gate_dram, writes=['gb'])
            for nb in range(8):
                s = load_w(wdram[nb * 128:(nb + 1) * 128, :])
                wv = wsl[s][:].rearrange("p (k c) -> p k c", k=32)
                for tt in range(NT):
                    pi = (nb * NT + tt) % 2
                    ps = psb[pi]
                    for kc in range(32):
                        P.op('pe', lambda e, ps=ps, kc=kc, tt=tt, wv=wv: e.matmul(
                            ps[:], lhsT=src[:, kc, tt * 128:(tt + 1) * 128], rhs=wv[:, kc, :],
                            start=(kc == 0), stop=(kc == 31)), reads=list(srck) + ['w%d' % s], writes=['psb%d' % pi])
                    tb = tmp[(nb * NT + tt) % 2]; tk = 'tmp%d' % ((nb * NT + tt) % 2)
                    P.op('dve', lambda e, ps=ps, tb=tb, nb=nb: e.tensor_tensor(
                        out=tb[:], in0=ps[:], in1=gb[:, nb * 512:(nb + 1) * 512], op=ALU.mult),
                        reads=['gb'], writes=[tk, 'psb%d' % pi])
                    xk = 'x1.%d.%d' % (tt, nb)
                    P.op('dve', lambda e, tb=tb, nb=nb, tt=tt: e.tensor_tensor(
                        out=x1[:, tt, nb * 512:(nb + 1) * 512], in0=x1[:, tt, nb * 512:(nb + 1) * 512], in1=tb[:],
                        op=ALU.add), reads=[tk, xk], writes=[xk])

        for ps_i in range(L2_TOK // L2_T):
            t0 = ps_i * L2_T
            for tt in range(NT):
                P.dma('sp', x1[:, tt, :], x[t0 + tt * 128:t0 + (tt + 1) * 128, :],
                      writes=['x1.%d.%d' % (tt, nb) for nb in range(8)])
            if ysrc is None:
                P.dma('sp', yTs[:], yT3[:, :, t0:t0 + L2_T], writes=['yTs'])
            else:
                for kc in range(32):
                    col = kc * 4 + ps_i
                    P.custom('pool', lambda e, kc=kc, col=col: e.indirect_dma_start(
                        out=yTs[:, kc, :], out_offset=None, in_=ysrc[0],
                        in_offset=bass.IndirectOffsetOnAxis(ap=idxy[:, col:col + 1], axis=0)),
                        reads=['yg2', 'idxy'], writes=['yTs.%d' % kc], semkey='d_yTs', inc=16)
            proj_residual(yTs, ['yTs'] if ysrc is None else ['yTs.%d' % k for k in range(32)], wo, g0b_d, 0)
            for tt in range(NT):
                xkeys = ['x1.%d.%d' % (tt, nb) for nb in range(8)]
                ss = small[:, tt:tt + 1]; rs = small[:, 4 + tt:5 + tt]
                P.op('dve', lambda e, ss=ss: e.memset(ss, 0.0), writes=['ss%d' % tt])
                P.op('act', lambda e, tt=tt, ss=ss: e.activation(out=xn[:, tt, :], in_=x1[:, tt, :], func=AF.Square, accum_out=ss),
                     reads=xkeys + ['ss%d' % tt], writes=['xn%d' % tt, 'ss%d' % tt])
                P.op('act', lambda e, ss=ss, rs=rs: e.activation(out=rs, in_=ss, func=AF.Sqrt, bias=EPS, scale=1.0 / D),
                     reads=['ss%d' % tt], writes=['rs%d' % tt])
                P.op('dve', lambda e, rs=rs: e.reciprocal(out=rs, in_=rs), reads=['rs%d' % tt], writes=['rs%d' % tt])
                P.op('dve', lambda e, tt=tt, rs=rs: e.tensor_scalar(out=xn[:, tt, :], in0=x1[:, tt, :], scalar1=rs, scalar2=None,
                                                                    op0=ALU.mult), reads=xkeys + ['rs%d' % tt], writes=['xn%d' % tt])
            for kc in range(32):
                half = kc % 2
                pk = 'pst%d' % half
                for tt in range(NT):
                    P.op('pe', lambda e, kc=kc, tt=tt, half=half: e.transpose(
                        out=pst[half][:, tt * 128:(tt + 1) * 128], in_=xn[:, tt, kc * 128:(kc + 1) * 128],
                        identity=idb[:]), reads=['xn%d' % tt, idk], writes=[pk])
                P.op('act', lambda e, kc=kc, half=half: e.activation(
                    out=hn1T[:, kc, :], in_=pst[half][:, 0:L2_T], func=AF.Identity,
                    bias=a2[:, kc:kc + 1], scale=a1[:, kc:kc + 1]), reads=['a1', 'a2'], writes=['hn1T', pk])
            for cc in range(32):
                s = load_w(wi[cc * 128:(cc + 1) * 128, :])
                wv = wsl[s][:].rearrange("p (k g c) -> p k g c", k=32, g=4)
                par = cc % 2
                pg = []
                for g in range(4):
                    bank = psb[2 + par * 2 + g // 2]
                    pv = bank[:, (g % 2) * 256:(g % 2) * 256 + L2_T]
                    pkey = 'pcb%d.%d' % (par, g // 2)
                    for kc in range(32):
                        P.op('pe', lambda e, pv=pv, kc=kc, g=g, wv=wv: e.matmul(
                            pv, lhsT=wv[:, kc, g, :], rhs=hn1T[:, kc, :], start=(kc == 0), stop=(kc == 31)),
                            reads=['hn1T', 'w%d' % s], writes=[pkey])
                    pg.append((pv, pkey))
                E = ev[par]; ek = 'ev%d' % par
                xin, cx, z, sg, tq = E[:, 0, :], E[:, 1, :], E[:, 2, :], E[:, 3, :], E[:, 4, :]
                P.op('act', lambda e, xin=xin, pv=pg[2][0]: e.activation(out=xin, in_=pv, func=AF.Copy),
                     writes=[ek + 'x', pg[2][1]])
                P.op('act', lambda e, sg=sg, pv=pg[3][0]: e.activation(out=sg, in_=pv, func=AF.Silu),
                     writes=[ek + 's', pg[3][1]])
                P.op('dve', lambda e, cx=cx, xin=xin, pv=pg[1][0]: e.tensor_tensor(out=cx, in0=pv, in1=xin, op=ALU.mult),
                     reads=[ek + 'x'], writes=[ek + 'c', pg[1][1]])
                cx3 = cx.rearrange("p (r c) -> p r c", c=64)
                z3 = z.rearrange("p (r c) -> p r c", c=64)
                P.op('dve', lambda e, z=z, cx=cx, cc=cc: e.tensor_scalar(out=z, in0=cx, scalar1=cw[:, cc * 3 + 1:cc * 3 + 2], scalar2=None,
                                                                        op0=ALU.mult), reads=[ek + 'c', 'cw'], writes=[ek + 'z'])
                P.op('dve', lambda e, z3=z3, cx3=cx3, cc=cc: e.scalar_tensor_tensor(
                    out=z3[:, :, 1:64], in0=cx3[:, :, 0:63], scalar=cw[:, cc * 3:cc * 3 + 1], in1=z3[:, :, 1:64],
                    op0=ALU.mult, op1=ALU.add), reads=[ek + 'c', ek + 'z', 'cw'], writes=[ek + 'z'])
                P.op('dve', lambda e, z3=z3, cx3=cx3, cc=cc: e.scalar_tensor_tensor(
                    out=z3[:, :, 0:63], in0=cx3[:, :, 1:64], scalar=cw[:, cc * 3 + 2:cc * 3 + 3], in1=z3[:, :, 0:63],
                    op0=ALU.mult, op1=ALU.add), reads=[ek + 'c', ek + 'z', 'cw'], writes=[ek + 'z'])
                P.op('dve', lambda e, tq=tq, z=z, sg=sg: e.tensor_tensor(out=tq, in0=z, in1=sg, op=ALU.mult),
                     reads=[ek + 'z', ek + 's'], writes=[ek + 't'])
                P.op('dve', lambda e, tq=tq, cc=cc, pv=pg[0][0]: e.tensor_tensor(out=y1T[:, cc, :], in0=pv, in1=tq, op=ALU.mult),
                     reads=[ek + 't'], writes=['y1T', pg[0][1]])
            proj_residual(y1T, ['y1T'], wo1, g1b_d, 0)
            P.dma('sp', gb[:], fnb_d, writes=['gb'])
            for tt in range(NT):
                xkeys = ['x1.%d.%d' % (tt, nb) for nb in range(8)]
                ss = small[:, 8 + tt:9 + tt]; rs = small[:, 12 + tt:13 + tt]
                P.op('dve', lambda e, ss=ss: e.memset(ss, 0.0), writes=['fs%d' % tt])
                P.op('act', lambda e, tt=tt, ss=ss: e.activation(out=xn[:, tt, :], in_=x1[:, tt, :], func=AF.Square, accum_out=ss),
                     reads=xkeys + ['fs%d' % tt], writes=['xn%d' % tt, 'fs%d' % tt])
                P.op('act', lambda e, ss=ss, rs=rs: e.activation(out=rs, in_=ss, func=AF.Sqrt, bias=EPS, scale=1.0 / D),
                     reads=['fs%d' % tt], writes=['fr%d' % tt])
                P.op('dve', lambda e, rs=rs: e.reciprocal(out=rs, in_=rs), reads=['fr%d' % tt], writes=['fr%d' % tt])
                P.op('dve', lambda e, tt=tt, rs=rs: e.scalar_tensor_tensor(
                    out=x1[:, tt, :], in0=x1[:, tt, :], scalar=rs, in1=gb[:], op0=ALU.mult, op1=ALU.mult),
                    reads=xkeys + ['fr%d' % tt, 'gb'], writes=xkeys)
                P.dma('sp', out[t0 + tt * 128:t0 + (tt + 1) * 128, :], x1[:, tt, :], reads=xkeys, semkey='st_out%d' % tt)
        P.barrier()
    P.stack = P.semstack


def emit_l2b(P, io):
    T = 512
    NT = 4
    x, wo, wi, wo1 = io['x'], io['wo'], io['wi'], io['wo1']
    cw_d, m1_d, g0b_d, g1b_d, fnb_d, out = io['cw'], io['m1'], io['g0b'], io['g1b'], io['fnb'], io['out']
    yT3 = io['yT'].rearrange("p (k t) -> p k t", k=32)
    with ExitStack() as st:
        P.stack = st
        idb, idk = make_identity(P)
        cw = P.sb([128, 96], F32); m1 = P.sb([128, 96], F32)
        a1 = P.sb([128, 32], F32); a2 = P.sb([128, 32], F32)
        P.dma('sp', cw[:], cw_d, writes=['cw'])
        P.dma('sp', m1[:], m1_d, writes=['m1'])
        P.op('dve', lambda e: e.scalar_tensor_tensor(out=a1[:], in0=m1[:, 32:64], scalar=1.0, in1=m1[:, 64:96],
                                                     op0=ALU.add, op1=ALU.mult), reads=['m1'], writes=['a1'])
        P.op('dve', lambda e: e.tensor_copy(out=a2[:], in_=m1[:, 0:32]), reads=['m1'], writes=['a2'])
        x1 = P.sb([128, NT, D], F32)
        bufA = P.sb([128, 32, T], BF16)
        bufB = P.sb([128, 32 * T], BF16)
        yTs = bufA; hn1T = bufA
        xn = bufB[:].rearrange("p (t d) -> p t d", t=NT)
        y1T = bufB[:].rearrange("p (k t) -> p k t", k=32)
        AK = ['bufA']; BK = ['bufB']
        gblk = [P.sb([128, 512], F32) for _ in range(2)]
        wsl = [P.sb([128, 32 * 512], BF16) for _ in range(2)]
        small = P.sb([128, 16], F32)
        tmp = [P.sb([128, 512], F32) for _ in range(2)]
        E = P.sb([128, 3, T], F32)
        psb = [P.ps([128, 512], F32) for _ in range(6)]
        pst = [P.ps([128, 1024], BF16) for _ in range(2)]
        wcount = [0]
        gcount = [0]

        def load_w(src_rows):
            s = wcount[0] % 2
            wcount[0] += 1
            P.dma('pool', wsl[s][:], src_rows, writes=['w%d' % s])
            return s

        def load_g(gate_dram, nb):
            gi = gcount[0] % 2
            gcount[0] += 1
            P.dma('sp', gblk[gi][:], gate_dram[:, nb * 512:(nb + 1) * 512], writes=['gblk%d' % gi])
            return gi

        def proj_residual(src, srck, wdram, gate_dram):
            for nb in range(8):
                s = load_w(wdram[nb * 128:(nb + 1) * 128, :])
                gi = load_g(gate_dram, nb)
                wv = wsl[s][:].rearrange("p (k c) -> p k c", k=32)
                for tt in range(NT):
                    pi = (nb * NT + tt) % 2
                    ps = psb[pi]
                    for kc in range(32):
                        P.op('pe', lambda e, ps=ps, kc=kc, tt=tt, wv=wv: e.matmul(
                            ps[:], lhsT=src[:, kc, tt * 128:(tt + 1) * 128], rhs=wv[:, kc, :],
                            start=(kc == 0), stop=(kc == 31)), reads=list(srck) + ['w%d' % s], writes=['psb%d' % pi])
                    tb = tmp[pi]; tk = 'tmp%d' % pi
                    P.op('dve', lambda e, ps=ps, tb=tb, gi=gi: e.tensor_tensor(out=tb[:], in0=ps[:], in1=gblk[gi][:], op=ALU.mult),
                         reads=['gblk%d' % gi], writes=[tk, 'psb%d' % pi])
                    xk = 'x1.%d.%d' % (tt, nb)
                    P.op('dve', lambda e, tb=tb, nb=nb, tt=tt: e.tensor_tensor(
                        out=x1[:, tt, nb * 512:(nb + 1) * 512], in0=x1[:, tt, nb * 512:(nb + 1) * 512], in1=tb[:],
                        op=ALU.add), reads=[tk, xk], writes=[xk])

        def row_stats(tt, base):
            xkeys = ['x1.%d.%d' % (tt, nb) for nb in range(8)]
            ss = small[:, base + tt:base + tt + 1]; rs = small[:, base + 4 + tt:base + 5 + tt]
            sk = 'ss%d.%d' % (base, tt); rk = 'rs%d.%d' % (base, tt)
            P.op('dve', lambda e, ss=ss: e.memset(ss, 0.0), writes=[sk])
            P.op('act', lambda e, tt=tt, ss=ss: e.activation(out=xn[:, tt, :], in_=x1[:, tt, :], func=AF.Square, accum_out=ss),
                 reads=xkeys + [sk], writes=BK + [sk])
            P.op('act', lambda e, ss=ss, rs=rs: e.activation(out=rs, in_=ss, func=AF.Sqrt, bias=EPS, scale=1.0 / D), reads=[sk], writes=[rk])
            P.op('dve', lambda e, rs=rs: e.reciprocal(out=rs, in_=rs), reads=[rk], writes=[rk])
            return xkeys, rs, rk

        for ps_i in range(L2_TOK // T):
            t0 = ps_i * T
            for tt in range(NT):
                P.dma('sp', x1[:, tt, :], x[t0 + tt * 128:t0 + (tt + 1) * 128, :], writes=['x1.%d.%d' % (tt, nb) for nb in range(8)])
            P.dma('sp', yTs[:], yT3[:, :, t0:t0 + T], writes=AK)
            proj_residual(yTs, AK, wo, g0b_d)
            for tt in range(NT):
                xkeys, rs, rk = row_stats(tt, 0)
                P.op('dve', lambda e, tt=tt, rs=rs: e.tensor_scalar(out=xn[:, tt, :], in0=x1[:, tt, :], scalar1=rs, scalar2=None, op0=ALU.mult),
                     reads=xkeys + [rk], writes=BK)
            for kc in range(32):
                half = kc % 2
                pk = 'pst%d' % half
                for tt in range(NT):
                    P.op('pe', lambda e, kc=kc, tt=tt, half=half: e.transpose(
                        out=pst[half][:, tt * 128:(tt + 1) * 128], in_=xn[:, tt, kc * 128:(kc + 1) * 128], identity=idb[:]),
                        reads=BK + [idk], writes=[pk])
                P.op('act', lambda e, kc=kc, half=half: e.activation(
                    out=hn1T[:, kc, :], in_=pst[half][:, 0:T], func=AF.Identity, bias=a2[:, kc:kc + 1], scale=a1[:, kc:kc + 1]),
                    reads=['a1', 'a2'], writes=AK + [pk])
            for cc in range(32):
                s = load_w(wi[cc * 128:(cc + 1) * 128, :])
                wv = wsl[s][:].rearrange("p (k g c) -> p k g c", k=32, g=4)
                pg = {}
                for g in (1, 2, 3, 0):
                    pv = psb[2 + g][:, 0:T]
                    pkey = 'pcb%d' % g
                    for kc in range(32):
                        P.op('pe', lambda e, pv=pv, kc=kc, g=g, wv=wv: e.matmul(
                            pv, lhsT=wv[:, kc, g, :], rhs=hn1T[:, kc, :], start=(kc == 0), stop=(kc == 31)),
                            reads=AK + ['w%d' % s], writes=[pkey])
                    pg[g] = (pv, pkey)
                xin, z, sg = E[:, 0, :], E[:, 1, :], E[:, 2, :]
                cx = xin
                tq = sg
                P.op('act', lambda e, pv=pg[2][0]: e.activation(out=xin, in_=pv, func=AF.Copy), writes=['evx', 'evc', pg[2][1]])
                P.op('act', lambda e, pv=pg[3][0]: e.activation(out=sg, in_=pv, func=AF.Silu), writes=['evs', 'evt', pg[3][1]])
                P.op('dve', lambda e, pv=pg[1][0]: e.tensor_tensor(out=cx, in0=pv, in1=xin, op=ALU.mult), reads=['evx'], writes=['evc', 'evx', pg[1][1]])
                cx3 = cx.rearrange("p (r c) -> p r c", c=64)
                z3 = z.rearrange("p (r c) -> p r c", c=64)
                P.op('dve', lambda e, cc=cc: e.tensor_scalar(out=z, in0=cx, scalar1=cw[:, cc * 3 + 1:cc * 3 + 2], scalar2=None, op0=ALU.mult),
                     reads=['evc', 'cw'], writes=['evz'])
                P.op('dve', lambda e, cc=cc: e.scalar_tensor_tensor(out=z3[:, :, 1:64], in0=cx3[:, :, 0:63], scalar=cw[:, cc * 3:cc * 3 + 1],
                                                                  in1=z3[:, :, 1:64], op0=ALU.mult, op1=ALU.add), reads=['evc', 'evz', 'cw'], writes=['evz'])
                P.op('dve', lambda e, cc=cc: e.scalar_tensor_tensor(out=z3[:, :, 0:63], in0=cx3[:, :, 1:64], scalar=cw[:, cc * 3 + 2:cc * 3 + 3],
                                                                  in1=z3[:, :, 0:63], op0=ALU.mult, op1=ALU.add), reads=['evc', 'evz', 'cw'], writes=['evz'])
                P.op('dve', lambda e: e.tensor_tensor(out=tq, in0=z, in1=sg, op=ALU.mult), reads=['evz', 'evs'], writes=['evt', 'evs'])
                P.op('dve', lambda e, cc=cc, pv=pg[0][0]: e.tensor_tensor(out=y1T[:, cc, :], in0=pv, in1=tq, op=ALU.mult),
                     reads=['evt'], writes=BK + [pg[0][1]])
            proj_residual(y1T, BK, wo1, g1b_d)
            stats = [row_stats(tt, 8) for tt in range(NT)]
            for nb in range(8):
                gi = load_g(fnb_d, nb)
                for tt in range(NT):
                    xkeys, rs, rk = stats[tt]
                    xk = 'x1.%d.%d' % (tt, nb)
                    P.op('dve', lambda e, tt=tt, nb=nb, rs=rs, gi=gi: e.scalar_tensor_tensor(
                        out=x1[:, tt, nb * 512:(nb + 1) * 512], in0=x1[:, tt, nb * 512:(nb + 1) * 512], scalar=rs, in1=gblk[gi][:],
                        op0=ALU.mult, op1=ALU.mult), reads=[xk, rk, 'gblk%d' % gi], writes=[xk])
            for tt in range(NT):
                P.dma('sp', out[t0 + tt * 128:t0 + (tt + 1) * 128, :], x1[:, tt, :], reads=['x1.%d.%d' % (tt, nb) for nb in range(8)],
                      semkey='st_out%d' % tt)
        P.barrier()
    P.stack = P.semstack


def l2_layouts(inputs):
    def blk(w):
        return np.ascontiguousarray(w.reshape(32, 128, 8, 512).transpose(2, 1, 0, 3)).reshape(8 * 128, 32 * 512)
    wo = blk(inputs['ab_w_out'][0])
    wo1 = blk(inputs['sc_w_out'][0])
    wi = inputs['sc_w_in'][0].reshape(32, 128, 4, 32, 128)
    wi = np.ascontiguousarray(wi.transpose(3, 1, 0, 2, 4)).reshape(32 * 128, 32 * 4 * 128)
    cw = np.ascontiguousarray(inputs['sc_conv'][0].reshape(3, 32, 128).transpose(2, 1, 0)).reshape(128, 96)
    return wo, wi, wo1, cw


def pcol(v):
    return np.ascontiguousarray(v.reshape(32, 128).T)


def run_l2(inputs, modv, yT_full):
    wo, wi, wo1, cw = l2_layouts(inputs)
    fnb = np.ascontiguousarray(np.broadcast_to(inputs['final_norm_w'][None, :], (128, D)))
    in_maps = []
    for r in range(NCORES):
        b, tq = r // 4, r % 4
        tsl = slice(tq * L2_TOK, (tq + 1) * L2_TOK)
        gate0 = modv[b, 2 * D:3 * D]
        sh1, sc1, gate1 = (modv[b, 3 * D + i * D:3 * D + (i + 1) * D] for i in range(3))
        m1 = np.concatenate([pcol(sh1), pcol(sc1), pcol(inputs['norm_w'][1])], axis=1)
        yTl = np.ascontiguousarray(yT_full[b][:, tsl].reshape(32, 128, L2_TOK).transpose(1, 0, 2)).reshape(128, 32 * L2_TOK)
        in_maps.append({
            "x": np.ascontiguousarray(inputs['x'][b, tsl]), "yT": yTl, "wo": wo, "wi": wi, "wo1": wo1, "cw": cw,
            "m1": np.ascontiguousarray(m1),
            "g0b": np.ascontiguousarray(np.broadcast_to(gate0[None, :], (128, D))),
            "g1b": np.ascontiguousarray(np.broadcast_to(gate1[None, :], (128, D))),
            "fnb": fnb})
    nc = build_l2()
    res = _launch(nc, in_maps)
    out = np.zeros((2, 4096, D), np.float32)
    for r in range(NCORES):
        b, tq = r // 4, r % 4
        out[b, tq * L2_TOK:(tq + 1) * L2_TOK] = res[r]["out"]
    return out


L1_TOK = 4352
NCHK = 34
XW = 4358
NEG = -1.0e30
M_MASKF, M_MASKB, M_NEGF, M_NEGB, M_STRF, M_STRB, M_BD2, M_LM0, M_ID = 0, 1, 2, 3, 4, 5, 6, 7, 13


def l1_groups():
    gs = [(0, 0, 256)]
    for g in range(1, 9):
        gs.append((g, 256 + 512 * (g - 1), 512))
    return gs


def xpad(u):
    return u + 2 if u < 256 else u + 5


L1_INPUTS = [("xa", [L1_TOK, D]), ("wmain", [24 * 128, 32 * 128]), ("wab", [128, 32 * 16]), ("mod0", [128, 160]),
             ("cwq", [128, 48]), ("alog", [128, 272]), ("dtb", [128, 272]), ("dnw", [128, 1]), ("lcw", [128, 16]),
             ("lcb", [128, 4]), ("lwr", [128, 1024]), ("lwi", [128, 1024]), ("lbr", [128, 8]), ("lbi", [128, 8]),
             ("llam", [128, 8]), ("msk", [128, 14 * 128])]


def build_l1(debug=False):
    nc = bass.Bass("TRN2", target_bir_lowering=False)
    io = {}
    for name, shape in L1_INPUTS:
        io[name] = nc.dram_tensor(name, list(shape), F32, kind="ExternalInput").ap()
    io['yT'] = nc.dram_tensor("yT", [1024, 4096], BF16, kind="ExternalOutput").ap()
    io['hnT_d'] = nc.dram_tensor("hnT_d", [128, 9 * 32 * 512], BF16, kind="Internal").ap()
    with ExitStack() as st:
        P = Prog(nc, st)
        emit_l1(P, io)
        P.finish()
        print("L1 instructions:", P.ninst)
    return nc


def emit_l1(P, io):
    xa, wmain, wab, mod0 = io['xa'], io['wmain'], io['wab'], io['mod0']
    cwq_d, alog_d, dtb_d, dnw_d = io['cwq'], io['alog'], io['dtb'], io['dnw']
    lcw_d, lcb_d, lwr_d, lwi_d = io['lcw'], io['lcb'], io['lwr'], io['lwi']
    lbr_d, lbi_d, llam_d, msk_d = io['lbr'], io['lbi'], io['llam'], io['msk']
    yT = io['yT']
    hnTg = io['hnT_d'].rearrange("p (g k t) -> p g k t", g=9, k=32)
    groups = l1_groups()

    with ExitStack() as st:
        P.stack = st
        msk = P.sb([128, 14, 128], F32)
        P.dma('sp', msk[:].rearrange("p a b -> p (a b)"), msk_d, writes=['msk'])
        ident = msk[:, M_ID, :]
        identb = P.sb([128, 128], BF16)
        P.op('dve', lambda e: e.tensor_copy(out=identb[:], in_=ident), reads=['msk'], writes=['identb'])
        ones = P.sb([128, 128], F32)
        P.op('pool', lambda e: e.memset(ones[:], 1.0), writes=['ones'])
        m0 = P.sb([128, 160], F32)
        P.dma('sp', m0[:], mod0, writes=['m0'])
        a1 = P.sb([128, 64], F32); a2 = P.sb([128, 64], F32)
        P.op('dve', lambda e: e.scalar_tensor_tensor(out=a1[:, 0:32], in0=m0[:, 32:64], scalar=1.0, in1=m0[:, 128:160],
                                                     op0=ALU.add, op1=ALU.mult), reads=['m0'], writes=['a1'])
        P.op('dve', lambda e: e.scalar_tensor_tensor(out=a1[:, 32:64], in0=m0[:, 96:128], scalar=1.0, in1=m0[:, 128:160],
                                                     op0=ALU.add, op1=ALU.mult), reads=['m0', 'a1'], writes=['a1'])
        P.op('dve', lambda e: e.tensor_copy(out=a2[:, 0:32], in_=m0[:, 0:32]), reads=['m0'], writes=['a2'])
        P.op('dve', lambda e: e.tensor_copy(out=a2[:, 32:64], in_=m0[:, 64:96]), reads=['m0', 'a2'], writes=['a2'])
        cwq = P.sb([128, 48], F32); P.dma('sp', cwq[:], cwq_d, writes=['cwq'])
        dnw = P.sb([128, 1], F32); P.dma('sp', dnw[:], dnw_d, writes=['dnw'])
        ABT = P.sb([128, NCHK, 16], F32)
        BETA = P.sb([128, NCHK, 8], F32); NBETA = P.sb([128, NCHK, 8], F32)
        G = P.sb([128, NCHK, 8], F32); GC = P.sb([128, NCHK, 8], F32)
        EGC = P.sb([128, NCHK, 8], F32); GE = P.sb([128, NCHK, 8], F32); DEND = P.sb([128, NCHK, 8], F32)
        stat = P.sb([128, 64], F32)

        with ExitStack() as ph:
            xt = [P.sb([128, D], F32, stack=ph) for _ in range(3)]
            xnb = [P.sb([128, 4, D], BF16, stack=ph) for _ in range(2)]
            hst = [P.sb([128, 32, 512], BF16, stack=ph) for _ in range(2)]
            wabs = P.sb([128, 32, 16], BF16, stack=ph)
            P.dma('pool', wabs[:].rearrange("p k c -> p (k c)"), wab, writes=['wabs'])
            pst = [P.ps([128, 1024], BF16, stack=ph) for _ in range(2)]
            psab = [P.ps([128, 512], F32, stack=ph) for _ in range(2)]
            ti = 0
            for (g, tok0, W) in groups:
                ntile = W // 128
                gs = g % 2
                mo = 32 if g == 0 else 0
                for tt in range(ntile):
                    s = ti % 3
                    sc = ti % 16
                    ss = stat[:, sc:sc + 1]; rs = stat[:, 16 + sc:17 + sc]
                    P.dma('sp', xt[s][:], xa[tok0 + tt * 128: tok0 + (tt + 1) * 128, :], writes=['xt%d' % s])
                    P.op('dve', lambda e, ss=ss: e.memset(ss, 0.0), writes=['ss%d' % sc])
                    P.op('act', lambda e, s=s, gs=gs, tt=tt, ss=ss: e.activation(out=xnb[gs][:, tt, :], in_=xt[s][:], func=AF.Square, accum_out=ss),
                         reads=['xt%d' % s, 'ss%d' % sc], writes=['xnb%d.%d' % (gs, tt), 'ss%d' % sc])
                    P.op('act', lambda e, ss=ss, rs=rs: e.activation(out=rs, in_=ss, func=AF.Sqrt, bias=EPS, scale=1.0 / D),
                         reads=['ss%d' % sc], writes=['rs%d' % sc])
                    P.op('dve', lambda e, rs=rs: e.reciprocal(out=rs, in_=rs), reads=['rs%d' % sc], writes=['rs%d' % sc])
                    P.op('dve', lambda e, s=s, gs=gs, tt=tt, rs=rs: e.tensor_scalar(out=xnb[gs][:, tt, :], in0=xt[s][:], scalar1=rs, scalar2=None, op0=ALU.mult),
                         reads=['xt%d' % s, 'rs%d' % sc], writes=['xnb%d.%d' % (gs, tt)])
                    ti += 1
                for kc in range(32):
                    half = kc % 2
                    for tt in range(ntile):
                        P.op('pe', lambda e, kc=kc, tt=tt, gs=gs, half=half: e.transpose(
                            out=pst[half][:, tt * 128:(tt + 1) * 128], in_=xnb[gs][:, tt, kc * 128:(kc + 1) * 128], identity=identb[:]),
                            reads=['xnb%d.%d' % (gs, tt), 'identb'], writes=['pst%d' % half])
                    P.op('act', lambda e, kc=kc, gs=gs, half=half, W=W, mo=mo: e.activation(
                        out=hst[gs][:, kc, 0:W], in_=pst[half][:, 0:W], func=AF.Identity,
                        bias=a2[:, mo + kc:mo + kc + 1], scale=a1[:, mo + kc:mo + kc + 1]),
                        reads=['a1', 'a2'], writes=['hst%d' % gs, 'pst%d' % half])
                for tt in range(ntile):
                    T = tok0 // 128 + tt
                    bank = psab[0] if T < 32 else psab[1]
                    bk = 'psab0' if T < 32 else 'psab1'
                    col = (T % 32) * 16
                    for kc in range(32):
                        P.op('pe', lambda e, bank=bank, col=col, kc=kc, gs=gs, tt=tt: e.matmul(
                            bank[:, col:col + 16], lhsT=hst[gs][:, kc, tt * 128:(tt + 1) * 128], rhs=wabs[:, kc, :],
                            start=(kc == 0), stop=(kc == 31)), reads=['hst%d' % gs, 'wabs'], writes=[bk])
                P.dma('sp', hnTg[:, g, :, 0:W], hst[gs][:, :, 0:W], reads=['hst%d' % gs], writes=['hnT.%d' % g])
            ABf = ABT[:].rearrange("p c k -> p (c k)")
            P.op('dve', lambda e: e.tensor_copy(out=ABf[:, 0:512], in_=psab[0][:, :]), writes=['ABT', 'psab0'])
            P.op('dve', lambda e: e.tensor_copy(out=ABf[:, 512:544], in_=psab[1][:, 0:32]), reads=['ABT'], writes=['ABT', 'psab1'])
        P.barrier()

        with ExitStack() as ph:
            alog = P.sb([128, NCHK, 8], F32, stack=ph); dtb = P.sb([128, NCHK, 8], F32, stack=ph)
            P.dma('sp', alog[:].rearrange("p c k -> p (c k)"), alog_d, writes=['alog'])
            P.dma('sp', dtb[:].rearrange("p c k -> p (c k)"), dtb_d, writes=['dtb'])
            tz = P.sb([128, NCHK, 8], F32, stack=ph)
            GF = P.sb([128, 2, NCHK * 4], F32, stack=ph)
            pg = [P.ps([128, 512], F32, stack=ph) for _ in range(3)]
            P.op('act', lambda e: e.activation(out=BETA[:], in_=ABT[:, :, 0:8], func=AF.Sigmoid), reads=['ABT'], writes=['BETA'])
            P.op('act', lambda e: e.mul(out=NBETA[:], in_=BETA[:], mul=-1.0), reads=['BETA'], writes=['NBETA'])
            P.op('dve', lambda e: e.tensor_tensor(out=tz[:], in0=ABT[:, :, 8:16], in1=dtb[:], op=ALU.add), reads=['ABT', 'dtb'], writes=['tz'])
            P.op('act', lambda e: e.activation(out=tz[:], in_=tz[:], func=AF.Exp), reads=['tz'], writes=['tz'])
            P.op('act', lambda e: e.activation(out=tz[:], in_=tz[:], func=AF.Ln, bias=1.0, scale=1.0), reads=['tz'], writes=['tz'])
            P.op('act', lambda e: e.activation(out=alog[:], in_=alog[:], func=AF.Exp), reads=['alog'], writes=['alog'])
            P.op('dve', lambda e: e.scalar_tensor_tensor(out=G[:], in0=tz[:], scalar=-1.0, in1=alog[:], op0=ALU.mult, op1=ALU.mult),
                 reads=['tz', 'alog'], writes=['G'])
            for d in range(2):
                P.op('dve', lambda e, d=d: e.tensor_copy(out=GF[:, d, :].rearrange("p (c k) -> p c k", k=4), in_=G[:, :, d * 4:(d + 1) * 4]),
                     reads=['G', 'GF'], writes=['GF'])
            for d in range(2):
                P.op('pe', lambda e, d=d: e.matmul(pg[d][:, 0:NCHK * 4], lhsT=msk[:, M_MASKF + d, :], rhs=GF[:, d, :], start=True, stop=True),
                     reads=['msk', 'GF'], writes=['pg%d' % d])
                P.op('dve', lambda e, d=d: e.tensor_copy(out=GC[:, :, d * 4:(d + 1) * 4], in_=pg[d][:, 0:NCHK * 4].rearrange("p (c k) -> p c k", k=4)),
                     reads=['GC'], writes=['GC', 'pg%d' % d])
            P.op('pe', lambda e: e.matmul(pg[2][:, 0:NCHK * 8], lhsT=ones[:], rhs=G[:].rearrange("p c k -> p (c k)"), start=True, stop=True),
                 reads=['ones', 'G'], writes=['pg2'])
            P.op('dve', lambda e: e.tensor_copy(out=tz[:].rearrange("p c k -> p (c k)"), in_=pg[2][:, 0:NCHK * 8]), reads=['tz'], writes=['tz', 'pg2'])
            P.op('act', lambda e: e.activation(out=GE[:], in_=tz[:], func=AF.Exp), reads=['tz'], writes=['GE'])
            P.op('act', lambda e: e.activation(out=EGC[:], in_=GC[:], func=AF.Exp), reads=['GC'], writes=['EGC'])
            P.op('dve', lambda e: e.tensor_tensor(out=tz[:], in0=tz[:], in1=GC[:], op=ALU.subtract), reads=['tz', 'GC'], writes=['tz'])
            P.op('act', lambda e: e.activation(out=DEND[:], in_=tz[:], func=AF.Exp), reads=['tz'], writes=['DEND'])
        P.barrier()

        def inproj(ph, chunks, evac, skip=lambda ci, g: False):
            n = len(chunks)
            wsb = P.sb([128, n, 32 * 128], BF16, stack=ph)
            for ci, ch in enumerate(chunks):
                P.dma('pool', wsb[:, ci, :], wmain[ch * 128:(ch + 1) * 128, :], writes=['wsb%d' % ci])
            hg = [P.sb([128, 32, 512], BF16, stack=ph) for _ in range(2)]
            psb = [P.ps([128, 512], F32, stack=ph) for _ in range(3)]
            cnt = 0
            for (g, tok0, W) in groups:
                s = g % 2
                P.dma('sp', hg[s][:, :, 0:W], hnTg[:, g, :, 0:W], reads=['hnT.%d' % g], writes=['hg%d' % s])
                for ci, ch in enumerate(chunks):
                    if skip(ci, g):
                        continue
                    pi = cnt % 3
                    cnt += 1
                    wv = wsb[:, ci, :].rearrange("p (k c) -> p k c", k=32)
                    for kc in range(32):
                        P.op('pe', lambda e, pi=pi, W=W, wv=wv, kc=kc, s=s: e.matmul(
                            psb[pi][:, 0:W], lhsT=wv[:, kc, :], rhs=hg[s][:, kc, 0:W], start=(kc == 0), stop=(kc == 31)),
                            reads=['wsb%d' % ci, 'hg%d' % s], writes=['ipb%d' % pi])
                    evac(ci, g, tok0, W, psb[pi][:, 0:W], 'ipb%d' % pi)

        def conv4(X, S_, wt, wbase, bias=None, eng='dve'):
            for (lo, hi) in ((0, 256), (256, L1_TOK)):
                p0 = xpad(lo) - 2
                n = hi - lo
                for j in range(4):
                    src = X[:, p0 + j:p0 + j + n]
                    w = wt[:, wbase + j:wbase + j + 1]
                    if j == 0:
                        if bias is None:
                            P.op(eng, lambda e, src=src, w=w, lo=lo, hi=hi: e.tensor_scalar(out=S_[:, lo:hi], in0=src, scalar1=w, scalar2=None, op0=ALU.mult),
                                 reads=['X', 'cw'], writes=['Sw'])
                        else:
                            P.op(eng, lambda e, src=src, w=w, lo=lo, hi=hi: e.tensor_scalar(out=S_[:, lo:hi], in0=src, scalar1=w, scalar2=bias, op0=ALU.mult, op1=ALU.add),
                                 reads=['X', 'cw'], writes=['Sw'])
                    else:
                        P.op(eng, lambda e, src=src, w=w, lo=lo, hi=hi: e.scalar_tensor_tensor(out=S_[:, lo:hi], in0=src, scalar=w, in1=S_[:, lo:hi], op0=ALU.mult, op1=ALU.add),
                             reads=['X', 'cw', 'Sw'], writes=['Sw'])

        blocks9 = [(0, 256)] + [(256 + 512 * i, 256 + 512 * (i + 1)) for i in range(8)]

        for hl in range(4):
            with ExitStack() as hs:
                XQ = [P.sb([128, XW], F32, stack=hs) for _ in range(3)]
                SG = P.sb([128, 4096], BF16, stack=hs)
                for i in range(3):
                    P.op('pool', lambda e, i=i: e.memset(XQ[i][:], 0.0), writes=['X%d.%d' % (i, g) for g in range(9)])
                with ExitStack() as ph:
                    def evac_head(ci, g, tok0, W, ps, pk):
                        if ci < 3:
                            dst = XQ[ci][:, xpad(tok0):xpad(tok0) + W]
                            if (ci + g) % 2 == 0:
                                P.op('act', lambda e, dst=dst, ps=ps: e.activation(out=dst, in_=ps, func=AF.Copy), writes=['X%d.%d' % (ci, g), pk])
                            else:
                                P.op('dve', lambda e, dst=dst, ps=ps: e.tensor_copy(out=dst, in_=ps), writes=['X%d.%d' % (ci, g), pk])
                        else:
                            P.op('act', lambda e, ps=ps, tok0=tok0, W=W: e.activation(out=SG[:, tok0 - 256:tok0 - 256 + W], in_=ps, func=AF.Silu),
                                 writes=['SG.%d' % g, pk])
                    inproj(ph, [hl, 4 + hl, 8 + hl, 16 + hl], evac_head, skip=lambda ci, g: (ci == 3 and g == 0))
                P.barrier()
                with ExitStack() as ph:
                    S_ = P.sb([128, L1_TOK], F32, stack=ph)
                    qT = P.sb([128, L1_TOK], BF16, stack=ph); kT = P.sb([128, L1_TOK], BF16, stack=ph); vT = P.sb([128, L1_TOK], BF16, stack=ph)
                    ktok = P.sb([128, NCHK, 128], BF16, stack=ph); vtok = P.sb([128, NCHK, 128], BF16, stack=ph)
                    OT = P.sb([128, 4096], F32, stack=ph)
                    ybuf = P.sb([128, 4096], BF16, stack=ph)
                    wk = [P.sb([128, 512], F32, stack=ph) for _ in range(3)]
                    pf = [P.ps([128, 512], F32, stack=ph) for _ in range(6)]
                    ptb = [P.ps([128, 1024], BF16, stack=ph) for _ in range(2)]
                    Xkeys = lambda i: ['X%d.%d' % (i, g) for g in range(9)]
                    for i, dstT in enumerate((qT, kT, vT)):
                        P.op('dve', lambda e: e.memset(stat[:, 32:33], 0.0), reads=Xkeys(i) + ['cwq'], writes=['X', 'cw'])
                        conv4(XQ[i], S_, cwq, (i * 4 + hl) * 4)
                        P.op('act', lambda e: e.activation(out=S_[:], in_=S_[:], func=AF.Silu), reads=['Sw'], writes=['Sw'])
                        if i == 2:
                            P.op('dve', lambda e: e.tensor_copy(out=vT[:], in_=S_[:]), reads=['Sw'], writes=['vT'])
                            continue
                        c1 = 128.0 if i == 0 else 1.0
                        for bi, (lo, hi) in enumerate(blocks9):
                            n = hi - lo
                            w0, w1 = wk[0], wk[1]
                            pb = pf[bi % 2]; pk = 'pf%d' % (bi % 2)
                            P.op('act', lambda e, lo=lo, hi=hi, n=n, w0=w0: e.activation(out=w0[:, 0:n], in_=S_[:, lo:hi], func=AF.Square), reads=['Sw'], writes=['wk0'])
                            P.op('pe', lambda e, n=n, w0=w0, pb=pb: e.matmul(pb[:, 0:n], lhsT=ones[:], rhs=w0[:, 0:n], start=True, stop=True), reads=['ones', 'wk0'], writes=[pk])
                            P.op('act', lambda e, n=n, w1=w1, pb=pb, c1=c1: e.activation(out=w1[:, 0:n], in_=pb[:, 0:n], func=AF.Sqrt, bias=c1 * EPS, scale=c1), writes=['wk1', pk])
                            P.op('dve', lambda e, n=n, w1=w1: e.reciprocal(out=w1[:, 0:n], in_=w1[:, 0:n]), reads=['wk1'], writes=['wk1'])
                            P.op('dve', lambda e, lo=lo, hi=hi, n=n, w1=w1, dstT=dstT: e.tensor_tensor(out=dstT[:, lo:hi], in0=S_[:, lo:hi], in1=w1[:, 0:n], op=ALU.mult),
                                 reads=['Sw', 'wk1'], writes=['qT' if i == 0 else 'kT'])
                    tcount = 0
                    for srcT, sk, dtok, dk in ((kT, 'kT', ktok, 'ktok'), (vT, 'vT', vtok, 'vtok')):
                        for c4 in range(0, NCHK, 4):
                            n = min(4, NCHK - c4)
                            par = tcount % 2; tcount += 1
                            for i2 in range(n):
                                P.op('pe', lambda e, par=par, i2=i2, c4=c4, srcT=srcT: e.transpose(
                                    out=ptb[par][:, i2 * 128:(i2 + 1) * 128], in_=srcT[:, (c4 + i2) * 128:(c4 + i2 + 1) * 128], identity=identb[:]),
                                    reads=[sk, 'identb'], writes=['ptb%d' % par])
                            eng = 'act' if par == 0 else 'dve'
                            if eng == 'act':
                                P.op('act', lambda e, par=par, n=n, c4=c4, dtok=dtok: e.activation(
                                    out=dtok[:, c4:c4 + n, :].rearrange("p c k -> p (c k)"), in_=ptb[par][:, 0:n * 128], func=AF.Copy),
                                    reads=[dk], writes=[dk, 'ptb%d' % par])
                            else:
                                P.op('dve', lambda e, par=par, n=n, c4=c4, dtok=dtok: e.tensor_copy(
                                    out=dtok[:, c4:c4 + n, :].rearrange("p c k -> p (c k)"), in_=ptb[par][:, 0:n * 128]),
                                    reads=[dk], writes=[dk, 'ptb%d' % par])
                    P.op('pool', lambda e: e.memset(OT[:], 0.0), writes=['OT.%d' % c for c in range(2, NCHK)])
                    P.barrier()
                    TFv = []
                    for i3 in range(3):
                        for a3 in range(3):
                            TFv.append(XQ[i3][:, a3 * 1280:(a3 + 1) * 1280].rearrange("p (s c) -> p s c", c=128))
                    TF = [[TFv[d * 4 + sl] for sl in range(4)] for d in range(2)]
                    TB = [[P.sb([128, 7, 128], BF16, stack=ph) for _ in range(4)] for _ in range(2)]
                    SF = [P.sb([128, 128], F32, stack=ph) for _ in range(2)]
                    SB_ = [P.sb([128, 128], BF16, stack=ph) for _ in range(2)]
                    FN = ['R', 'E', 'DmT', 'EG', 'DmTs', 'MTn', 'Yd', 'Xd', 'Cn', 'Zn']
                    BN = ['AqkT', 'Yb', 'Kg', 'WnT', 'Up', 'QgT', 'Ke']
                    bankc = [0, 0, 0, 0]

                    def step(c, d, slot, par):
                        col = d * 4 + hl
                        tf = {nm: TF[d][slot][:, i3, :] for i3, nm in enumerate(FN)}
                        tb = {nm: TB[d][slot][:, i3, :] for i3, nm in enumerate(BN)}
                        tf['S'] = SF[d][:, :]
                        tb['Sb'] = SB_[d][:, :]
                        K = lambda nm: (nm + str(d)) if nm in ('S', 'Sb') else (nm + str(d) + 'abcd'[slot])
                        TL = []
                        TS = []
                        cur = [TL]

                        def OP(eng, fn, reads=(), writes=()):
                            cur[0].append((eng, fn, tuple(reads), tuple(writes)))

                        def bank():
                            b = (d + 2 * par) if cur[0] is TL else (4 + d)
                            return pf[b][:, 0:128], 'pf%d' % b
                        gcol = G[:, c, col:col + 1]; gccol = GC[:, c, col:col + 1]
                        ksl = kT[:, c * 128:(c + 1) * 128]; qsl = qT[:, c * 128:(c + 1) * 128]
                        latent = c >= 2
                        OP('act', lambda e: e.activation(out=tf['R'], in_=msk[:, M_MASKF + d, :], func=AF.Copy, scale=gcol),
                             reads=['msk', 'G'], writes=[K('R')])
                        pgc, kgc = bank()
                        OP('pe', lambda e: e.matmul(pgc, lhsT=ones[:], rhs=tf['R'], start=True, stop=True), reads=['ones', K('R')], writes=[kgc])
                        OP('dve', lambda e: e.scalar_tensor_tensor(out=tf['E'], in0=pgc, scalar=gccol, in1=msk[:, M_NEGF + d, :], op0=ALU.subtract, op1=ALU.add),
                             reads=['GC', 'msk'], writes=[K('E'), kgc])
                        if latent:
                            OP('act', lambda e: e.activation(out=tf['EG'], in_=pgc, func=AF.Exp), writes=[K('EG'), kgc])
                        OP('act', lambda e: e.activation(out=tf['DmT'], in_=tf['E'], func=AF.Exp), reads=[K('E')], writes=[K('DmT')])
                        OP('dve', lambda e: e.tensor_tensor(out=tf['DmTs'], in0=tf['DmT'], in1=msk[:, M_STRF + d, :], op=ALU.mult),
                             reads=[K('DmT'), 'msk'], writes=[K('DmTs')])
                        pkk, kkk = bank()
                        OP('pe', lambda e: e.matmul(pkk, lhsT=ksl, rhs=ksl, start=True, stop=True), reads=['kT'], writes=[kkk])
                        OP('dve', lambda e: e.scalar_tensor_tensor(out=tf['MTn'], in0=pkk, scalar=NBETA[:, c, col:col + 1], in1=tf['DmTs'], op0=ALU.mult, op1=ALU.mult),
                             reads=['NBETA', K('DmTs')], writes=[K('MTn'), kkk])
                        if latent:
                            pqk, kqk = bank()
                            OP('pe', lambda e: e.matmul(pqk, lhsT=ksl, rhs=qsl, start=True, stop=True), reads=['kT', 'qT'], writes=[kqk])
                            OP('dve', lambda e: e.tensor_tensor(out=tb['AqkT'], in0=pqk, in1=tf['DmT'], op=ALU.mult), reads=[K('DmT')], writes=[K('AqkT'), kqk])
                        OP('dve', lambda e: e.tensor_tensor(out=tf['Yd'], in0=tf['MTn'], in1=msk[:, M_BD2, :], op=ALU.mult), reads=[K('MTn'), 'msk'], writes=[K('Yd')])
                        OP('dve', lambda e: e.tensor_tensor(out=tf['Yd'], in0=tf['Yd'], in1=ident, op=ALU.add), reads=[K('Yd'), 'msk'], writes=[K('Yd')])
                        ptx, ktx = bank()
                        OP('pe', lambda e: e.transpose(out=ptx, in_=tf['Yd'], identity=ident), reads=[K('Yd'), 'msk'], writes=[ktx])
                        OP('act', lambda e: e.activation(out=tf['Xd'], in_=ptx, func=AF.Copy), writes=[K('Xd'), ktx])
                        for li in range(6):
                            last = li == 5
                            OP('dve', lambda e, li=li: e.tensor_tensor(out=tf['Cn'], in0=tf['MTn'], in1=msk[:, M_LM0 + li, :], op=ALU.mult),
                                 reads=[K('MTn'), 'msk'], writes=[K('Cn')])
                            pz, kz = bank()
                            OP('pe', lambda e, pz=pz: e.matmul(pz, lhsT=tf['Cn'], rhs=tf['Xd'], start=True, stop=True), reads=[K('Cn'), K('Xd')], writes=[kz])
                            OP('act', lambda e, pz=pz: e.activation(out=tf['Zn'], in_=pz, func=AF.Copy), writes=[K('Zn'), kz])
                            if not last:
                                px, kx = bank()
                                OP('pe', lambda e, px=px: e.matmul(px, lhsT=tf['Yd'], rhs=tf['Zn'], start=True, stop=True), reads=[K('Yd'), K('Zn')], writes=[kx])
                                if True:
                                    OP('dve', lambda e, px=px: e.tensor_tensor(out=tf['Xd'], in0=px, in1=tf['Xd'], op=ALU.add), reads=[K('Xd')], writes=[K('Xd'), kx])
                            py, ky = bank()
                            OP('pe', lambda e, py=py: e.matmul(py, lhsT=tf['Zn'], rhs=tf['Yd'], start=True, stop=True), reads=[K('Yd'), K('Zn')], writes=[ky])
                            if False:
                                OP('dve', lambda e, px=px: e.tensor_tensor(out=tf['Xd'], in0=px, in1=tf['Xd'], op=ALU.add), reads=[K('Xd')], writes=[K('Xd'), kx])
                            OP('dve', lambda e, py=py: e.tensor_tensor(out=tf['Yd'], in0=py, in1=tf['Yd'], op=ALU.add), reads=[K('Yd')], writes=[K('Yd'), ky])
                        OP('act', lambda e: e.activation(out=tb['Yb'], in_=tf['Yd'], func=AF.Copy), reads=[K('Yd')], writes=[K('Yb')])
                        OP('pool', lambda e: e.tensor_scalar(out=tb['Kg'], in0=ktok[:, c, :], scalar1=EGC[:, c, col:col + 1], scalar2=None, op0=ALU.mult),
                             reads=['ktok', 'EGC'], writes=[K('Kg')])
                        pw, kw = bank()
                        OP('pe', lambda e: e.matmul(pw, lhsT=tb['Kg'], rhs=tb['Yb'], start=True, stop=True), reads=[K('Kg'), K('Yb')], writes=[kw])
                        OP('act', lambda e: e.mul(out=tb['WnT'], in_=pw, mul=-1.0), writes=[K('WnT'), kw])
                        if latent:
                            OP('dve', lambda e: e.tensor_tensor(out=tb['QgT'], in0=qsl, in1=tf['EG'], op=ALU.mult), reads=['qT', K('EG')], writes=[K('QgT')])
                        OP('pool', lambda e: e.tensor_scalar(out=tb['Ke'], in0=ktok[:, c, :], scalar1=DEND[:, c, col:col + 1], scalar2=None, op0=ALU.mult),
                             reads=['ktok', 'DEND'], writes=[K('Ke')])
                        cur[0] = TS
                        pu, ku = bank()
                        OP('pe', lambda e: e.matmul(pu, lhsT=tb['Yb'], rhs=vtok[:, c, :], start=True, stop=False), reads=[K('Yb'), 'vtok'], writes=[ku])
                        OP('pe', lambda e: e.matmul(pu, lhsT=tb['WnT'], rhs=tb['Sb'], start=False, stop=True), reads=[K('WnT'), K('Sb')], writes=[ku])
                        OP('dve', lambda e: e.tensor_scalar(out=tb['Up'], in0=pu, scalar1=BETA[:, c, col:col + 1], scalar2=None, op0=ALU.mult),
                             reads=['BETA'], writes=[K('Up'), ku])
                        if latent:
                            po, ko = bank()
                            OP('pe', lambda e: e.matmul(po, lhsT=tb['Sb'], rhs=tb['QgT'], start=True, stop=False), reads=[K('Sb'), K('QgT')], writes=[ko])
                            OP('pe', lambda e: e.matmul(po, lhsT=tb['Up'], rhs=tb['AqkT'], start=False, stop=True), reads=[K('Up'), K('AqkT')], writes=[ko])
                            osl = OT[:, (c - 2) * 128:(c - 1) * 128]
                            OP('dve', lambda e: e.tensor_tensor(out=osl, in0=po, in1=osl, op=ALU.add), reads=['OT.%d' % c], writes=['OT.%d' % c, ko])
                        psn, ksn = bank()
                        OP('pe', lambda e: e.matmul(psn, lhsT=tb['Ke'], rhs=tb['Up'], start=True, stop=True), reads=[K('Ke'), K('Up')], writes=[ksn])
                        OP('dve', lambda e: e.scalar_tensor_tensor(out=tf['S'], in0=tf['S'], scalar=GE[:, c, col:col + 1], in1=psn, op0=ALU.mult, op1=ALU.add),
                             reads=[K('S'), 'GE'], writes=[K('S'), ksn])
                        OP('act', lambda e: e.activation(out=tb['Sb'], in_=tf['S'], func=AF.Copy), reads=[K('S')], writes=[K('Sb')])
                        return TL, TS

                    for d in range(2):
                        P.op('pool', lambda e, d=d: e.memset(SF[d][:], 0.0), writes=['S%d' % d])
                        P.op('pool', lambda e, d=d: e.memset(SB_[d][:], 0.0), writes=['Sb%d' % d])

                    def interleave(lists):
                        n = max(len(l) for l in lists)
                        for k in range(n):
                            for l in lists:
                                if k < len(l):
                                    P.op(*l[k])
                    order_f = list(range(NCHK))
                    order_b = [1, 0] + list(range(NCHK - 1, 1, -1))
                    iters = NCHK // 2
                    prev = None
                    for j in range(iters + 1):
                        lists = []
                        st4 = None
                        if j < iters:
                            base = (j % 2) * 2
                            st4 = [step(order_f[2 * j], 0, base, 0), step(order_b[2 * j], 1, base, 0),
                                   step(order_f[2 * j + 1], 0, base + 1, 1), step(order_b[2 * j + 1], 1, base + 1, 1)]
                            lists += [x[0] for x in st4]
                        if prev is not None:
                            lists += [prev[0][1] + prev[2][1], prev[1][1] + prev[3][1]]
                        interleave(lists)
                        prev = st4
                    for bi in range(8):
                        lo, hi = bi * 512, (bi + 1) * 512
                        okeys = ['OT.%d' % c for c in range(2 + bi * 4, 6 + bi * 4)]
                        pb = pf[bi % 2]; pk = 'pf%d' % (bi % 2)
                        P.op('act', lambda e, lo=lo, hi=hi: e.activation(out=wk[0][:], in_=OT[:, lo:hi], func=AF.Square), reads=okeys, writes=['wk0'])
                        P.op('pe', lambda e, pb=pb: e.matmul(pb[:], lhsT=ones[:], rhs=wk[0][:], start=True, stop=True), reads=['ones', 'wk0'], writes=[pk])
                        P.op('act', lambda e, pb=pb: e.activation(out=wk[1][:], in_=pb[:], func=AF.Sqrt, bias=EPS, scale=1.0 / 128), writes=['wk1', pk])
                        P.op('dve', lambda e: e.reciprocal(out=wk[1][:], in_=wk[1][:]), reads=['wk1'], writes=['wk1'])
                        P.op('dve', lambda e, lo=lo, hi=hi: e.tensor_tensor(out=wk[2][:], in0=OT[:, lo:hi], in1=wk[1][:], op=ALU.mult), reads=okeys + ['wk1'], writes=['wk2'])
                        P.op('dve', lambda e, lo=lo, hi=hi: e.scalar_tensor_tensor(out=ybuf[:, lo:hi], in0=wk[2][:], scalar=dnw[:, 0:1], in1=SG[:, lo:hi], op0=ALU.mult, op1=ALU.mult),
                             reads=['wk2', 'dnw'] + ['SG.%d' % (bi + 1)], writes=['ybuf'])
                    P.dma('sp', yT[hl * 128:(hl + 1) * 128, :], ybuf[:], reads=['ybuf'], writes=['yTout.%d' % hl], semkey='st_y%d' % hl)
                P.barrier()

        lcw = P.sb([128, 16], F32); P.dma('sp', lcw[:], lcw_d, writes=['lcw'])
        lcb = P.sb([128, 4], F32); P.dma('sp', lcb[:], lcb_d, writes=['lcb'])
        lwr = P.sb([128, 1024], F32); P.dma('sp', lwr[:], lwr_d, writes=['lwr'])
        lwi = P.sb([128, 1024], F32); P.dma('sp', lwi[:], lwi_d, writes=['lwi'])
        lbr = P.sb([128, 8], F32); P.dma('sp', lbr[:], lbr_d, writes=['lbr'])
        lbi = P.sb([128, 8], F32); P.dma('sp', lbi[:], lbi_d, writes=['lbi'])
        lam = P.sb([128, 8], F32); P.dma('sp', lam[:], llam_d, writes=['lam'])
        n8 = P.sb([128, 8], F32); n16 = P.sb([128, 8], F32)
        P.op('act', lambda e: e.activation(out=lam[:], in_=lam[:], func=AF.Exp, scale=-1.0), reads=['lam'], writes=['lam'])
        P.op('act', lambda e: e.activation(out=lam[:], in_=lam[:], func=AF.Ln, bias=1.0, scale=1.0), reads=['lam'], writes=['lam'])
        P.op('act', lambda e: e.mul(out=n8[:], in_=lam[:], mul=-8.0), reads=['lam'], writes=['n8'])
        P.op('act', lambda e: e.mul(out=n16[:], in_=lam[:], mul=-16.0), reads=['lam'], writes=['n16'])

        for lp in range(2):
            with ExitStack() as hs:
                XL = [P.sb([128, XW], F32, stack=hs) for _ in range(2)]
                SGL = [P.sb([128, 4096], BF16, stack=hs) for _ in range(2)]
                for i in range(2):
                    P.op('pool', lambda e, i=i: e.memset(XL[i][:], 0.0), writes=['XL%d.%d' % (i, g) for g in range(9)])
                with ExitStack() as ph:
                    def evac_lru(ci, g, tok0, W, ps, pk):
                        if ci < 2:
                            if g == 0:
                                dst = XL[ci][:, 2:258]
                                src = ps
                            else:
                                r0 = 8 * (g - 1)
                                dst = XL[ci][:, 261:261 + 4096].rearrange("p (c r) -> p r c", r=64)[:, r0:r0 + 8, :]
                                src = ps.rearrange("p (r c) -> p r c", c=64)
                            if (ci + g) % 2 == 0:
                                P.op('act', lambda e, dst=dst, src=src: e.activation(out=dst, in_=src, func=AF.Copy), writes=['XL%d.%d' % (ci, g), pk])
                            else:
                                P.op('dve', lambda e, dst=dst, src=src: e.tensor_copy(out=dst, in_=src), writes=['XL%d.%d' % (ci, g), pk])
                        else:
                            P.op('act', lambda e, ps=ps, tok0=tok0, W=W, ci=ci: e.activation(out=SGL[ci - 2][:, tok0 - 256:tok0 - 256 + W], in_=ps, func=AF.Silu),
                                 writes=['SGL%d.%d' % (ci - 2, g), pk])
                    inproj(ph, [12 + 2 * lp, 13 + 2 * lp, 20 + 2 * lp, 21 + 2 * lp], evac_lru, skip=lambda ci, g: (ci >= 2 and g == 0))
                P.barrier()
                with ExitStack() as ph:
                    XC = P.sb([128, L1_TOK], F32, stack=ph)
                    A = P.sb([128, L1_TOK], F32, stack=ph); Bv = P.sb([128, L1_TOK], F32, stack=ph)
                    HF = P.sb([128, L1_TOK], F32, stack=ph); HB = P.sb([128, L1_TOK], F32, stack=ph)
                    ybl = P.sb([128, 4096], BF16, stack=ph)
                    wk = [P.sb([128, 512], F32, stack=ph) for _ in range(3)]
                    pf = [P.ps([128, 512], F32, stack=ph) for _ in range(4)]
                    for cl in range(2):
                        c = 2 * lp + cl
                        P.op('dve', lambda e: e.memset(stat[:, 32:33], 0.0), reads=['XL%d.%d' % (cl, g) for g in range(9)] + ['lcw', 'lcb'], writes=['X', 'cw'])
                        conv4(XL[cl], XC, lcw, c * 4, bias=lcb[:, c:c + 1])
                        for d in range(2):
                            pc = d * 4 + c
                            for bi, (lo, hi) in enumerate(blocks9):
                                n = hi - lo
                                pr = pf[(bi % 2) * 2]; pi_ = pf[(bi % 2) * 2 + 1]
                                kr = 'pf%d' % ((bi % 2) * 2); ki = 'pf%d' % ((bi % 2) * 2 + 1)
                                P.op('pe', lambda e, pr=pr, n=n, lo=lo, hi=hi, pc=pc: e.matmul(pr[:, 0:n], lhsT=lwr[:, pc * 128:(pc + 1) * 128], rhs=XC[:, lo:hi], start=True, stop=True),
                                     reads=['lwr', 'Sw'], writes=[kr])
                                P.op('pe', lambda e, pi_=pi_, n=n, lo=lo, hi=hi, pc=pc: e.matmul(pi_[:, 0:n], lhsT=lwi[:, pc * 128:(pc + 1) * 128], rhs=XC[:, lo:hi], start=True, stop=True),
                                     reads=['lwi', 'Sw'], writes=[ki])
                                P.op('act', lambda e, pr=pr, n=n, pc=pc: e.activation(out=wk[0][:, 0:n], in_=pr[:, 0:n], func=AF.Sigmoid, bias=lbr[:, pc:pc + 1], scale=1.0),
                                     reads=['lbr'], writes=['wk0', kr])
                                P.op('act', lambda e, pi_=pi_, n=n, pc=pc: e.activation(out=wk[1][:, 0:n], in_=pi_[:, 0:n], func=AF.Sigmoid, bias=lbi[:, pc:pc + 1], scale=1.0),
                                     reads=['lbi'], writes=['wk1', ki])
                                P.op('act', lambda e, n=n, lo=lo, hi=hi, pc=pc: e.activation(out=A[:, lo:hi], in_=wk[0][:, 0:n], func=AF.Exp, scale=n8[:, pc:pc + 1]),
                                     reads=['wk0', 'n8'], writes=['A'])
                                P.op('act', lambda e, n=n, pc=pc: e.activation(out=wk[2][:, 0:n], in_=wk[0][:, 0:n], func=AF.Exp, scale=n16[:, pc:pc + 1]),
                                     reads=['wk0', 'n16'], writes=['wk2'])
                                P.op('act', lambda e, n=n: e.activation(out=wk[2][:, 0:n], in_=wk[2][:, 0:n], func=AF.Sqrt, bias=1.0, scale=-1.0),
                                     reads=['wk2'], writes=['wk2'])
                                P.op('dve', lambda e, n=n, lo=lo, hi=hi: e.tensor_tensor(out=wk[1][:, 0:n], in0=wk[1][:, 0:n], in1=XC[:, lo:hi], op=ALU.mult),
                                     reads=['wk1', 'Sw'], writes=['wk1'])
                                P.op('dve', lambda e, n=n, lo=lo, hi=hi: e.tensor_tensor(out=Bv[:, lo:hi], in0=wk[1][:, 0:n], in1=wk[2][:, 0:n], op=ALU.mult),
                                     reads=['wk1', 'wk2'], writes=['Bv'])
                            if d == 0:
                                P.op('dve', lambda e: e.tensor_tensor_scan(out=HF[:, 0:256], data0=A[:, 0:256], data1=Bv[:, 0:256], initial=0.0, op0=ALU.mult, op1=ALU.add),
                                     reads=['A', 'Bv'], writes=['HF'])
                                P.op('dve', lambda e: e.tensor_tensor_scan(out=HF[:, 256:L1_TOK], data0=A[:, 256:L1_TOK], data1=Bv[:, 256:L1_TOK], initial=HF[:, 255:256], op0=ALU.mult, op1=ALU.add),
                                     reads=['A', 'Bv', 'HF'], writes=['HF'])
                            else:
                                P.op('dve', lambda e: e.tensor_tensor_scan(out=HB[:, 255::-1], data0=A[:, 255::-1], data1=Bv[:, 255::-1], initial=0.0, op0=ALU.mult, op1=ALU.add),
                                     reads=['A', 'Bv'], writes=['HB'])
                                P.op('dve', lambda e: e.tensor_tensor_scan(out=HB[:, L1_TOK - 1:255:-1], data0=A[:, L1_TOK - 1:255:-1], data1=Bv[:, L1_TOK - 1:255:-1], initial=HB[:, 0:1], op0=ALU.mult, op1=ALU.add),
                                     reads=['A', 'Bv', 'HB'], writes=['HB'])
                        P.op('dve', lambda e: e.tensor_tensor(out=HF[:, 256:L1_TOK], in0=HF[:, 256:L1_TOK], in1=HB[:, 256:L1_TOK], op=ALU.add), reads=['HF', 'HB'], writes=['HF'])
                        P.op('dve', lambda e, cl=cl: e.tensor_tensor(
                            out=ybl[:].rearrange("p (r c) -> p r c", c=64), in0=HF[:, 256:L1_TOK].rearrange("p (c r) -> p r c", r=64),
                            in1=SGL[cl][:].rearrange("p (r c) -> p r c", c=64), op=ALU.mult),
                            reads=['HF'] + ['SGL%d.%d' % (cl, g) for g in range(1, 9)], writes=['ybl'])
                        P.dma('sp', yT[512 + c * 128:512 + (c + 1) * 128, :], ybl[:], reads=['ybl'], writes=['yTout.%d' % (4 + c)], semkey='st_yl%d' % c)
                P.barrier()
    P.stack = P.semstack


def l1_masks():
    i = np.arange(128)
    s, t = i[:, None], i[None, :]
    m = np.zeros((14, 128, 128), np.float32)
    m[M_MASKF] = (s <= t); m[M_MASKB] = (s >= t)
    m[M_NEGF] = np.where(t >= s, 0.0, NEG); m[M_NEGB] = np.where(t <= s, 0.0, NEG)
    m[M_STRF] = (t > s); m[M_STRB] = (t < s)
    m[M_BD2] = (s // 2 == t // 2) & (s != t)
    for li, b in enumerate((2, 4, 8, 16, 32, 64)):
        m[M_LM0 + li] = (s // (2 * b) == t // (2 * b)) & (s // b != t // b)
    m[M_ID] = (s == t)
    return np.ascontiguousarray(m.transpose(1, 0, 2)).reshape(128, 14 * 128)


def l1_in_maps(inputs, modv):
    w_in = inputs['ab_w_in'][0]
    qc = inputs['ab_qkv_conv'][0]
    msk = l1_masks()
    in_maps = []
    for r in range(NCORES):
        b, j = r // 4, r % 4
        heads = [4 * j + h for h in range(4)]
        cols = []
        for base in (0, 2048, 4096):
            for h in heads:
                cols.append(np.arange(base + h * 128, base + (h + 1) * 128))
        for c in range(4):
            cols.append(np.arange(6144 + (4 * j + c) * 128, 6144 + (4 * j + c + 1) * 128))
        for h in heads:
            cols.append(np.arange(8256 + h * 128, 8256 + (h + 1) * 128))
        for c in range(4):
            cols.append(np.arange(10304 + (4 * j + c) * 128, 10304 + (4 * j + c + 1) * 128))
        cols = np.concatenate(cols)
        wc = w_in[:, cols].reshape(32, 128, 24, 128)
        wmain = np.ascontiguousarray(wc.transpose(2, 1, 0, 3)).reshape(24 * 128, 32 * 128)
        abcols = [8192 + d * 16 + h for d in range(2) for h in heads] + [8224 + d * 16 + h for d in range(2) for h in heads]
        wab = np.ascontiguousarray(w_in[:, abcols].reshape(32, 128, 16).transpose(1, 0, 2)).reshape(128, 512)
        if modv is not None:
            sh0, sc0 = modv[b, 0:D], modv[b, D:2 * D]
            shc, scc = modv[2, 0:D], modv[2, D:2 * D]
            mod0 = np.concatenate([pcol(sh0), pcol(sc0), pcol(shc), pcol(scc), pcol(inputs['norm_w'][0])], axis=1)
        else:
            mod0 = np.zeros((128, 160), np.float32)
        cwq = np.zeros((128, 12, 4), np.float32)
        for which, base in enumerate((0, 2048, 4096)):
            for hl, h in enumerate(heads):
                cwq[:, which * 4 + hl, :] = qc[:, base + h * 128: base + (h + 1) * 128].T
        al = np.array([inputs['ab_a_log'][0][d, h] for d in range(2) for h in heads], np.float32)
        dt = np.array([inputs['ab_dt_bias'][0][d, h] for d in range(2) for h in heads], np.float32)
        alog = np.ascontiguousarray(np.broadcast_to(al[None, None, :], (128, NCHK, 8))).reshape(128, 272)
        dtb = np.ascontiguousarray(np.broadcast_to(dt[None, None, :], (128, NCHK, 8))).reshape(128, 272)
        lcw = np.zeros((128, 4, 4), np.float32); lcb = np.zeros((128, 4), np.float32)
        lwr = np.zeros((128, 2, 4, 128), np.float32); lwi = np.zeros((128, 2, 4, 128), np.float32)
        lbr = np.zeros((128, 2, 4), np.float32); lbi = np.zeros((128, 2, 4), np.float32); llam = np.zeros((128, 2, 4), np.float32)
        for c in range(4):
            n = 4 * j + c
            sl = slice(n * 128, (n + 1) * 128)
            lcw[:, c, :] = inputs['ab_lru_conv_w'][0][:, sl].T
            lcb[:, c] = inputs['ab_lru_conv_b'][0][sl]
            for d in range(2):
                lwr[:, d, c, :] = inputs['ab_lru_w_r'][0][d, n]
                lwi[:, d, c, :] = inputs['ab_lru_w_i'][0][d, n]
                lbr[:, d, c] = inputs['ab_lru_b_r'][0][d, sl]
                lbi[:, d, c] = inputs['ab_lru_b_i'][0][d, sl]
                llam[:, d, c] = inputs['ab_lru_lambda'][0][d, sl]
        in_maps.append({
            "xa": np.ascontiguousarray(np.concatenate([inputs['ctx'][b], inputs['x'][b]], axis=0)),
            "wmain": wmain, "wab": wab, "mod0": np.ascontiguousarray(mod0), "cwq": cwq.reshape(128, 48),
            "alog": alog, "dtb": dtb, "dnw": np.ascontiguousarray(inputs['ab_dn_norm'][0].reshape(128, 1)),
            "lcw": lcw.reshape(128, 16), "lcb": lcb, "lwr": lwr.reshape(128, 1024), "lwi": lwi.reshape(128, 1024),
            "lbr": lbr.reshape(128, 8), "lbi": lbi.reshape(128, 8), "llam": llam.reshape(128, 8), "msk": msk})
    return in_maps


def run_l1(inputs, modv):
    in_maps = l1_in_maps(inputs, modv)
    nc = build_l1()
    res = _launch(nc, in_maps)
    yT_full = np.zeros((2, 4096, 4096), ml_dtypes.bfloat16)
    for r in range(NCORES):
        b, j = r // 4, r % 4
        y = res[r]["yT"]
        yT_full[b, j * 512:(j + 1) * 512] = y[0:512]
        yT_full[b, 2048 + j * 512:2048 + (j + 1) * 512] = y[512:1024]
    return yT_full


I32 = mybir.dt.int32


def build_fused(stop=None):
    nc = bass.Bass("TRN2", target_bir_lowering=False, num_devices=NCORES)
    def din(name, shape, dt=F32):
        return nc.dram_tensor(name, list(shape), dt, kind="ExternalInput").ap()
    def dint(name, shape, dt=F32):
        return nc.dram_tensor(name, list(shape), dt, kind="Internal").ap()
    csT = din("csT", [128, 96]); w0 = din("w0", [D, L0_COLS]); bias0 = din("bias0", [3, L0_COLS])
    idxm_d = din("idxm", [8, 2], I32); selm_d = din("selm", [8, 8 * 128]); nw_d = din("nw", [128, 64])
    io1 = {}; io2 = {}; idxy_d = None
    if stop is None:
        io1 = {name: din(name, shape) for name, shape in L1_INPUTS if name != 'mod0'}
        io2 = {'x': din("x", [L2_TOK, D]), 'wo': din("wo", [8 * 128, 32 * 512]), 'wi': din("wi", [32 * 128, 32 * 4 * 128]),
               'wo1': din("wo1", [8 * 128, 32 * 512]), 'cw': din("cw", [128, 96]), 'fnb': din("fnb", [128, D])}
        idxy_d = din("idxy", [128, 128], I32)
        io2['out'] = nc.dram_tensor("out", [L2_TOK, D], F32, kind="ExternalOutput").ap()
    msrc = dint("msrc", [3, L0_COLS]); MG = dint("MG", [24, L0_COLS]); MG2 = dint("MG2", [24, L0_COLS])
    mod0_s = dint("mod0_s", [128, 160]); m1_s = dint("m1_s", [128, 96])
    g0b_s = dint("g0b_s", [128, D]); g1b_s = dint("g1b_s", [128, D])
    io1['mod0'] = mod0_s
    io1['hnT_d'] = dint("hnT_d", [128, 9 * 32 * 512], BF16)
    ysrc = dint("ysrc", [1024, 4096], BF16)
    io1['yT'] = ysrc
    YG = dint("YG", [4096, 4096], BF16); YG2 = dint("YG2", [4096, 4096], BF16)
    io2['m1'] = m1_s; io2['g0b'] = g0b_s; io2['g1b'] = g1b_s
    with ExitStack() as st:
        P = Prog(nc, st)
        emit_l0(P, csT, w0, bias0, msrc, okey='st_msrc')
        P.custom('pool', lambda e: e.collective_compute("AllGather", ALU.bypass, replica_groups=[list(range(NCORES))],
                                                        ins=[msrc.opt()], outs=[MG.opt()]),
                 reads=['l0out'], writes=['mg'], semkey='cc_mg', inc=1)
        P.dma('sp', MG2, MG, reads=['mg'], writes=['mg2'])
        P.barrier()
        if stop == 'ag':
            dbg = nc.dram_tensor("dbg", [24, L0_COLS], F32, kind="ExternalOutput").ap()
            P.dma('sp', dbg, MG2, semkey='st_dbg0')
            P.finish()
            return nc
        with ExitStack() as ph:
            P.stack = ph
            idf, idfk = make_identity(P, F32)
            idx = P.sb([8, 2], I32); P.dma('sp', idx[:], idxm_d, writes=['idxm'])
            sel = P.sb([8, 8, 128], F32); P.dma('sp', sel[:].rearrange("k r m -> k (r m)"), selm_d, writes=['sel'])
            nwt = P.sb([128, 64], F32); P.dma('sp', nwt[:], nw_d, writes=['nwt'])
            Tb = P.sb([8, L0_COLS], F32); Tc = P.sb([8, L0_COLS], F32)
            P.custom('pool', lambda e: e.indirect_dma_start(out=Tb[:, :], out_offset=None, in_=MG2,
                                                            in_offset=bass.IndirectOffsetOnAxis(ap=idx[:, 0:1], axis=0)),
                     reads=['mg2', 'idxm'], writes=['Tb'], semkey='d_Tb', inc=16)
            P.custom('pool', lambda e: e.indirect_dma_start(out=Tc[:, :], out_offset=None, in_=MG2,
                                                            in_offset=bass.IndirectOffsetOnAxis(ap=idx[:, 1:2], axis=0)),
                     reads=['mg2', 'idxm'], writes=['Tc'], semkey='d_Tc', inc=16)
            if stop == 'gather':
                dbg = nc.dram_tensor("dbg", [16, L0_COLS], F32, kind="ExternalOutput").ap()
                P.dma('sp', dbg[0:8, :], Tb[:], reads=['Tb'], semkey='st_dbg0')
                P.dma('sp', dbg[8:16, :], Tc[:], reads=['Tc'], semkey='st_dbg1')
                P.finish()
                P.stack = P.semstack
                return nc
            VB = P.sb([128, 192], F32); VC = P.sb([128, 192], F32)
            pv = [P.ps([128, 512], F32) for _ in range(4)]
            for vi, (T, tk, V, vk) in enumerate(((Tb, 'Tb', VB, 'VB'), (Tc, 'Tc', VC, 'VC'))):
                for cb in range(24):
                    P.op('pe', lambda e, vi=vi, cb=cb, T=T: e.transpose(out=pv[vi][:, cb * 8:(cb + 1) * 8], in_=T[0:8, cb * 128:(cb + 1) * 128],
                                                                         identity=idf[0:8, 0:8]), reads=[tk, idfk], writes=['pv%d' % vi])
                P.op('dve', lambda e, vi=vi, V=V: e.tensor_copy(out=V[:].rearrange("p (r cb) -> p cb r", cb=24),
                                                                in_=pv[vi][:, 0:192].rearrange("p (cb r) -> p cb r", r=8)),
                     writes=[vk, 'pv%d' % vi])
            mod0t = P.sb([128, 160], F32); m1t = P.sb([128, 96], F32)
            P.op('dve', lambda e: e.tensor_copy(out=mod0t[:, 0:64], in_=VB[:, 0:64]), reads=['VB'], writes=['mod0t'])
            P.op('dve', lambda e: e.tensor_copy(out=mod0t[:, 64:128], in_=VC[:, 0:64]), reads=['VC', 'mod0t'], writes=['mod0t'])
            P.op('dve', lambda e: e.tensor_copy(out=mod0t[:, 128:160], in_=nwt[:, 0:32]), reads=['nwt', 'mod0t'], writes=['mod0t'])
            P.op('dve', lambda e: e.tensor_copy(out=m1t[:, 0:64], in_=VB[:, 96:160]), reads=['VB'], writes=['m1t'])
            P.op('dve', lambda e: e.tensor_copy(out=m1t[:, 64:96], in_=nwt[:, 32:64]), reads=['nwt', 'm1t'], writes=['m1t'])
            P.dma('sp', mod0_s, mod0t[:], reads=['mod0t'], writes=['mod0s'])
            P.dma('sp', m1_s, m1t[:], reads=['m1t'], writes=['m1s'])
            gbt = [P.sb([128, D], F32) for _ in range(2)]
            cnt = 0
            for gi, r0 in enumerate((2, 6)):
                for (rr, c0, n, dst) in ((r0, 2048, 1024, 0), (r0 + 1, 0, 3072, 1024)):
                    for off in range(0, n, 512):
                        pi = 2 + cnt % 2
                        cnt += 1
                        P.op('pe', lambda e, pi=pi, rr=rr, c0=c0, off=off: e.matmul(pv[pi][:, 0:512], lhsT=sel[:, rr, :], rhs=Tb[0:8, c0 + off:c0 + off + 512],
                                                                                 start=True, stop=True), reads=['sel', 'Tb'], writes=['pv%d' % pi])
                        P.op('act', lambda e, pi=pi, gi=gi, dst=dst, off=off: e.activation(out=gbt[gi][:, dst + off:dst + off + 512], in_=pv[pi][:, 0:512], func=AF.Copy),
                             reads=['gbt%d' % gi], writes=['gbt%d' % gi, 'pv%d' % pi])
            P.dma('sp', g0b_s, gbt[0][:], reads=['gbt0'], writes=['g0bs'])
            P.dma('act', g1b_s, gbt[1][:], reads=['gbt1'], writes=['g1bs'])
            P.barrier()
        P.stack = P.semstack
        if stop == 'mod':
            dbg = nc.dram_tensor("dbg", [128, 160 + 96 + 2 * D], F32, kind="ExternalOutput").ap()
            P.dma('sp', dbg[:, 0:160], mod0_s, semkey='st_dbg0')
            P.dma('sp', dbg[:, 160:256], m1_s, semkey='st_dbg1')
            P.dma('sp', dbg[:, 256:256 + D], g0b_s, semkey='st_dbg2')
            P.dma('sp', dbg[:, 256 + D:256 + 2 * D], g1b_s, semkey='st_dbg3')
            P.finish()
            return nc
        emit_l1(P, io1)
        P.barrier()
        P.custom('pool', lambda e: e.collective_compute("AllGather", ALU.bypass, replica_groups=[[0, 1, 2, 3], [4, 5, 6, 7]],
                                                        ins=[ysrc.opt()], outs=[YG.opt()]),
                 reads=['yTout.%d' % i for i in range(8)], writes=['yg'], semkey='cc_yg', inc=1)
        for i in range(8):
            P.dma('sp' if i % 2 == 0 else 'act', YG2[i * 512:(i + 1) * 512, :], YG[i * 512:(i + 1) * 512, :], reads=['yg'], writes=['yg2.%d' % i])
        P.barrier()
        YV = YG2.rearrange("r (b t) -> (r b) t", t=L2_T)
        emit_l2(P, io2, ysrc=(YV, idxy_d))
        P.finish()
        print("fused instructions:", P.ninst, "semaphores:", len(P.sems))
    return nc


def run_fused(inputs):
    c, c_ctx, mod_w, mod_b = inputs['c'], inputs['c_ctx'], inputs['mod_w'], inputs['mod_b']
    cs = np.concatenate([c, c_ctx[None, :]], axis=0)
    csT = np.ascontiguousarray(cs.reshape(3, 32, 128).transpose(2, 1, 0)).reshape(128, 96)
    wall = np.concatenate([mod_w[0], mod_w[1]], axis=1)
    ball = np.concatenate([mod_b[0], mod_b[1]], axis=0)
    wo, wi, wo1, cw = l2_layouts(inputs)
    fnb = np.ascontiguousarray(np.broadcast_to(inputs['final_norm_w'][None, :], (128, D)))
    selm = np.zeros((8, 8, 128), np.float32)
    for r in range(8):
        selm[r, r, :] = 1.0
    nw = np.ascontiguousarray(np.concatenate([pcol(inputs['norm_w'][0]), pcol(inputs['norm_w'][1])], axis=1))
    l1maps = l1_in_maps(inputs, None)
    in_maps = []
    for r in range(NCORES):
        b, tq = r // 4, r % 4
        sl = slice(r * L0_COLS, (r + 1) * L0_COLS)
        m = dict(l1maps[r])
        m.pop('mod0')
        m.update({"csT": csT, "w0": np.ascontiguousarray(wall[:, sl]),
                  "bias0": np.ascontiguousarray(np.broadcast_to(ball[sl][None, :], (3, L0_COLS))),
                  "idxm": np.array([[r8 * 3 + b, r8 * 3 + 2] for r8 in range(8)], np.int32),
                  "selm": selm.reshape(8, 8 * 128), "nw": nw,
                  "x": np.ascontiguousarray(inputs['x'][b, tq * L2_TOK:(tq + 1) * L2_TOK]),
                  "wo": wo, "wi": wi, "wo1": wo1, "cw": cw, "fnb": fnb})
        idxy = np.zeros((128, 32, 4), np.int32)
        p = np.arange(128)
        for kc in range(32):
            if kc < 16:
                jj, local = kc // 4, (kc % 4) * 128 + p
            else:
                jj, local = (kc - 16) // 4, 512 + ((kc - 16) % 4) * 128 + p
            for ps_i in range(4):
                idxy[:, kc, ps_i] = (jj * 1024 + local) * 16 + tq * 4 + ps_i
        m["idxy"] = idxy.reshape(128, 128)
        in_maps.append(m)
    nc = build_fused()
    res = _launch(nc, in_maps)
    out = np.zeros((2, 4096, D), np.float32)
    for r in range(NCORES):
        b, tq = r // 4, r % 4
        out[b, tq * L2_TOK:(tq + 1) * L2_TOK] = res[r]["out"]
    return out


FUSED = False


def kernel(**inputs):
    inputs = {k: np.asarray(v) for k, v in inputs.items()}
    if FUSED:
        return run_fused(inputs)
    modv = run_l0(inputs['c'], inputs['c_ctx'], inputs['mod_w'], inputs['mod_b'])
    yT_full = run_l1(inputs, modv)
    return run_l2(inputs, modv, yT_full)
```
